# Optimizing a Trainium2 kernel written in Bass

```python
import jax
import jax.numpy as jnp
from jax import lax
import numpy as np

D_MODEL = 2048
BATCH = 32
SEQ = 256
DEPTH = 4
DEC_BATCH = 8
DEC_SEQ = 2048
PAST_LEN = 256

GRID_W = 64
HEAD_DIM = 128
N_HEADS = D_MODEL // HEAD_DIM
N_KV_HEADS = N_HEADS // 4
Q_GROUP = N_HEADS // N_KV_HEADS
Q_BLOCK = 128
ROPE_BASE = 10000.0
ROPE_FREQS = HEAD_DIM // 4
LRU_WIDTH = D_MODEL
LRU_BLOCKS = 16
LRU_BW = LRU_WIDTH // LRU_BLOCKS
CONV_W = 4
CONV_PAD_L = 2
LRU_C = 8.0
CHUNK = 128
CMLP_WIDTH = D_MODEL
CMLP_GROUPS = 16
CMLP_GW = CMLP_WIDTH // CMLP_GROUPS
D_FF = 4 * D_MODEL
N_BRANCH = 3
N_MOD = 6
EPS = 1e-6
Q_W = N_HEADS * HEAD_DIM
KV_W = N_KV_HEADS * HEAD_DIM
IN_SPLITS = (Q_W, KV_W, KV_W, LRU_WIDTH, LRU_WIDTH, CMLP_WIDTH, CMLP_WIDTH, N_BRANCH * D_MODEL)
IN_W = Q_W + 2 * KV_W + 2 * LRU_WIDTH + 2 * CMLP_WIDTH + N_BRANCH * D_MODEL

kernel_name = 'hybrid_gated_rglru_gqa_chunkmlp_step'


def _rmsnorm(x, g):
    xf = x.astype(jnp.float32)
    y = xf * lax.rsqrt(jnp.mean(xf * xf, axis=-1, keepdims=True) + EPS)
    return (y * g.astype(jnp.float32)).astype(x.dtype)


def _split_in(z):
    idx = []
    off = 0
    for s in IN_SPLITS[:-1]:
        off += s
        idx.append(off)
    return jnp.split(z, idx, axis=-1)


def _modulation(cond, w_mod, b_mod):
    m = jax.nn.silu(cond) @ w_mod + b_mod
    return [t[:, None, :] for t in jnp.split(m, N_MOD, axis=-1)]


def _axial_rope(n_tokens):
    rows = n_tokens // GRID_W
    pos_row = jnp.repeat(jnp.arange(rows), GRID_W).astype(jnp.float32)
    pos_col = (jnp.arange(n_tokens) % GRID_W).astype(jnp.float32)
    inv = ROPE_BASE ** (-jnp.arange(ROPE_FREQS, dtype=jnp.float32) / ROPE_FREQS)
    ang = jnp.stack([pos_row[:, None] * inv, pos_col[:, None] * inv], axis=1)
    return jnp.cos(ang), jnp.sin(ang)


def _apply_rope(x, cos, sin):
    xs = x.astype(jnp.float32).reshape(*x.shape[:-1], 2, 2, ROPE_FREQS)
    xa = xs[..., 0, :]
    xb = xs[..., 1, :]
    c = cos[None, :, None]
    s = sin[None, :, None]
    out = jnp.stack([xa * c - xb * s, xa * s + xb * c], axis=-2)
    return out.reshape(x.shape).astype(x.dtype)


def _attend(q, k, v):
    b, tq = q.shape[0], q.shape[1]
    nb = tq // Q_BLOCK
    qb = q.reshape(b, nb, Q_BLOCK, N_KV_HEADS, Q_GROUP, HEAD_DIM).transpose(1, 0, 2, 3, 4, 5)
    scale = HEAD_DIM ** -0.5

    def one_block(qblk):
        s = jnp.einsum('bqhgd,bkhd->bhgqk', qblk, k, preferred_element_type=jnp.float32) * scale
        p = jax.nn.softmax(s, axis=-1)
        return jnp.einsum('bhgqk,bkhd->bqhgd', p.astype(v.dtype), v)

    o = lax.map(one_block, qb)
    return o.transpose(1, 0, 2, 3, 4, 5).reshape(b, tq, Q_W)


def _dwconv(x, w, b):
    t = x.shape[1]
    xp = jnp.pad(x, ((0, 0), (CONV_PAD_L, CONV_W - 1 - CONV_PAD_L), (0, 0)))
    y = b
    for j in range(CONV_W):
        y = y + xp[:, j:j + t] * w[j]
    return y


def _blockdiag(x, w, b):
    xb = x.reshape(*x.shape[:-1], LRU_BLOCKS, LRU_BW)
    return jnp.einsum('btnc,ncd->btnd', xb, w).reshape(x.shape) + b


def _lin_combine(e1, e2):
    a1, b1 = e1
    a2, b2 = e2
    return (a1 * a2, a2 * b1 + b2)


def _rglru_dir(x, w_a, b_a, w_x, b_x, lam, h0, reverse):
    xf = x.astype(jnp.float32)
    r = jax.nn.sigmoid(_blockdiag(x, w_a, b_a).astype(jnp.float32))
    i = jax.nn.sigmoid(_blockdiag(x, w_x, b_x).astype(jnp.float32))
    log_a = -LRU_C * r * jax.nn.softplus(-lam.astype(jnp.float32))
    a = jnp.exp(log_a)
    bterm = jnp.sqrt(jnp.maximum(-jnp.expm1(2.0 * log_a), 0.0)) * (i * xf)
    edge = -1 if reverse else 0
    bterm = bterm.at[:, edge].add(a[:, edge] * h0.astype(jnp.float32))
    _, h = lax.associative_scan(_lin_combine, (a, bterm), axis=1, reverse=reverse)
    return h


def _chunk_mix(zu, zv, g_norm, w_s, b_s):
    b, t, _ = zu.shape
    u = jax.nn.gelu(zu)
    v = _rmsnorm(jax.nn.gelu(zv), g_norm)
    vb = v.reshape(b, t // CHUNK, CHUNK, CMLP_GROUPS, CMLP_GW)
    mixed = jnp.einsum('gqp,bnpgc->bnqgc', w_s, vb) + b_s.T[None, None, :, :, None]
    return u * mixed.reshape(b, t, CMLP_WIDTH)


def _layer(x, cond, p, rope, ctx_k, ctx_v, h0):
    b, t, _ = x.shape
    sh1, sc1, g1, sh2, sc2, g2 = _modulation(cond, p['w_mod'], p['b_mod'])
    h = _rmsnorm(x, p['g_pre_mix']) * (1 + sc1) + sh1
    zq, zk, zv, zlx, zlg, zcu, zcv, zg = _split_in(h @ p['w_in'])
    q = _rmsnorm(zq.reshape(b, t, N_HEADS, HEAD_DIM), p['g_q'])
    k = _rmsnorm(zk.reshape(b, t, N_KV_HEADS, HEAD_DIM), p['g_k'])
    v = zv.reshape(b, t, N_KV_HEADS, HEAD_DIM)
    if rope is None:
        k_all, v_all = k, v
    else:
        cos, sin = rope
        q = _apply_rope(q, cos, sin)
        k = _apply_rope(k, cos, sin)
        k_all = jnp.concatenate([k, ctx_k.astype(k.dtype)], axis=1)
        v_all = jnp.concatenate([v, ctx_v.astype(v.dtype)], axis=1)
    y_attn = _attend(q, k_all, v_all) @ p['w_attn_out']
    xc = _dwconv(zlx, p['conv_w'], p['conv_b'])
    if h0 is None:
        h0 = jnp.zeros((b, 2, LRU_WIDTH), jnp.float32)
    hf = _rglru_dir(xc, p['lru_wa'][0], p['lru_ba'][0], p['lru_wx'][0], p['lru_bx'][0], p['lru_lam'][0], h0[:, 0], False)
    hb = _rglru_dir(xc, p['lru_wa'][1], p['lru_ba'][1], p['lru_wx'][1], p['lru_bx'][1], p['lru_lam'][1], h0[:, 1], True)
    y_lru = ((hf + hb).astype(x.dtype) * jax.nn.gelu(zlg)) @ p['w_lru_out']
    lru_final = jnp.stack([hf[:, -1], hb[:, 0]], axis=1).astype(x.dtype)
    y_cm = _chunk_mix(zcu, zcv, p['cm_g'], p['cm_ws'], p['cm_bs']) @ p['w_cm_out']
    gates = jax.nn.sigmoid(zg.reshape(b, t, N_BRANCH, D_MODEL))
    merged = gates[:, :, 0] * y_attn + gates[:, :, 1] * y_lru + gates[:, :, 2] * y_cm
    x = x + g1 * _rmsnorm(merged @ p['w_out'], p['g_post_mix'])
    h2 = _rmsnorm(x, p['g_pre_ff']) * (1 + sc2) + sh2
    f = jnp.square(jax.nn.relu(h2 @ p['w_ff1'])) @ p['w_ff2']
    x = x + g2 * _rmsnorm(f, p['g_post_ff'])
    return x, k, v, lru_final


def setup_inputs(seed: int = 0) -> dict:
    key = jax.random.key(seed)
    ks = jax.random.split(key, 32)
    f32 = jnp.float32

    def nrm(k, shape, scale):
        return jax.random.normal(k, shape, f32) * scale

    def gain(k, shape):
        return 1.0 + 0.02 * jax.random.normal(k, shape, f32)

    u = jax.random.uniform(ks[23], (DEPTH, 2, LRU_WIDTH), f32, minval=0.9, maxval=0.999)
    sig = u ** (1.0 / LRU_C)
    lru_lam = jnp.log(sig) - jnp.log1p(-sig)
    return {
        'x_prompt': nrm(ks[0], (BATCH, SEQ, D_MODEL), 1.0),
        'x_sample': nrm(ks[1], (DEC_BATCH, DEC_SEQ, D_MODEL), 1.0),
        'cache_k': nrm(ks[2], (DEC_BATCH, DEPTH, PAST_LEN, N_KV_HEADS, HEAD_DIM), 1.0),
        'cache_v': nrm(ks[3], (DEC_BATCH, DEPTH, PAST_LEN, N_KV_HEADS, HEAD_DIM), 1.0),
        'state_lru': nrm(ks[4], (DEC_BATCH, DEPTH, 2, LRU_WIDTH), 0.5),
        'c': nrm(ks[5], (DEC_BATCH, D_MODEL), 1.0),
        'c_ctx': nrm(ks[6], (D_MODEL,), 1.0),
        'w_mod': nrm(ks[7], (DEPTH, D_MODEL, N_MOD * D_MODEL), 0.5 * D_MODEL ** -0.5),
        'b_mod': nrm(ks[8], (DEPTH, N_MOD * D_MODEL), 0.02),
        'g_pre_mix': gain(ks[9], (DEPTH, D_MODEL)),
        'g_post_mix': gain(ks[10], (DEPTH, D_MODEL)),
        'g_pre_ff': gain(ks[11], (DEPTH, D_MODEL)),
        'g_post_ff': gain(ks[12], (DEPTH, D_MODEL)),
        'w_in': nrm(ks[13], (DEPTH, D_MODEL, IN_W), D_MODEL ** -0.5),
        'g_q': gain(ks[14], (DEPTH, HEAD_DIM)),
        'g_k': gain(ks[15], (DEPTH, HEAD_DIM)),
        'w_attn_out': nrm(ks[16], (DEPTH, Q_W, D_MODEL), Q_W ** -0.5),
        'conv_w': nrm(ks[17], (DEPTH, CONV_W, LRU_WIDTH), CONV_W ** -0.5),
        'conv_b': nrm(ks[18], (DEPTH, LRU_WIDTH), 0.02),
        'lru_wa': nrm(ks[19], (DEPTH, 2, LRU_BLOCKS, LRU_BW, LRU_BW), LRU_BW ** -0.5),
        'lru_ba': nrm(ks[20], (DEPTH, 2, LRU_WIDTH), 0.1),
        'lru_wx': nrm(ks[21], (DEPTH, 2, LRU_BLOCKS, LRU_BW, LRU_BW), LRU_BW ** -0.5),
        'lru_bx': nrm(ks[22], (DEPTH, 2, LRU_WIDTH), 0.1),
        'lru_lam': lru_lam,
        'w_lru_out': nrm(ks[24], (DEPTH, LRU_WIDTH, D_MODEL), LRU_WIDTH ** -0.5),
        'cm_g': gain(ks[25], (DEPTH, CMLP_WIDTH)),
        'cm_ws': nrm(ks[26], (DEPTH, CMLP_GROUPS, CHUNK, CHUNK), CHUNK ** -0.5),
        'cm_bs': nrm(ks[27], (DEPTH, CMLP_GROUPS, CHUNK), 0.1),
        'w_cm_out': nrm(ks[28], (DEPTH, CMLP_WIDTH, D_MODEL), CMLP_WIDTH ** -0.5),
        'w_out': nrm(ks[29], (DEPTH, D_MODEL, D_MODEL), D_MODEL ** -0.5),
        'w_ff1': nrm(ks[30], (DEPTH, D_MODEL, D_FF), D_MODEL ** -0.5),
        'w_ff2': nrm(ks[31], (DEPTH, D_FF, D_MODEL), D_FF ** -0.5),
    }


def reference(x_prompt, x_sample, cache_k, cache_v, state_lru, c, c_ctx,
              w_mod, b_mod, g_pre_mix, g_post_mix, g_pre_ff, g_post_ff,
              w_in, g_q, g_k, w_attn_out, conv_w, conv_b,
              lru_wa, lru_ba, lru_wx, lru_bx, lru_lam, w_lru_out,
              cm_g, cm_ws, cm_bs, w_cm_out, w_out, w_ff1, w_ff2):
    rope = _axial_rope(x_sample.shape[1])
    cond_ctx = c_ctx[None, :]
    y_p = x_prompt
    y_s = x_sample
    new_k, new_v, new_s = [], [], []
    for l in range(DEPTH):
        p = {
            'w_mod': w_mod[l], 'b_mod': b_mod[l],
            'g_pre_mix': g_pre_mix[l], 'g_post_mix': g_post_mix[l],
            'g_pre_ff': g_pre_ff[l], 'g_post_ff': g_post_ff[l],
            'w_in': w_in[l], 'g_q': g_q[l], 'g_k': g_k[l], 'w_attn_out': w_attn_out[l],
            'conv_w': conv_w[l], 'conv_b': conv_b[l],
            'lru_wa': lru_wa[l], 'lru_ba': lru_ba[l], 'lru_wx': lru_wx[l], 'lru_bx': lru_bx[l],
            'lru_lam': lru_lam[l], 'w_lru_out': w_lru_out[l],
            'cm_g': cm_g[l], 'cm_ws': cm_ws[l], 'cm_bs': cm_bs[l], 'w_cm_out': w_cm_out[l],
            'w_out': w_out[l], 'w_ff1': w_ff1[l], 'w_ff2': w_ff2[l],
        }
        y_p, k_l, v_l, s_l = _layer(y_p, cond_ctx, p, None, None, None, None)
        new_k.append(k_l)
        new_v.append(v_l)
        new_s.append(s_l)
        y_s, _, _, _ = _layer(y_s, c, p, rope, cache_k[:, l], cache_v[:, l], state_lru[:, l])
    new_cache_k = jnp.stack(new_k, axis=1)
    new_cache_v = jnp.stack(new_v, axis=1)
    new_state_lru = jnp.stack(new_s, axis=1)
    return (y_prompt_out := y_p, y_s, new_cache_k, new_cache_v, new_state_lru)
```

```python
import numpy as np
from contextlib import ExitStack
import concourse.bass as bass
import concourse.mybir as mybir
from concourse.bass_utils import run_bass_kernel_spmd

F32 = mybir.dt.float32
BF16 = mybir.dt.bfloat16
ALU = mybir.AluOpType
AF = mybir.ActivationFunctionType
ENGS = ("pe", "act", "dve", "pool", "sp")

D = 2048
KC = 16
T = 3072
TSMP = 2048
NTB = 6
DEPTH = 4
INW = 17408
DFF = 8192
OQ, OK_, OV, OLX, OLG, OCU, OCV, OG = 0, 2048, 2560, 3072, 5120, 7168, 9216, 11264
EPS = 1e-6
SCALE = 128 ** -0.5
GC1 = 0.044715
GC2 = 1.5957691216057308
P_BMOD, P_GPM, P_GPO, P_GPF, P_GOF, P_CW, P_CB, P_BA, P_BX, P_LAM, P_GQ, P_GK, NPL = (
    0, 96, 112, 128, 144, 160, 224, 240, 272, 304, 336, 337, 338)


class Chan:
    def __init__(self, sem):
        self.sem = sem
        self.cum = 0


class Sched:
    def __init__(self, nc, stack):
        self.nc = nc
        self.q = {e: [] for e in ENGS}
        self.cnt = {e: 0 for e in ENGS}
        self.esem = {e: stack.enter_context(nc.semaphore("prog_" + e)) for e in ENGS}
        self.seen = {e: {} for e in ENGS}
        self.lastw = {}
        self.readers = {}
        self.stack = stack
        self.chans = {}

    def chan(self, name):
        if name not in self.chans:
            self.chans[name] = Chan(self.stack.enter_context(self.nc.semaphore("ch_" + name)))
        return self.chans[name]

    def _need(self, eng, tok):
        sem, val, _ = tok
        k = id(sem)
        if self.seen[eng].get(k, 0) >= val:
            return
        self.seen[eng][k] = val
        self.q[eng].append(("wait", sem, val))

    def _deps(self, eng, reads, writes):
        for k in reads:
            t = self.lastw.get(k)
            if t is not None:
                self._need(eng, t)
        for k in writes:
            t = self.lastw.get(k)
            if t is not None and t[2] != eng:
                self._need(eng, t)
            for t in self.readers.get(k, {}).values():
                if t[2] != eng:
                    self._need(eng, t)

    def _commit(self, tok, reads, writes):
        for k in writes:
            self.lastw[k] = tok
            self.readers[k] = {}
        for k in reads:
            self.readers.setdefault(k, {})[id(tok[0])] = tok

    def op(self, eng, fn, reads=(), writes=()):
        self._deps(eng, reads, writes)
        self.cnt[eng] += 1
        tok = (self.esem[eng], self.cnt[eng], eng)
        self.q[eng].append(("op", fn))
        self._commit(tok, reads, writes)

    def dma(self, eng, chname, out, in_, reads=(), writes=()):
        ch = self.chan(chname)
        self._deps(eng, reads, writes)
        if ch.cum:
            self._need(eng, (ch.sem, ch.cum, None))
        ch.cum += 16
        tok = (ch.sem, ch.cum, None)
        self.q[eng].append(("dma", out, in_, ch.sem))
        self._commit(tok, reads, writes)

    def barrier(self):
        toks = [(self.esem[e], self.cnt[e], None) for e in ENGS if self.cnt[e]]
        toks += [(c.sem, c.cum, None) for c in self.chans.values() if c.cum]
        for e in ENGS:
            for t in toks:
                self._need(e, t)
        self.lastw = {}
        self.readers = {}

    def emit(self, block):
        engobj = {"pe": "tensor", "act": "scalar", "dve": "vector", "pool": "gpsimd", "sp": "sync"}

        def runner(e):
            def f(eng):
                sem = self.esem[e]
                for item in self.q[e]:
                    if item[0] == "wait":
                        eng.wait_ge(item[1], item[2])
                    elif item[0] == "op":
                        item[1](eng).then_inc(sem, 1)
                    else:
                        eng.dma_start(out=item[1], in_=item[2]).then_inc(item[3], 16)
            return f

        for e in ENGS:
            getattr(block, engobj[e])(runner(e))


def build(nlayers=DEPTH, debug=False, stop=None):
    nc = bass.Bass("TRN2", target_bir_lowering=False)
    st = ExitStack()
    with st:
        def din(name, shape, dt=F32):
            return nc.dram_tensor(name, list(shape), dt, kind="ExternalInput").ap()

        def dout(name, shape, dt=F32):
            return nc.dram_tensor(name, list(shape), dt, kind="ExternalOutput").ap()

        def dscr(name, shape, dt):
            kind = "ExternalOutput" if debug else "Internal"
            return nc.dram_tensor(name, list(shape), dt, kind=kind).ap()

        xT0 = din("xT0", [D, T])
        condT = din("condT", [128, KC, 2])
        kctx = din("kctx", [DEPTH, 512, 256])
        vctx = din("vctx", [DEPTH, 256, 512])
        h0T = din("h0T", [128, DEPTH * 2 * 16])
        ptab = din("ptab", [128, DEPTH * NPL])
        cosT = din("cosT", [128, TSMP])
        sinT = din("sinT", [128, TSMP])
        permd = din("permd", [128, 128])
        w_mod = din("w_mod", [nlayers, D, 6 * D])
        w_in = din("w_in", [nlayers, D, INW])
        w_attn_out = din("w_attn_out", [nlayers, D, D])
        w_lru_out = din("w_lru_out", [nlayers, D, D])
        w_cm_out = din("w_cm_out", [nlayers, D, D])
        w_out = din("w_out", [nlayers, D, D])
        w_ff1 = din("w_ff1", [nlayers, D, DFF])
        w_ff2 = din("w_ff2", [nlayers, DFF, D])
        lru_wa = din("lru_wa", [nlayers, 2, 16, 128, 128])
        lru_wx = din("lru_wx", [nlayers, 2, 16, 128, 128])
        wsT = din("wsT", [nlayers, 128, 16, 128])
        cm_bs = din("cm_bs", [nlayers, 1, D])
        cm_g = din("cm_g", [nlayers, 1, D])

        yT = dout("yT", [D, T])
        knew = dout("knew", [DEPTH, 512, 1024])
        vnew = dout("vnew", [DEPTH, 1024, 512])
        nsT = dout("nsT", [128, 4 * DEPTH * 2 * 16])

        QT = dscr("QT", [D, T], BF16)
        KT = dscr("KT", [512, T], BF16)
        VS = dscr("VS", [T, 512], BF16)
        LRUO = dscr("LRUO", [D, T], BF16)
        GCU = dscr("GCU", [D, T], BF16)
        GCV = dscr("GCV", [T, D], BF16)
        CM = dscr("CM", [D, T], BF16)
        GG = dscr("GG", [3 * D, T], BF16)
        ATT = dscr("ATT", [D, T], BF16)
        OT = dscr("OT", [D, T], F32)
        H2T = dscr("H2T", [D, T], BF16)
        FT = dscr("FT", [D, T], F32)

        S = Sched(nc, st)
        ARENA_W = 53000
        arena = st.enter_context(nc.sbuf_tensor("arena", [128, ARENA_W], F32))
        PS = [st.enter_context(nc.psum_tensor("ps%d" % i, [128, 512], F32)) for i in range(8)]

        def carve(off, nbytes, dt, pat=None, **kw):
            assert off % 4 == 0 and nbytes % 4 == 0 and off + nbytes <= ARENA_W * 4, (off, nbytes)
            a = arena[:, off // 4:(off + nbytes) // 4]
            if dt != F32:
                a = a.bitcast(dt)
            if pat is not None:
                a = a.rearrange(pat, **kw)
            return a

        class Region:
            def __init__(self, base, size):
                self.base, self.size, self.cur = base, size, 0

            def reset(self):
                self.cur = 0

            def get(self, shape, dt=F32):
                n = 1
                for s_ in shape:
                    n *= s_
                nb = n * (4 if dt == F32 else 2)
                nb = (nb + 31) // 32 * 32
                assert self.cur + nb <= self.size, (self.cur, nb, self.size)
                off = self.base + self.cur
                self.cur += nb
                if len(shape) == 1:
                    return carve(off, nb, dt)[:, 0:shape[0]]
                if len(shape) == 2:
                    return carve(off, nb, dt)[:, 0:n].rearrange("p (a b) -> p a b", a=shape[0])
                return carve(off, nb, dt)[:, 0:n].rearrange("p (a b c) -> p a b c", a=shape[0], b=shape[1])

        RC = Region(0, 12288)
        RA = Region(12288, 98304)
        RW = Region(12288 + 98304, 32768)
        RM = Region(12288 + 98304 + 32768, ARENA_W * 4 - (12288 + 98304 + 32768))

        ONES11 = RC.get([128], BF16)
        ONES7 = RC.get([128], BF16)
        ONES1 = RC.get([128], BF16)
        PERM = RC.get([128], BF16)
        PTt = RC.get([DEPTH * NPL])
        CONDB = RC.get([KC, 2], BF16)
        MOD = RC.get([96, 2])
        A1 = RC.get([16, 2]); B1 = MOD[:, 0:16, :]
        G1P = RC.get([16, 2])
        A2 = RC.get([16, 2]); B2 = MOD[:, 48:64, :]
        G2P = RC.get([16, 2])
        NBA = RC.get([32]); NBX = RC.get([32]); CC = RC.get([32]); C2 = RC.get([32])
        NS = RC.get([4 * DEPTH * 2 * 16])
        H0 = RC.get([DEPTH * 2 * 16])
        SSQ = RC.get([96])
        RCV = RC.get([24])
        CTMP = RC.get([KC, 2])

        blk = st.enter_context(nc.Block())

        def ACT(out, in_, func, reads, writes, bias=None, scale=None, accum=None):
            kw = {}
            if bias is not None:
                kw["bias"] = bias
            if scale is not None:
                kw["scale"] = scale
            if accum is not None:
                kw["accum_out"] = accum
            S.op("act", lambda e: e.activation(out=out, in_=in_, func=func, **kw), reads=reads, writes=writes)

        def TSC(out, in0, s1, s2, op0, op1, reads, writes, eng="dve"):
            if op1 is None:
                S.op(eng, lambda e: e.tensor_scalar(out=out, in0=in0, scalar1=s1, scalar2=None, op0=op0), reads=reads, writes=writes)
            else:
                S.op(eng, lambda e: e.tensor_scalar(out=out, in0=in0, scalar1=s1, scalar2=s2, op0=op0, op1=op1), reads=reads, writes=writes)

        def TT(out, in0, in1, op, reads, writes, eng="dve"):
            S.op(eng, lambda e: e.tensor_tensor(out=out, in0=in0, in1=in1, op=op), reads=reads, writes=writes)

        def STT(out, in0, scalar, in1, op0, op1, reads, writes):
            S.op("dve", lambda e: e.scalar_tensor_tensor(out=out, in0=in0, scalar=scalar, in1=in1, op0=op0, op1=op1), reads=reads, writes=writes)

        def CP(out, in_, reads, writes, eng="dve"):
            S.op(eng, lambda e: e.tensor_copy(out=out, in_=in_), reads=reads, writes=writes)

        def MM(ps_ap, pairs, reads, pskey):
            def f(e):
                n = len(pairs)
                ins = None
                for i, (l, r) in enumerate(pairs):
                    ins = e.matmul(ps_ap, lhsT=l, rhs=r, start=(i == 0), stop=(i == n - 1))
                return ins
            S.op("pe", f, reads=reads, writes=[pskey])

        def MSET(ap, val, writes, eng="dve"):
            S.op(eng, lambda e: e.memset(ap, val), writes=writes)

        WSLOT = [RW.get([8192], BF16), RW.get([8192], BF16)]
        wctr = [0]

        class WStream:
            def __init__(self, specs):
                self.specs = specs
                self.issued = 0
                self.info = {}

            def _issue(self, i):
                slot = wctr[0] % 2
                wctr[0] += 1
                parts = self.specs[i]
                kcn = parts[0][1]
                ntot = sum(p[2] for p in parts)
                view = WSLOT[slot][:, 0:kcn * ntot].rearrange("p (kc n) -> p kc n", kc=kcn)
                c0 = 0
                for (src, kcn_, n) in parts:
                    srcv = src.rearrange("(kc p) n -> p kc n", p=128)
                    for k0 in range(0, kcn_, 16):
                        S.dma("pool", "w%d" % slot, view[:, k0:k0 + 16, c0:c0 + n],
                              srcv[:, k0:k0 + 16, :], writes=[("w", slot)])
                    c0 += n
                self.info[i] = (view, ("w", slot))

            def get(self, i):
                while self.issued <= min(i + 1, len(self.specs) - 1):
                    self._issue(self.issued)
                    self.issued += 1
                return self.info[i]

        def psk(i):
            return "ps%d" % i

        S.dma("sp", "c0", PTt, ptab[:, :], writes=["pt"])
        S.dma("sp", "c1", H0, h0T[:, :], writes=["h0"])
        CONDF = RM.get([KC, 2])
        S.dma("sp", "c2", CONDF, condT[:, :, :], writes=["condf"])
        S.dma("pool", "c3", PERM, permd[:, :], writes=["perm"])
        MSET(ONES11, 2.0 ** -11, ["ones11"])
        MSET(ONES7, 2.0 ** -7, ["ones7"])
        MSET(ONES1, 1.0, ["ones1"])
        MSET(NS, 0.0, ["ns"])
        ACT(CTMP, CONDF, AF.Exp, ["condf"], ["ctmp"], scale=-1.0)
        ACT(CTMP, CTMP, AF.Ln, ["ctmp"], ["ctmp"], bias=1.0)
        ACT(CTMP, CTMP, AF.Exp, ["ctmp"], ["ctmp"], scale=-1.0)
        TT(CONDB, CTMP, CONDF, ALU.mult, ["ctmp", "condf"], ["condb"])
        S.barrier()

        for l in range(nlayers):
            pc = l * NPL
            XSRC = xT0 if l == 0 else yT

            def xt_reads(tb, l=l):
                return [] if l == 0 else [("xt", tb)]

            RM.reset()
            ws = WStream([[(w_mod[l][:, cg * 512:(cg + 1) * 512], KC, 512)] for cg in range(24)])
            PSM = PS[0][:, 0:192]
            for cg in range(24):
                wv, wk = ws.get(cg)

                def f(e, wv=wv, cg=cg):
                    ins = None
                    for j in range(4):
                        c0 = (cg * 4 + j) * 2
                        for kc in range(KC):
                            ins = e.matmul(PSM[:, c0:c0 + 2], lhsT=wv[:, kc, j * 128:(j + 1) * 128],
                                           rhs=CONDB[:, kc, :], start=(kc == 0), stop=(kc == KC - 1))
                    return ins
                S.op("pe", f, reads=[wk, "condb"], writes=[psk(0)])
            PSMv = PSM.rearrange("p (a c) -> p a c", c=2)
            for c in range(2):
                TT(MOD[:, :, c], PSMv[:, :, c], PTt[:, pc + P_BMOD:pc + P_BMOD + 96], ALU.add,
                   [psk(0), "pt"], [("mod", c)])
            for c in range(2):
                STT(A1[:, :, c], MOD[:, 16:32, c], 1.0, PTt[:, pc + P_GPM:pc + P_GPM + 16], ALU.add, ALU.mult,
                    [("mod", c), "pt"], [("a1", c)])
                TT(G1P[:, :, c], MOD[:, 32:48, c], PTt[:, pc + P_GPO:pc + P_GPO + 16], ALU.mult,
                   [("mod", c), "pt"], [("g1p", c)])
                STT(A2[:, :, c], MOD[:, 64:80, c], 1.0, PTt[:, pc + P_GPF:pc + P_GPF + 16], ALU.add, ALU.mult,
                    [("mod", c), "pt"], [("a2", c)])
                TT(G2P[:, :, c], MOD[:, 80:96, c], PTt[:, pc + P_GOF:pc + P_GOF + 16], ALU.mult,
                   [("mod", c), "pt"], [("g2p", c)])
            TSC(NBA, PTt[:, pc + P_BA:pc + P_BA + 32], -1.0, None, ALU.mult, None, ["pt"], ["nba"])
            TSC(NBX, PTt[:, pc + P_BX:pc + P_BX + 32], -1.0, None, ALU.mult, None, ["pt"], ["nbx"])
            ACT(CC, PTt[:, pc + P_LAM:pc + P_LAM + 32], AF.Exp, ["pt"], ["cc"], scale=-1.0)
            ACT(CC, CC, AF.Ln, ["cc"], ["cc"], bias=1.0)
            TSC(C2, CC, -16.0, None, ALU.mult, None, ["cc"], ["c2"])
            TSC(CC, CC, -8.0, None, ALU.mult, None, ["cc"], ["cc"])
            S.barrier()
            if stop == "P0":
                break

            RM.reset()
            RA.reset()
            HALL = RA.get([KC, T], BF16)
            XB = [RM.get([KC, 512]), carve(RW.base, 32768, F32, "p (a b) -> p a b", a=KC)]
            SQ = RM.get([KC, 512], BF16)
            RS = RM.get([512])
            TMP = [RM.get([512]), RM.get([512])]

            def norm_stats(src, srckey, sqkey="sq", rskey="rs", psi=1):
                ACT(SQ, src, AF.Square, [srckey], [sqkey])
                MM(PS[psi][:, :], [(ONES11, SQ[:, kc, :]) for kc in range(KC)], ["ones11", sqkey], psk(psi))
                ACT(RS, PS[psi][:, :], AF.Ln, [psk(psi)], [rskey], bias=EPS)
                ACT(RS, RS, AF.Exp, [rskey], [rskey], scale=-0.5)

            for tb in range(NTB):
                c = 0 if tb < 4 else 1
                xb = XB[tb % 2]
                xk = ("xb", tb % 2)
                S.dma("sp", "xb%d" % (tb % 2), xb, XSRC[:, tb * 512:(tb + 1) * 512].rearrange("(kc p) t -> p kc t", p=128),
                      reads=xt_reads(tb), writes=[xk])
                norm_stats(xb, xk)
                for kc in range(KC):
                    tmp = TMP[kc % 2]
                    tk = ("tmp", kc % 2)
                    STT(tmp, xb[:, kc, :], A1[:, kc, c:c + 1], RS, ALU.mult, ALU.mult, [xk, ("a1", c), "rs"], [tk])
                    ACT(HALL[:, kc, tb * 512:(tb + 1) * 512], tmp, AF.Identity, [tk, ("mod", c)], [("hall", tb)],
                        bias=B1[:, kc, c:c + 1])
            S.barrier()
            if stop == "P1":
                break

            RM.reset()
            COS = RM.get([TSMP]); SIN = RM.get([TSMP])
            S.dma("sp", "cos", COS, cosT[:, :], writes=["cos"])
            S.dma("sp", "sin", SIN, sinT[:, :], writes=["sin"])
            SQ1 = [RM.get([512], BF16) for _ in range(2)]
            RS1 = [RM.get([512]) for _ in range(2)]
            QN = [RM.get([512]) for _ in range(2)]
            QNB = [RM.get([512], BF16) for _ in range(2)]
            T1 = [RM.get([512]) for _ in range(2)]
            T2 = [RM.get([512]) for _ in range(2)]
            QO = [RM.get([512], BF16) for _ in range(2)]
            VTb = [RM.get([512], BF16) for _ in range(2)]
            VF = [RM.get([512]) for _ in range(2)]
            specs = [[(w_in[l][:, OQ + g * 512:OQ + (g + 1) * 512], KC, 512)] for g in range(4)]
            specs.append([(w_in[l][:, OK_:OK_ + 512], KC, 512)])
            specs.append([(w_in[l][:, OV:OV + 512], KC, 512)])
            ws = WStream(specs)
            it = 0
            lim = stop.split(":")[1] if (stop and stop.startswith("P2a:")) else None
            for g in range({None: 5, "q1": 1, "q": 4, "qk": 5, "norope": 1, "nodma": 1}[lim]):
                wv, wk = ws.get(g)
                isk = (g == 4)
                gcol = pc + (P_GK if isk else P_GQ)
                for tb in range(1 if lim in ("q1", "norope", "nodma") else NTB):
                    hk_ = ("hall", tb)
                    for j in range(4):
                        r = it % 2
                        it += 1
                        pz, psn, pw = 2 + r, 4 + r, 6 + r
                        MM(PS[pz][:, :], [(wv[:, kc, j * 128:(j + 1) * 128], HALL[:, kc, tb * 512:(tb + 1) * 512]) for kc in range(KC)],
                           [wk, hk_], psk(pz))
                        ACT(SQ1[r], PS[pz][:, :], AF.Square, [psk(pz)], [("sq1", r)])
                        MM(PS[psn][:, :], [(ONES7, SQ1[r])], ["ones7", ("sq1", r)], psk(psn))
                        ACT(RS1[r], PS[psn][:, :], AF.Ln, [psk(psn)], [("rs1", r)], bias=EPS)
                        ACT(RS1[r], RS1[r], AF.Exp, [("rs1", r)], [("rs1", r)], scale=-0.5)
                        STT(QN[r], PS[pz][:, :], PTt[:, gcol:gcol + 1], RS1[r], ALU.mult, ALU.mult,
                            [psk(pz), "pt", ("rs1", r)], [("qn", r)])
                        if tb < 4 and lim != "norope":
                            ACT(QNB[r], QN[r], AF.Copy, [("qn", r)], [("qnb", r)])
                            MM(PS[pw][:, :], [(PERM, QNB[r])], ["perm", ("qnb", r)], psk(pw))
                            TT(T1[r], QN[r], COS[:, tb * 512:(tb + 1) * 512], ALU.mult, [("qn", r), "cos"], [("t1", r)])
                            TT(T2[r], PS[pw][:, :], SIN[:, tb * 512:(tb + 1) * 512], ALU.mult, [psk(pw), "sin"], [("t2", r)])
                            TT(QO[r], T1[r], T2[r], ALU.add, [("t1", r), ("t2", r)], [("qo", r)])
                        else:
                            ACT(QO[r], QN[r], AF.Copy, [("qn", r)], [("qo", r)])
                        if lim == "nodma":
                            pass
                        elif isk:
                            S.dma("sp", "qo%d" % r, KT[j * 128:(j + 1) * 128, tb * 512:(tb + 1) * 512], QO[r],
                                  reads=[("qo", r)], writes=[("kt", tb)])
                            if tb >= 4:
                                S.dma("sp", "kf%d" % r, knew[l][j * 128:(j + 1) * 128, (tb - 4) * 512:(tb - 3) * 512], QN[r],
                                      reads=[("qn", r)], writes=[])
                        else:
                            h = g * 4 + j
                            S.dma("sp", "qo%d" % r, QT[h * 128:(h + 1) * 128, tb * 512:(tb + 1) * 512], QO[r],
                                  reads=[("qo", r)], writes=[("qt", tb)])
            wv, wk = ws.get(5) if lim is None else (None, None)
            for tt in range(24 if lim is None else 0):
                r = tt % 2
                pv = 0 + r
                MM(PS[pv][:, :], [(HALL[:, kc, tt * 128:(tt + 1) * 128], wv[:, kc, :]) for kc in range(KC)],
                   [wk, ("hall", tt // 4)], psk(pv))
                ACT(VTb[r], PS[pv][:, :], AF.Copy, [psk(pv)], [("vt", r)])
                S.dma("sp", "vt%d" % r, VS[tt * 128:(tt + 1) * 128, :], VTb[r], reads=[("vt", r)], writes=[("vs", tt // 4)])
                if tt >= 16:
                    ACT(VF[r], PS[pv][:, :], AF.Copy, [psk(pv)], [("vf", r)])
                    S.dma("sp", "vf%d" % r, vnew[l][(tt - 16) * 128:(tt - 15) * 128, :], VF[r], reads=[("vf", r)], writes=[])
            S.barrier()
            if stop and stop.startswith("P2a"):
                break

            RM.reset()
            XPS = RM.get([2052])
            XPP = RM.get([4, 259])
            GLG = RM.get([T], BF16)
            XC = RM.get([TSMP])
            XCB = RM.get([TSMP], BF16)
            HF = RM.get([TSMP])
            G1 = RM.get([512]); AA = RM.get([512]); G2 = RM.get([512])
            HBB = [RM.get([512]), RM.get([512])]
            SM = RM.get([512])
            GX2 = [RM.get([512]) for _ in range(2)]
            GXS = [RM.get([512]) for _ in range(2)]
            GW = [RM.get([512]) for _ in range(2)]
            LW = [RM.get([4, 128], BF16) for _ in range(2)]
            MSET(XPS, 0.0, ["xp"])
            MSET(XPP, 0.0, ["xp"])
            ws = WStream([[(w_in[l][:, OLX + n * 128:OLX + (n + 1) * 128], KC, 128),
                           (w_in[l][:, OLG + n * 128:OLG + (n + 1) * 128], KC, 128)] for n in range(16)])

            def gelu6(ps_ap, pkey, out_ap, outkeys, r):
                x2, xs, w_ = GX2[r], GXS[r], GW[r]
                ACT(x2, ps_ap, AF.Square, [pkey], [("gx2", r)])
                ACT(xs, ps_ap, AF.Copy, [pkey], [("gxs", r)])
                TSC(w_, x2, GC1, 1.0, ALU.mult, ALU.add, [("gx2", r)], [("gw", r)])
                TT(w_, w_, xs, ALU.mult, [("gw", r), ("gxs", r)], [("gw", r)])
                ACT(x2, w_, AF.Exp, [("gw", r)], [("gx2", r)], scale=-GC2)
                ACT(x2, x2, AF.Ln, [("gx2", r)], [("gx2", r)], bias=1.0)
                ACT(x2, x2, AF.Exp, [("gx2", r)], [("gx2", r)], scale=-1.0)
                TT(out_ap, x2, xs, ALU.mult, [("gx2", r), ("gxs", r)], outkeys)

            def lru_gates(n, d, xcb_blk, xc_blk, lw, lwk):
                idx = d * 16 + n
                MM(PS[2][:, :], [(lw[:, d, :], xcb_blk)], [lwk, "xcb"], psk(2))
                MM(PS[3][:, :], [(lw[:, 2 + d, :], xcb_blk)], [lwk, "xcb"], psk(3))
                ACT(G1, PS[2][:, :], AF.Exp, [psk(2), "nba"], ["g1"], scale=-1.0, bias=NBA[:, idx:idx + 1])
                ACT(G2, PS[3][:, :], AF.Exp, [psk(3), "nbx"], ["g2"], scale=-1.0, bias=NBX[:, idx:idx + 1])
                ACT(G1, G1, AF.Ln, ["g1"], ["g1"], bias=1.0)
                ACT(G2, G2, AF.Ln, ["g2"], ["g2"], bias=1.0)
                ACT(G1, G1, AF.Exp, ["g1"], ["g1"], scale=-1.0)
                ACT(G2, G2, AF.Exp, ["g2"], ["g2"], scale=-1.0)
                ACT(AA, G1, AF.Exp, ["g1", "cc"], ["aa"], scale=CC[:, idx:idx + 1])
                TT(G2, G2, xc_blk, ALU.mult, ["g2", "xc"], ["g2"])
                ACT(G1, G1, AF.Exp, ["g1", "c2"], ["g1"], scale=C2[:, idx:idx + 1])
                ACT(G1, G1, AF.Ln, ["g1"], ["g1"], scale=-0.9999999, bias=1.0)
                ACT(G1, G1, AF.Exp, ["g1"], ["g1"], scale=0.5)
                TT(G2, G1, G2, ALU.mult, ["g1", "g2"], ["g2"])

            def conv(dst, src_at, n):
                cw = pc + P_CW
                TSC(dst, src_at(0), PTt[:, cw + n:cw + n + 1], PTt[:, pc + P_CB + n:pc + P_CB + n + 1],
                    ALU.mult, ALU.add, ["xp", "pt"], ["xc"])
                for j in range(1, 4):
                    STT(dst, src_at(j), PTt[:, cw + j * 16 + n:cw + j * 16 + n + 1], dst, ALU.mult, ALU.add,
                        ["xp", "pt", "xc"], ["xc"])

            git = 0
            for n in range(16):
                wv, wk = ws.get(n)
                lw = LW[n % 2]
                lwk = ("lw", n % 2)
                S.dma("pool", "lwa%d" % (n % 2), lw[:, 0:2, :], lru_wa[l][:, n].rearrange("d c e -> c d e"), writes=[lwk])
                S.dma("pool", "lwx%d" % (n % 2), lw[:, 2:4, :], lru_wx[l][:, n].rearrange("d c e -> c d e"), writes=[lwk])
                for tb in range(NTB):
                    r = git % 2
                    git += 1
                    px, pg = 0 + r, 4 + r
                    MM(PS[px][:, :], [(wv[:, kc, 0:128], HALL[:, kc, tb * 512:(tb + 1) * 512]) for kc in range(KC)],
                       [wk, ("hall", tb)], psk(px))
                    MM(PS[pg][:, :], [(wv[:, kc, 128:256], HALL[:, kc, tb * 512:(tb + 1) * 512]) for kc in range(KC)],
                       [wk, ("hall", tb)], psk(pg))
                    if tb < 4:
                        ACT(XPS[:, 2 + tb * 512:2 + (tb + 1) * 512], PS[px][:, :], AF.Copy, [psk(px)], ["xp"])
                    else:
                        ACT(XPP[:, (tb - 4) * 2:(tb - 3) * 2, 2:258], PS[px][:, :].rearrange("p (s t) -> p s t", s=2),
                            AF.Copy, [psk(px)], ["xp"])
                    gelu6(PS[pg][:, :], psk(pg), GLG[:, tb * 512:(tb + 1) * 512], ["glg"], r)
                conv(XC, lambda j: XPS[:, j:j + TSMP], n)
                ACT(XCB, XC, AF.Copy, ["xc"], ["xcb"])
                for tb in range(4):
                    sl = slice(tb * 512, (tb + 1) * 512)
                    lru_gates(n, 0, XCB[:, sl], XC[:, sl], lw, lwk)
                    init = H0[:, (l * 2 + 0) * 16 + n:(l * 2 + 0) * 16 + n + 1] if tb == 0 else HF[:, tb * 512 - 1:tb * 512]
                    S.op("dve", lambda e, sl=sl, init=init: e.tensor_tensor_scan(
                        out=HF[:, sl], data0=AA, data1=G2, initial=init, op0=ALU.mult, op1=ALU.add),
                        reads=["aa", "g2", "hf", "h0"], writes=["hf"])
                for tb in range(3, -1, -1):
                    sl = slice(tb * 512, (tb + 1) * 512)
                    hb = HBB[tb % 2]
                    lru_gates(n, 1, XCB[:, sl], XC[:, sl], lw, lwk)
                    init = H0[:, (l * 2 + 1) * 16 + n:(l * 2 + 1) * 16 + n + 1] if tb == 3 else HBB[(tb + 1) % 2][:, 0:1]
                    S.op("dve", lambda e, hb=hb, init=init: e.tensor_tensor_scan(
                        out=hb[:, ::-1], data0=AA[:, ::-1], data1=G2[:, ::-1], initial=init, op0=ALU.mult, op1=ALU.add),
                        reads=["aa", "g2", ("hbb", (tb + 1) % 2), "h0"], writes=[("hbb", tb % 2)])
                    TT(SM, hb, HF[:, sl], ALU.add, [("hbb", tb % 2), "hf"], ["sm"])
                    TT(GLG[:, sl], SM, GLG[:, sl], ALU.mult, ["sm", "glg"], ["glg"])
                S.dma("sp", "lruo", LRUO[n * 128:(n + 1) * 128, 0:TSMP], GLG[:, 0:TSMP], reads=["glg"], writes=[("lruo", n)])
                XCp = XC[:, 0:1024].rearrange("p (s t) -> p s t", s=4)
                conv(XCp, lambda j: XPP[:, :, j:j + 256], n)
                ACT(XCB[:, 0:1024], XC[:, 0:1024], AF.Copy, ["xc"], ["xcb"])
                for tb in range(2):
                    sl = slice(tb * 512, (tb + 1) * 512)
                    lru_gates(n, 0, XCB[:, sl], XC[:, sl], lw, lwk)
                    for s2 in range(2):
                        ss = slice(tb * 512 + s2 * 256, tb * 512 + (s2 + 1) * 256)
                        sb = slice(s2 * 256, (s2 + 1) * 256)
                        S.op("dve", lambda e, ss=ss, sb=sb: e.tensor_tensor_scan(
                            out=HF[:, ss], data0=AA[:, sb], data1=G2[:, sb], initial=0.0, op0=ALU.mult, op1=ALU.add),
                            reads=["aa", "g2"], writes=["hf"])
                for tb in range(2):
                    sl = slice(tb * 512, (tb + 1) * 512)
                    hb = HBB[tb % 2]
                    lru_gates(n, 1, XCB[:, sl], XC[:, sl], lw, lwk)
                    for s2 in range(2):
                        sb = slice(s2 * 256, (s2 + 1) * 256)
                        S.op("dve", lambda e, hb=hb, sb=sb: e.tensor_tensor_scan(
                            out=hb[:, sb][:, ::-1], data0=AA[:, sb][:, ::-1], data1=G2[:, sb][:, ::-1], initial=0.0,
                            op0=ALU.mult, op1=ALU.add),
                            reads=["aa", "g2"], writes=[("hbb", tb % 2)])
                    for s2 in range(2):
                        seq = tb * 2 + s2
                        o = ((seq * DEPTH + l) * 2) * 16 + n
                        CP(NS[:, o:o + 1], HF[:, seq * 256 + 255:seq * 256 + 256], ["hf"], ["ns"])
                        CP(NS[:, o + 16:o + 17], hb[:, s2 * 256:s2 * 256 + 1], [("hbb", tb % 2)], ["ns"])
                    TT(SM, hb, HF[:, sl], ALU.add, [("hbb", tb % 2), "hf"], ["sm"])
                    gs = slice(TSMP + tb * 512, TSMP + (tb + 1) * 512)
                    TT(GLG[:, gs], SM, GLG[:, gs], ALU.mult, ["sm", "glg"], ["glg"])
                S.dma("sp", "lruo2", LRUO[n * 128:(n + 1) * 128, TSMP:T], GLG[:, TSMP:T], reads=["glg"], writes=[("lruo", n)])
            S.barrier()
            if stop == "P2b":
                break

            RM.reset()
            GX2 = [RM.get([512]) for _ in range(2)]
            GXS = [RM.get([512]) for _ in range(2)]
            GW = [RM.get([512]) for _ in range(2)]
            GO = [RM.get([512], BF16) for _ in range(2)]
            JUNK = RM.get([512], BF16)
            MSET(SSQ, 0.0, ["ssq"])

            def gelut(ps_ap, pkey, out_ap, outkeys, r):
                x2, xh, w_ = GX2[r], GXS[r], GW[r]
                ACT(x2, ps_ap, AF.Square, [pkey], [("gx2", r)])
                ACT(xh, ps_ap, AF.Identity, [pkey], [("gxs", r)], scale=0.5)
                TSC(w_, x2, GC1, 1.0, ALU.mult, ALU.add, [("gx2", r)], [("gw", r)])
                TT(w_, w_, xh, ALU.mult, [("gw", r), ("gxs", r)], [("gw", r)])
                ACT(x2, w_, AF.Tanh, [("gw", r)], [("gx2", r)], scale=GC2)
                STT(out_ap, x2, 1.0, xh, ALU.add, ALU.mult, [("gx2", r), ("gxs", r)], outkeys)

            specs = [[(w_in[l][:, OCU + g * 512:OCU + (g + 1) * 512], KC, 512)] for g in range(4)]
            specs += [[(w_in[l][:, OCV + g * 512:OCV + (g + 1) * 512], KC, 512)] for g in range(4)]
            specs += [[(w_in[l][:, OG + g * 512:OG + (g + 1) * 512], KC, 512)] for g in range(12)]
            ws = WStream(specs)
            it = 0
            for g in range(4):
                wv, wk = ws.get(g)
                for tb in range(NTB):
                    for j in range(4):
                        r = it % 2
                        it += 1
                        pz = 0 + r
                        MM(PS[pz][:, :], [(wv[:, kc, j * 128:(j + 1) * 128], HALL[:, kc, tb * 512:(tb + 1) * 512]) for kc in range(KC)],
                           [wk, ("hall", tb)], psk(pz))
                        gelut(PS[pz][:, :], psk(pz), GO[r], [("go", r)], r)
                        fc = g * 4 + j
                        S.dma("sp", "go%d" % r, GCU[fc * 128:(fc + 1) * 128, tb * 512:(tb + 1) * 512], GO[r],
                              reads=[("go", r)], writes=[("gcu", tb)])
            for g in range(4):
                wv, wk = ws.get(4 + g)
                for tt in range(24):
                    r = it % 2
                    it += 1
                    pz = 0 + r
                    MM(PS[pz][:, :], [(HALL[:, kc, tt * 128:(tt + 1) * 128], wv[:, kc, :]) for kc in range(KC)],
                       [wk, ("hall", tt // 4)], psk(pz))
                    gelut(PS[pz][:, :], psk(pz), GO[r], [("go", r)], r)
                    ACT(JUNK, GO[r], AF.Square, [("go", r)], ["junk", "ssq"], accum=SSQ[:, tt * 4 + g:tt * 4 + g + 1])
                    S.dma("sp", "go%d" % r, GCV[tt * 128:(tt + 1) * 128, g * 512:(g + 1) * 512], GO[r],
                          reads=[("go", r)], writes=[("gcv", tt)])
            for g in range(12):
                wv, wk = ws.get(8 + g)
                for tb in range(NTB):
                    for j in range(4):
                        r = it % 2
                        it += 1
                        pz = 0 + r
                        MM(PS[pz][:, :], [(wv[:, kc, j * 128:(j + 1) * 128], HALL[:, kc, tb * 512:(tb + 1) * 512]) for kc in range(KC)],
                           [wk, ("hall", tb)], psk(pz))
                        ACT(GO[r], PS[pz][:, :], AF.Tanh, [psk(pz)], [("go", r)], scale=0.5)
                        fc = g * 4 + j
                        S.dma("sp", "go%d" % r, GG[fc * 128:(fc + 1) * 128, tb * 512:(tb + 1) * 512], GO[r],
                              reads=[("go", r)], writes=[("gg", tb)])
            S.barrier()
            if stop == "P2c":
                break

            RM.reset()
            RA.reset()
            GCUB = [RA.get([KC, 512], BF16) for _ in range(2)]
            CMB = [RA.get([KC, 512], BF16) for _ in range(2)]
            BSB = RA.get([D])
            CMG = RA.get([D])
            WST = RM.get([16, 128], BF16)
            GCVT = [RM.get([D], BF16) for _ in range(2)]
            VCM = [RM.get([D], BF16) for _ in range(2)]
            TMPX = [RM.get([512]) for _ in range(2)]
            SS = RM.get([24])
            S.op("dve", lambda e: e.tensor_reduce(out=SS, in_=SSQ.rearrange("p (t g) -> p t g", g=4),
                                                  axis=mybir.AxisListType.X, op=ALU.add), reads=["ssq"], writes=["ss"])
            ACT(RCV, SS, AF.Ln, ["ss"], ["rcv"], scale=1.0 / D, bias=EPS)
            ACT(RCV, RCV, AF.Exp, ["rcv"], ["rcv"], scale=-0.5)
            S.dma("pool", "wst", WST, wsT[l], writes=["wst"])
            S.dma("sp", "bsb", BSB, cm_bs[l][0:1, :].partition_broadcast(128), writes=["bsb"])
            S.dma("sp", "cmg", CMG, cm_g[l][0:1, :].partition_broadcast(128), writes=["cmg"])
            it = 0
            for tb in range(NTB):
                rb = tb % 2
                S.dma("sp", "gcub%d" % rb, GCUB[rb], GCU[:, tb * 512:(tb + 1) * 512].rearrange("(kc p) t -> p kc t", p=128),
                      reads=[("gcu", tb)], writes=[("gcub", rb)])
                for t4 in range(4):
                    tt = tb * 4 + t4
                    rt = tt % 2
                    S.dma("sp", "gcvt%d" % rt, GCVT[rt], GCV[tt * 128:(tt + 1) * 128, :], reads=[("gcv", tt)], writes=[("gcvt", rt)])
                    STT(VCM[rt], GCVT[rt], RCV[:, tt:tt + 1], CMG, ALU.mult, ALU.mult, [("gcvt", rt), "rcv", "cmg"], [("vcm", rt)])
                    for g4 in range(4):
                        r = it % 2
                        it += 1
                        px = 0 + r

                        def f(e, px=px, rt=rt, g4=g4):
                            ins = None
                            for gi in range(4):
                                g = g4 * 4 + gi
                                ins = e.matmul(PS[px][:, gi * 128:(gi + 1) * 128], lhsT=VCM[rt][:, g * 128:(g + 1) * 128],
                                               rhs=WST[:, g, :], start=True, stop=True)
                            return ins
                        S.op("pe", f, reads=[("vcm", rt), "wst"], writes=[psk(px)])
                        TT(TMPX[r], PS[px][:, :], BSB[:, g4 * 512:(g4 + 1) * 512], ALU.add, [psk(px), "bsb"], [("tmpx", r)])
                        TT(CMB[rb][:, g4 * 4:(g4 + 1) * 4, t4 * 128:(t4 + 1) * 128],
                           TMPX[r].rearrange("p (g q) -> p g q", g=4),
                           GCUB[rb][:, g4 * 4:(g4 + 1) * 4, t4 * 128:(t4 + 1) * 128], ALU.mult,
                           [("tmpx", r), ("gcub", rb)], [("cmb", rb)])
                S.dma("sp", "cmb%d" % rb, CM[:, tb * 512:(tb + 1) * 512].rearrange("(kc p) t -> p kc t", p=128), CMB[rb],
                      reads=[("cmb", rb)], writes=[("cm", tb)])
            S.barrier()
            if stop == "P2d":
                break

            RM.reset()
            RA.reset()
            KTS = RA.get([4, 2304], BF16)
            VSS = RA.get([18, 512], BF16)
            KTP = RA.get([4, 1024], BF16)
            VSP = RA.get([8, 512], BF16)
            QB = [RA.get([16, 512], BF16) for _ in range(2)]
            ATTB = [RM.get([16, 512], BF16) for _ in range(2)]
            PTL = [RM.get([512], BF16) for _ in range(4)]
            RD = [RM.get([512]) for _ in range(2)]
            S.dma("sp", "kts", KTS[:, :, 0:TSMP], KT[:, 0:TSMP].rearrange("(h d) t -> d h t", d=128),
                  reads=[("kt", tb) for tb in range(4)], writes=["kts"])
            S.dma("pool", "ktsc", KTS[:, :, TSMP:2304], kctx[l].rearrange("(h d) t -> d h t", d=128), writes=["kts"])
            S.dma("sp", "vss", VSS[:, 0:16, :], VS[0:TSMP, :].rearrange("(tt p) c -> p tt c", p=128),
                  reads=[("vs", tb) for tb in range(4)], writes=["vss"])
            S.dma("pool", "vssc", VSS[:, 16:18, :], vctx[l].rearrange("(tt p) c -> p tt c", p=128), writes=["vss"])
            S.dma("sp", "ktp", KTP, KT[:, TSMP:T].rearrange("(h d) t -> d h t", d=128),
                  reads=[("kt", 4), ("kt", 5)], writes=["ktp"])
            S.dma("sp", "vsp", VSP, VS[TSMP:T, :].rearrange("(tt p) c -> p tt c", p=128),
                  reads=[("vs", 4), ("vs", 5)], writes=["vsp"])
            ai = 0
            pi_ = 0
            for tb in range(NTB):
                rb = tb % 2
                S.dma("sp", "qb%d" % rb, QB[rb], QT[:, tb * 512:(tb + 1) * 512].rearrange("(h d) t -> d h t", d=128),
                      reads=[("qt", tb)], writes=[("qb", rb)])
                for hk in range(4):
                    for qs in range(4):
                        ra = ai % 2
                        ai += 1
                        po, pd = 4 + ra, 6 + ra
                        rhs = QB[rb][:, hk * 4:(hk + 1) * 4, qs * 128:(qs + 1) * 128]
                        if tb < 4:
                            tiles = [(KTS[:, hk, kt * 128:(kt + 1) * 128], VSS[:, kt, hk * 128:(hk + 1) * 128]) for kt in range(18)]
                            kv = ["kts", "vss"]
                        else:
                            seq = (tb - 4) * 2 + qs // 2
                            tiles = [(KTP[:, hk, seq * 256 + kt * 128:seq * 256 + (kt + 1) * 128],
                                      VSP[:, seq * 2 + kt, hk * 128:(hk + 1) * 128]) for kt in range(2)]
                            kv = ["ktp", "vsp"]
                        nk = len(tiles)
                        slots = []

                        def qk(i):
                            ps_i = pi_ % 4
                            MM(PS[ps_i][:, :], [(tiles[i][0], rhs)], [kv[0], ("qb", rb)], psk(ps_i))
                            return ps_i
                        ps_i = qk(0)
                        pi_ += 1
                        for kt in range(nk):
                            cur = ps_i
                            if kt + 1 < nk:
                                ps_i = qk(kt + 1)
                                pi_ += 1
                            pt = PTL[cur]
                            ACT(pt, PS[cur][:, :], AF.Exp, [psk(cur)], [("ptl", cur)], scale=SCALE)
                            st_, sp_ = (kt == 0), (kt == nk - 1)
                            S.op("pe", lambda e, po=po, pd=pd, v=tiles[kt][1], pt=pt, st_=st_, sp_=sp_: (
                                e.matmul(PS[po][:, :], lhsT=v, rhs=pt, start=st_, stop=sp_),
                                e.matmul(PS[pd][:, :], lhsT=ONES1, rhs=pt, start=st_, stop=sp_))[1],
                                reads=[kv[1], ("ptl", cur), "ones1"], writes=[psk(po), psk(pd)])
                        S.op("dve", lambda e, ra=ra, pd=pd: e.reciprocal(out=RD[ra], in_=PS[pd][:, :]), reads=[psk(pd)], writes=[("rd", ra)])
                        TT(ATTB[rb][:, hk * 4:(hk + 1) * 4, qs * 128:(qs + 1) * 128],
                           PS[po][:, :].rearrange("p (g q) -> p g q", g=4), RD[ra].rearrange("p (g q) -> p g q", g=4),
                           ALU.mult, [psk(po), ("rd", ra)], [("attb", rb)])
                S.dma("sp", "attb%d" % rb, ATT[:, tb * 512:(tb + 1) * 512].rearrange("(h d) t -> d h t", d=128), ATTB[rb],
                      reads=[("attb", rb)], writes=[("att", tb)])
            S.barrier()
            if stop == "P3":
                break

            RM.reset()
            RA.reset()
            A3 = [RA.get([KC, 768], BF16) for _ in range(3)]
            MRG = RA.get([KC, 768], BF16)
            ACC = RM.get([4, 768])
            GT = [RM.get([4, 768], BF16) for _ in range(2)]
            TM = [RM.get([384]) for _ in range(2)]
            OTL = [RM.get([384]) for _ in range(2)]
            srcs = [ATT, LRUO, CM]
            wouts = [w_attn_out, w_lru_out, w_cm_out]
            it = 0
            gi_ = 0
            for tg in range(4):
                c0 = tg * 768
                allk = [("att", tb) for tb in range(NTB)] + [("lruo", n) for n in range(16)] + [("cm", tb) for tb in range(NTB)]
                for b in range(3):
                    S.dma("sp", "a3%d" % b, A3[b], srcs[b][:, c0:c0 + 768].rearrange("(kc p) t -> p kc t", p=128),
                          reads=allk, writes=[("a3", b)])
                specs = []
                for cg in range(4):
                    for b in range(3):
                        specs.append([(wouts[b][l][:, cg * 512:(cg + 1) * 512], KC, 512)])
                for cg in range(4):
                    specs.append([(w_out[l][:, cg * 512:(cg + 1) * 512], KC, 512)])
                ws = WStream(specs)
                for cg in range(4):
                    for b in range(3):
                        wv, wk = ws.get(cg * 3 + b)
                        gr = gi_ % 2
                        gi_ += 1
                        r0 = b * D + cg * 512
                        S.dma("sp", "gt%d" % gr, GT[gr], GG[r0:r0 + 512, c0:c0 + 768].rearrange("(j p) t -> p j t", p=128),
                              reads=[("gg", tb) for tb in range(NTB)], writes=[("gt", gr)])
                        for sub in range(2):
                            ss = slice(sub * 384, (sub + 1) * 384)
                            for j in range(4):
                                r = it % 2
                                it += 1
                                pz = 0 + r
                                MM(PS[pz][:, 0:384], [(wv[:, kc, j * 128:(j + 1) * 128], A3[b][:, kc, ss]) for kc in range(KC)],
                                   [wk, ("a3", b)], psk(pz))
                                if b == 0:
                                    STT(ACC[:, j, ss], GT[gr][:, j, ss], 1.0, PS[pz][:, 0:384], ALU.add, ALU.mult,
                                        [("gt", gr), psk(pz)], [("acc", j, sub)])
                                else:
                                    STT(TM[r], GT[gr][:, j, ss], 1.0, PS[pz][:, 0:384], ALU.add, ALU.mult,
                                        [("gt", gr), psk(pz)], [("tm", r)])
                                    TT(ACC[:, j, ss], ACC[:, j, ss], TM[r], ALU.add, [("acc", j, sub), ("tm", r)], [("acc", j, sub)])
                                if b == 2:
                                    ACT(MRG[:, cg * 4 + j, ss], ACC[:, j, ss], AF.Copy, [("acc", j, sub)], ["mrg"])
                for cg in range(4):
                    wv, wk = ws.get(12 + cg)
                    for sub in range(2):
                        ss = slice(sub * 384, (sub + 1) * 384)
                        for j in range(4):
                            r = it % 2
                            it += 1
                            pz = 0 + r
                            MM(PS[pz][:, 0:384], [(wv[:, kc, j * 128:(j + 1) * 128], MRG[:, kc, ss]) for kc in range(KC)],
                               [wk, "mrg"], psk(pz))
                            ACT(OTL[r], PS[pz][:, 0:384], AF.Identity, [psk(pz)], [("otl", r)], scale=0.5)
                            fc = cg * 4 + j
                            S.dma("sp", "otl%d" % r, OT[fc * 128:(fc + 1) * 128, c0 + sub * 384:c0 + (sub + 1) * 384], OTL[r],
                                  reads=[("otl", r)], writes=["ot"])
            S.barrier()
            if stop == "P4a":
                break

            RM.reset()
            RA.reset()
            OB = RA.get([KC, 512])
            XBr = RA.get([KC, 512])
            SQ = RA.get([KC, 512], BF16)
            RS = RM.get([512])
            TMP = [RM.get([512]), RM.get([512])]
            H2O = RM.get([KC, 512], BF16)
            for tb in range(NTB):
                c = 0 if tb < 4 else 1
                S.dma("sp", "ob", OB, OT[:, tb * 512:(tb + 1) * 512].rearrange("(kc p) t -> p kc t", p=128), reads=["ot"], writes=["ob"])
                S.dma("sp", "xbr", XBr, XSRC[:, tb * 512:(tb + 1) * 512].rearrange("(kc p) t -> p kc t", p=128),
                      reads=xt_reads(tb), writes=["xbr"])
                norm_stats(OB, "ob")
                for kc in range(KC):
                    tmp, tk = TMP[kc % 2], ("tmp", kc % 2)
                    STT(tmp, OB[:, kc, :], G1P[:, kc, c:c + 1], RS, ALU.mult, ALU.mult, ["ob", ("g1p", c), "rs"], [tk])
                    TT(XBr[:, kc, :], XBr[:, kc, :], tmp, ALU.add, ["xbr", tk], ["xbr"])
                S.dma("sp", "xst", yT[:, tb * 512:(tb + 1) * 512].rearrange("(kc p) t -> p kc t", p=128), XBr,
                      reads=["xbr"], writes=[("xt", tb)])
                norm_stats(XBr, "xbr")
                for kc in range(KC):
                    tmp, tk = TMP[kc % 2], ("tmp", kc % 2)
                    STT(tmp, XBr[:, kc, :], A2[:, kc, c:c + 1], RS, ALU.mult, ALU.mult, ["xbr", ("a2", c), "rs"], [tk])
                    ACT(H2O[:, kc, :], tmp, AF.Identity, [tk, ("mod", c)], ["h2o"], bias=B2[:, kc, c:c + 1])
                S.dma("sp", "h2o", H2T[:, tb * 512:(tb + 1) * 512].rearrange("(kc p) t -> p kc t", p=128), H2O,
                      reads=["h2o"], writes=["h2t"])
            S.barrier()
            if stop == "P4b":
                break

            RM.reset()
            RA.reset()
            F1 = RA.get([64, 768], BF16)
            H2G = RM.get([KC, 768], BF16)
            SQX = [RM.get([384]) for _ in range(2)]
            FO = [RM.get([384]) for _ in range(2)]
            it = 0
            for tg in range(4):
                c0 = tg * 768
                S.dma("sp", "h2g", H2G, H2T[:, c0:c0 + 768].rearrange("(kc p) t -> p kc t", p=128), reads=["h2t"], writes=["h2g"])
                specs = [[(w_ff1[l][:, cg * 512:(cg + 1) * 512], KC, 512)] for cg in range(16)]
                specs += [[(w_ff2[l][:, og * 128:(og + 1) * 128], 64, 128)] for og in range(16)]
                ws = WStream(specs)
                for cg in range(16):
                    wv, wk = ws.get(cg)
                    for sub in range(2):
                        ss = slice(sub * 384, (sub + 1) * 384)
                        for j in range(4):
                            r = it % 2
                            it += 1
                            pz = 0 + r
                            MM(PS[pz][:, 0:384], [(wv[:, kc, j * 128:(j + 1) * 128], H2G[:, kc, ss]) for kc in range(KC)],
                               [wk, "h2g"], psk(pz))
                            ACT(SQX[r], PS[pz][:, 0:384], AF.Square, [psk(pz)], [("sqx", r)])
                            STT(F1[:, cg * 4 + j, ss], PS[pz][:, 0:384], 0.0, SQX[r], ALU.is_gt, ALU.mult,
                                [psk(pz), ("sqx", r)], ["f1"])
                for og in range(16):
                    wv, wk = ws.get(16 + og)
                    for sub in range(2):
                        ss = slice(sub * 384, (sub + 1) * 384)
                        r = it % 2
                        it += 1
                        pz = 0 + r
                        MM(PS[pz][:, 0:384], [(wv[:, kc, :], F1[:, kc, ss]) for kc in range(64)], [wk, "f1"], psk(pz))
                        ACT(FO[r], PS[pz][:, 0:384], AF.Copy, [psk(pz)], [("fo", r)])
                        S.dma("sp", "fo%d" % r, FT[og * 128:(og + 1) * 128, c0 + sub * 384:c0 + (sub + 1) * 384], FO[r],
                              reads=[("fo", r)], writes=["ft"])
            S.barrier()
            if stop == "P5":
                break

            RM.reset()
            RA.reset()
            OB = RA.get([KC, 512])
            XBr = RA.get([KC, 512])
            SQ = RA.get([KC, 512], BF16)
            RS = RM.get([512])
            TMP = [RM.get([512]), RM.get([512])]
            for tb in range(NTB):
                c = 0 if tb < 4 else 1
                S.dma("sp", "ob", OB, FT[:, tb * 512:(tb + 1) * 512].rearrange("(kc p) t -> p kc t", p=128), reads=["ft"], writes=["ob"])
                S.dma("sp", "xbr", XBr, yT[:, tb * 512:(tb + 1) * 512].rearrange("(kc p) t -> p kc t", p=128),
                      reads=[("xt", tb)], writes=["xbr"])
                norm_stats(OB, "ob")
                for kc in range(KC):
                    tmp, tk = TMP[kc % 2], ("tmp", kc % 2)
                    STT(tmp, OB[:, kc, :], G2P[:, kc, c:c + 1], RS, ALU.mult, ALU.mult, ["ob", ("g2p", c), "rs"], [tk])
                    TT(XBr[:, kc, :], XBr[:, kc, :], tmp, ALU.add, ["xbr", tk], ["xbr"])
                S.dma("sp", "xst", yT[:, tb * 512:(tb + 1) * 512].rearrange("(kc p) t -> p kc t", p=128), XBr,
                      reads=["xbr"], writes=[("xt", tb)])
            S.barrier()
            if stop == "P7":
                break

        S.dma("sp", "nsout", nsT[:, :], NS, reads=["ns"], writes=[])
        S.barrier()
        S.emit(blk)
    return nc


def _fm(v):
    v = np.asarray(v, np.float32)
    lead = v.shape[:-1]
    return np.ascontiguousarray(np.moveaxis(v.reshape(*lead, 16, 128), -1, 0))


def _host_tables():
    d = np.arange(128)
    f = d % 32
    axis = d // 64
    inv = (10000.0 ** (-np.arange(32, dtype=np.float32) / 32)).astype(np.float32)
    t = np.arange(TSMP)
    pos = np.stack([(t // 64).astype(np.float32), (t % 64).astype(np.float32)], 0)
    ang = pos[axis, :] * inv[f][:, None]
    cos = np.cos(ang).astype(np.float32)
    sin = np.sin(ang).astype(np.float32)
    isb = (d % 64) >= 32
    sgn = np.where(isb, 1.0, -1.0).astype(np.float32)
    sinS = sin * sgn[:, None]
    partner = np.where(isb, d - 32, d + 32)
    perm = np.zeros((128, 128), np.float32)
    perm[partner, d] = 1.0
    return cos, sinS, perm


def _prep(inputs):
    I = {k: np.asarray(v) for k, v in inputs.items()}
    cos, sinS, perm = _host_tables()
    pt = np.zeros((128, DEPTH, NPL), np.float32)
    for l in range(DEPTH):
        pt[:, l, P_BMOD:P_BMOD + 96] = I["b_mod"][l].reshape(96, 128).T
        pt[:, l, P_GPM:P_GPM + 16] = I["g_pre_mix"][l].reshape(16, 128).T
        pt[:, l, P_GPO:P_GPO + 16] = I["g_post_mix"][l].reshape(16, 128).T
        pt[:, l, P_GPF:P_GPF + 16] = I["g_pre_ff"][l].reshape(16, 128).T
        pt[:, l, P_GOF:P_GOF + 16] = I["g_post_ff"][l].reshape(16, 128).T
        pt[:, l, P_CW:P_CW + 64] = I["conv_w"][l].reshape(4, 16, 128).transpose(2, 0, 1).reshape(128, 64)
        pt[:, l, P_CB:P_CB + 16] = I["conv_b"][l].reshape(16, 128).T
        pt[:, l, P_BA:P_BA + 32] = I["lru_ba"][l].reshape(2, 16, 128).transpose(2, 0, 1).reshape(128, 32)
        pt[:, l, P_BX:P_BX + 32] = I["lru_bx"][l].reshape(2, 16, 128).transpose(2, 0, 1).reshape(128, 32)
        pt[:, l, P_LAM:P_LAM + 32] = I["lru_lam"][l].reshape(2, 16, 128).transpose(2, 0, 1).reshape(128, 32)
        pt[:, l, P_GQ] = I["g_q"][l]
        pt[:, l, P_GK] = I["g_k"][l]
    shared = {
        "ptab": np.ascontiguousarray(pt.reshape(128, DEPTH * NPL)),
        "cosT": cos, "sinT": sinS, "permd": perm,
        "w_mod": I["w_mod"], "w_in": I["w_in"], "w_attn_out": I["w_attn_out"], "w_lru_out": I["w_lru_out"],
        "w_cm_out": I["w_cm_out"], "w_out": I["w_out"], "w_ff1": I["w_ff1"], "w_ff2": I["w_ff2"],
        "lru_wa": I["lru_wa"], "lru_wx": I["lru_wx"],
        "wsT": np.ascontiguousarray(I["cm_ws"].transpose(0, 3, 1, 2)),
        "cm_bs": np.ascontiguousarray(I["cm_bs"].reshape(DEPTH, 1, D)),
        "cm_g": np.ascontiguousarray(I["cm_g"].reshape(DEPTH, 1, D)),
    }
    in_maps = []
    for i in range(8):
        xs = I["x_sample"][i]
        xp = I["x_prompt"][4 * i:4 * i + 4].reshape(1024, D)
        xT = np.ascontiguousarray(np.concatenate([xs, xp], 0).T)
        cond = np.stack([I["c"][i], I["c_ctx"]], -1)
        condT = np.ascontiguousarray(cond.reshape(16, 128, 2).transpose(1, 0, 2))
        kc_ = np.ascontiguousarray(I["cache_k"][i].transpose(0, 2, 3, 1).reshape(DEPTH, 512, 256))
        vc_ = np.ascontiguousarray(I["cache_v"][i].reshape(DEPTH, 256, 512))
        h0 = np.ascontiguousarray(I["state_lru"][i].reshape(DEPTH, 2, 16, 128).transpose(3, 0, 1, 2).reshape(128, DEPTH * 32))
        m = dict(shared)
        m.update({"xT0": xT, "condT": condT, "kctx": kc_, "vctx": vc_, "h0T": h0})
        in_maps.append(m)
    return in_maps


def _gather(results):
    y_p = np.zeros((32, 256, D), np.float32)
    y_s = np.zeros((8, 2048, D), np.float32)
    nk = np.zeros((32, DEPTH, 256, 4, 128), np.float32)
    nv = np.zeros((32, DEPTH, 256, 4, 128), np.float32)
    ns = np.zeros((32, DEPTH, 2, D), np.float32)
    for i, r in enumerate(results):
        y = np.asarray(r["yT"]).T
        y_s[i] = y[:2048]
        y_p[4 * i:4 * i + 4] = y[2048:].reshape(4, 256, D)
        k = np.asarray(r["knew"]).reshape(DEPTH, 4, 128, 4, 256)
        nk[4 * i:4 * i + 4] = k.transpose(3, 0, 4, 1, 2)
        v = np.asarray(r["vnew"]).reshape(DEPTH, 4, 256, 4, 128)
        nv[4 * i:4 * i + 4] = v.transpose(1, 0, 2, 3, 4)
        s = np.asarray(r["nsT"]).reshape(128, 4, DEPTH, 2, 16)
        ns[4 * i:4 * i + 4] = s.transpose(1, 2, 3, 4, 0).reshape(4, DEPTH, 2, D)
    return y_p, y_s, nk, nv, ns


def kernel(**inputs):
    in_maps = _prep(inputs)
    nc = build()
    res = run_bass_kernel_spmd(nc, in_maps, core_ids=list(range(8)))
    return _gather(res.results)
```

```python
import numpy as np
from contextlib import ExitStack
import concourse.bass as bass
import concourse.mybir as mybir
from concourse.bass_utils import run_bass_kernel_spmd

F32 = mybir.dt.float32
BF16 = mybir.dt.bfloat16
ALU = mybir.AluOpType
AF = mybir.ActivationFunctionType
ENGS = ("pe", "act", "dve", "pool", "sp")

D = 2048
KC = 16
T = 3072
TSMP = 2048
NTB = 6
DEPTH = 4
INW = 17408
DFF = 8192
OQ, OK_, OV, OLX, OLG, OCU, OCV, OG = 0, 2048, 2560, 3072, 5120, 7168, 9216, 11264
EPS = 1e-6
SCALE = 128 ** -0.5
GC1 = 0.044715
GC2 = 1.5957691216057308
P_BMOD, P_GPM, P_GPO, P_GPF, P_GOF, P_CW, P_CB, P_BA, P_BX, P_LAM, P_GQ, P_GK, NPL = (
    0, 96, 112, 128, 144, 160, 224, 240, 272, 304, 336, 337, 338)


class Chan:
    def __init__(self, sem):
        self.sem = sem
        self.cum = 0


class Sched:
    def __init__(self, nc, stack):
        self.nc = nc
        self.q = {e: [] for e in ENGS}
        self.cnt = {e: 0 for e in ENGS}
        self.esem = {e: stack.enter_context(nc.semaphore("prog_" + e)) for e in ENGS}
        self.seen = {e: {} for e in ENGS}
        self.lastw = {}
        self.readers = {}
        self.stack = stack
        self.chans = {}

    def chan(self, name):
        if name not in self.chans:
            self.chans[name] = Chan(self.stack.enter_context(self.nc.semaphore("ch_" + name)))
        return self.chans[name]

    def _need(self, eng, tok):
        sem, val, _ = tok
        k = id(sem)
        if self.seen[eng].get(k, 0) >= val:
            return
        self.seen[eng][k] = val
        self.q[eng].append(("wait", sem, val))

    def _deps(self, eng, reads, writes):
        for k in reads:
            t = self.lastw.get(k)
            if t is not None:
                self._need(eng, t)
        for k in writes:
            t = self.lastw.get(k)
            if t is not None and t[2] != eng:
                self._need(eng, t)
            for t in self.readers.get(k, {}).values():
                if t[2] != eng:
                    self._need(eng, t)

    def _commit(self, tok, reads, writes):
        for k in writes:
            self.lastw[k] = tok
            self.readers[k] = {}
        for k in reads:
            self.readers.setdefault(k, {})[id(tok[0])] = tok

    def op(self, eng, fn, reads=(), writes=()):
        self._deps(eng, reads, writes)
        self.cnt[eng] += 1
        tok = (self.esem[eng], self.cnt[eng], eng)
        self.q[eng].append(("op", fn))
        self._commit(tok, reads, writes)

    def dma(self, eng, chname, out, in_, reads=(), writes=()):
        ch = self.chan(chname)
        self._deps(eng, reads, writes)
        if ch.cum:
            self._need(eng, (ch.sem, ch.cum, None))
        ch.cum += 16
        tok = (ch.sem, ch.cum, None)
        self.q[eng].append(("dma", out, in_, ch.sem))
        self._commit(tok, reads, writes)

    def barrier(self):
        toks = [(self.esem[e], self.cnt[e], None) for e in ENGS if self.cnt[e]]
        toks += [(c.sem, c.cum, None) for c in self.chans.values() if c.cum]
        for e in ENGS:
            for t in toks:
                self._need(e, t)
        self.lastw = {}
        self.readers = {}

    def emit(self, block):
        engobj = {"pe": "tensor", "act": "scalar", "dve": "vector", "pool": "gpsimd", "sp": "sync"}

        def runner(e):
            def f(eng):
                sem = self.esem[e]
                for item in self.q[e]:
                    if item[0] == "wait":
                        eng.wait_ge(item[1], item[2])
                    elif item[0] == "op":
                        item[1](eng).then_inc(sem, 1)
                    else:
                        eng.dma_start(out=item[1], in_=item[2]).then_inc(item[3], 16)
            return f

        for e in ENGS:
            getattr(block, engobj[e])(runner(e))


def build(nlayers=DEPTH, debug=False, stop=None):
    nc = bass.Bass("TRN2", target_bir_lowering=False)
    st = ExitStack()
    with st:
        def din(name, shape, dt=F32):
            return nc.dram_tensor(name, list(shape), dt, kind="ExternalInput").ap()

        def dout(name, shape, dt=F32):
            return nc.dram_tensor(name, list(shape), dt, kind="ExternalOutput").ap()

        def dscr(name, shape, dt):
            kind = "ExternalOutput" if debug else "Internal"
            return nc.dram_tensor(name, list(shape), dt, kind=kind).ap()

        xT0 = din("xT0", [D, T])
        condT = din("condT", [128, KC, 2])
        kctx = din("kctx", [DEPTH, 512, 256])
        vctx = din("vctx", [DEPTH, 256, 512])
        h0T = din("h0T", [128, DEPTH * 2 * 16])
        ptab = din("ptab", [128, DEPTH * NPL])
        cosT = din("cosT", [128, TSMP])
        sinT = din("sinT", [128, TSMP])
        permd = din("permd", [128, 128])
        w_mod = din("w_mod", [nlayers, D, 6 * D])
        w_in = din("w_in", [nlayers, D, INW])
        w_attn_out = din("w_attn_out", [nlayers, D, D])
        w_lru_out = din("w_lru_out", [nlayers, D, D])
        w_cm_out = din("w_cm_out", [nlayers, D, D])
        w_out = din("w_out", [nlayers, D, D])
        w_ff1 = din("w_ff1", [nlayers, D, DFF])
        w_ff2 = din("w_ff2", [nlayers, DFF, D])
        lru_wa = din("lru_wa", [nlayers, 2, 16, 128, 128])
        lru_wx = din("lru_wx", [nlayers, 2, 16, 128, 128])
        wsT = din("wsT", [nlayers, 128, 16, 128])
        cm_bs = din("cm_bs", [nlayers, 1, D])
        cm_g = din("cm_g", [nlayers, 1, D])

        yT = dout("yT", [D, T])
        knew = dout("knew", [DEPTH, 512, 1024])
        vnew = dout("vnew", [DEPTH, 1024, 512])
        nsT = dout("nsT", [128, 4 * DEPTH * 2 * 16])

        QT = dscr("QT", [D, T], BF16)
        KT = dscr("KT", [512, T], BF16)
        VS = dscr("VS", [T, 512], BF16)
        LRUO = dscr("LRUO", [D, T], BF16)
        GCU = dscr("GCU", [D, T], BF16)
        GCV = dscr("GCV", [T, D], BF16)
        CM = dscr("CM", [D, T], BF16)
        GG = dscr("GG", [3 * D, T], BF16)
        ATT = dscr("ATT", [D, T], BF16)
        OT = dscr("OT", [D, T], F32)
        H2T = dscr("H2T", [D, T], BF16)
        FT = dscr("FT", [D, T], F32)

        S = Sched(nc, st)
        ARENA_W = 53000
        arena = st.enter_context(nc.sbuf_tensor("arena", [128, ARENA_W], F32))
        PS = [st.enter_context(nc.psum_tensor("ps%d" % i, [128, 512], F32)) for i in range(8)]

        def carve(off, nbytes, dt, pat=None, **kw):
            assert off % 4 == 0 and nbytes % 4 == 0 and off + nbytes <= ARENA_W * 4, (off, nbytes)
            a = arena[:, off // 4:(off + nbytes) // 4]
            if dt != F32:
                a = a.bitcast(dt)
            if pat is not None:
                a = a.rearrange(pat, **kw)
            return a

        class Region:
            def __init__(self, base, size):
                self.base, self.size, self.cur = base, size, 0

            def reset(self):
                self.cur = 0

            def get(self, shape, dt=F32):
                n = 1
                for s_ in shape:
                    n *= s_
                nb = n * (4 if dt == F32 else 2)
                nb = (nb + 31) // 32 * 32
                assert self.cur + nb <= self.size, (self.cur, nb, self.size)
                off = self.base + self.cur
                self.cur += nb
                if len(shape) == 1:
                    return carve(off, nb, dt)[:, 0:shape[0]]
                if len(shape) == 2:
                    return carve(off, nb, dt)[:, 0:n].rearrange("p (a b) -> p a b", a=shape[0])
                return carve(off, nb, dt)[:, 0:n].rearrange("p (a b c) -> p a b c", a=shape[0], b=shape[1])

        RC = Region(0, 12288)
        RA = Region(12288, 98304)
        RW = Region(12288 + 98304, 32768)
        RM = Region(12288 + 98304 + 32768, ARENA_W * 4 - (12288 + 98304 + 32768))

        ONES11 = RC.get([128], BF16)
        ONES7 = RC.get([128], BF16)
        ONES1 = RC.get([128], BF16)
        PERM = RC.get([128], BF16)
        PTt = RC.get([DEPTH * NPL])
        CONDB = RC.get([KC, 2], BF16)
        MOD = RC.get([96, 2])
        A1 = RC.get([16, 2]); B1 = MOD[:, 0:16, :]
        G1P = RC.get([16, 2])
        A2 = RC.get([16, 2]); B2 = MOD[:, 48:64, :]
        G2P = RC.get([16, 2])
        NBA = RC.get([32]); NBX = RC.get([32]); CC = RC.get([32]); C2 = RC.get([32])
        NS = RC.get([4 * DEPTH * 2 * 16])
        H0 = RC.get([DEPTH * 2 * 16])
        SSQ = RC.get([96])
        RCV = RC.get([24])
        CTMP = RC.get([KC, 2])

        blk = st.enter_context(nc.Block())

        def ACT(out, in_, func, reads, writes, bias=None, scale=None, accum=None):
            kw = {}
            if bias is not None:
                kw["bias"] = bias
            if scale is not None:
                kw["scale"] = scale
            if accum is not None:
                kw["accum_out"] = accum
            S.op("act", lambda e: e.activation(out=out, in_=in_, func=func, **kw), reads=reads, writes=writes)

        def TSC(out, in0, s1, s2, op0, op1, reads, writes, eng="dve"):
            if op1 is None:
                S.op(eng, lambda e: e.tensor_scalar(out=out, in0=in0, scalar1=s1, scalar2=None, op0=op0), reads=reads, writes=writes)
            else:
                S.op(eng, lambda e: e.tensor_scalar(out=out, in0=in0, scalar1=s1, scalar2=s2, op0=op0, op1=op1), reads=reads, writes=writes)

        def TT(out, in0, in1, op, reads, writes, eng="dve"):
            S.op(eng, lambda e: e.tensor_tensor(out=out, in0=in0, in1=in1, op=op), reads=reads, writes=writes)

        def STT(out, in0, scalar, in1, op0, op1, reads, writes):
            S.op("dve", lambda e: e.scalar_tensor_tensor(out=out, in0=in0, scalar=scalar, in1=in1, op0=op0, op1=op1), reads=reads, writes=writes)

        def CP(out, in_, reads, writes, eng="dve"):
            S.op(eng, lambda e: e.tensor_copy(out=out, in_=in_), reads=reads, writes=writes)

        def MM(ps_ap, pairs, reads, pskey):
            def f(e):
                n = len(pairs)
                ins = None
                for i, (l, r) in enumerate(pairs):
                    ins = e.matmul(ps_ap, lhsT=l, rhs=r, start=(i == 0), stop=(i == n - 1))
                return ins
            S.op("pe", f, reads=reads, writes=[pskey])

        def MSET(ap, val, writes, eng="dve"):
            S.op(eng, lambda e: e.memset(ap, val), writes=writes)

        WSLOT = [RW.get([8192], BF16), RW.get([8192], BF16)]
        wctr = [0]

        class WStream:
            def __init__(self, specs, off=0, tag="w"):
                self.specs = specs
                self.issued = 0
                self.info = {}
                self.off, self.tag = off, tag
                self.ctr = wctr if tag == "w" else [0]

            def _issue(self, i):
                slot = self.ctr[0] % 2
                self.ctr[0] += 1
                parts = self.specs[i]
                kcn = parts[0][1]
                ntot = sum(p[2] for p in parts)
                assert self.off + kcn * ntot <= 8192
                view = WSLOT[slot][:, self.off:self.off + kcn * ntot].rearrange("p (kc n) -> p kc n", kc=kcn)
                c0 = 0
                for (src, kcn_, n) in parts:
                    srcv = src.rearrange("(kc p) n -> p kc n", p=128)
                    for k0 in range(0, kcn_, 16):
                        S.dma("pool", "%s%d" % (self.tag, slot), view[:, k0:k0 + 16, c0:c0 + n],
                              srcv[:, k0:k0 + 16, :], writes=[(self.tag, slot)])
                    c0 += n
                self.info[i] = (view, (self.tag, slot))

            def get(self, i):
                while self.issued <= min(i + 1, len(self.specs) - 1):
                    self._issue(self.issued)
                    self.issued += 1
                return self.info[i]

        def psk(i):
            return "ps%d" % i

        S.dma("sp", "c0", PTt, ptab[:, :], writes=["pt"])
        S.dma("sp", "c1", H0, h0T[:, :], writes=["h0"])
        CONDF = RM.get([KC, 2])
        S.dma("sp", "c2", CONDF, condT[:, :, :], writes=["condf"])
        S.dma("pool", "c3", PERM, permd[:, :], writes=["perm"])
        MSET(ONES11, 2.0 ** -11, ["ones11"])
        MSET(ONES7, 2.0 ** -7, ["ones7"])
        MSET(ONES1, 1.0, ["ones1"])
        MSET(NS, 0.0, ["ns"])
        ACT(CTMP, CONDF, AF.Exp, ["condf"], ["ctmp"], scale=-1.0)
        ACT(CTMP, CTMP, AF.Ln, ["ctmp"], ["ctmp"], bias=1.0)
        ACT(CTMP, CTMP, AF.Exp, ["ctmp"], ["ctmp"], scale=-1.0)
        TT(CONDB, CTMP, CONDF, ALU.mult, ["ctmp", "condf"], ["condb"])
        S.barrier()

        for l in range(nlayers):
            pc = l * NPL
            XSRC = xT0 if l == 0 else yT

            def xt_reads(tb, l=l):
                return [] if l == 0 else [("xt", tb)]

            RM.reset()
            ws = WStream([[(w_mod[l][:, cg * 512:(cg + 1) * 512], KC, 512)] for cg in range(24)])
            PSM = PS[0][:, 0:192]
            for cg in range(24):
                wv, wk = ws.get(cg)

                def f(e, wv=wv, cg=cg):
                    ins = None
                    for j in range(4):
                        c0 = (cg * 4 + j) * 2
                        for kc in range(KC):
                            ins = e.matmul(PSM[:, c0:c0 + 2], lhsT=wv[:, kc, j * 128:(j + 1) * 128],
                                           rhs=CONDB[:, kc, :], start=(kc == 0), stop=(kc == KC - 1))
                    return ins
                S.op("pe", f, reads=[wk, "condb"], writes=[psk(0)])
            PSMv = PSM.rearrange("p (a c) -> p a c", c=2)
            for c in range(2):
                TT(MOD[:, :, c], PSMv[:, :, c], PTt[:, pc + P_BMOD:pc + P_BMOD + 96], ALU.add,
                   [psk(0), "pt"], [("mod", c)])
            for c in range(2):
                STT(A1[:, :, c], MOD[:, 16:32, c], 1.0, PTt[:, pc + P_GPM:pc + P_GPM + 16], ALU.add, ALU.mult,
                    [("mod", c), "pt"], [("a1", c)])
                TT(G1P[:, :, c], MOD[:, 32:48, c], PTt[:, pc + P_GPO:pc + P_GPO + 16], ALU.mult,
                   [("mod", c), "pt"], [("g1p", c)])
                STT(A2[:, :, c], MOD[:, 64:80, c], 1.0, PTt[:, pc + P_GPF:pc + P_GPF + 16], ALU.add, ALU.mult,
                    [("mod", c), "pt"], [("a2", c)])
                TT(G2P[:, :, c], MOD[:, 80:96, c], PTt[:, pc + P_GOF:pc + P_GOF + 16], ALU.mult,
                   [("mod", c), "pt"], [("g2p", c)])
            TSC(NBA, PTt[:, pc + P_BA:pc + P_BA + 32], -1.0, None, ALU.mult, None, ["pt"], ["nba"])
            TSC(NBX, PTt[:, pc + P_BX:pc + P_BX + 32], -1.0, None, ALU.mult, None, ["pt"], ["nbx"])
            ACT(CC, PTt[:, pc + P_LAM:pc + P_LAM + 32], AF.Exp, ["pt"], ["cc"], scale=-1.0)
            ACT(CC, CC, AF.Ln, ["cc"], ["cc"], bias=1.0)
            TSC(C2, CC, -16.0, None, ALU.mult, None, ["cc"], ["c2"])
            TSC(CC, CC, -8.0, None, ALU.mult, None, ["cc"], ["cc"])
            S.barrier()
            if stop == "P0":
                break

            RM.reset()
            RA.reset()
            HALL = RA.get([KC, T], BF16)
            XB = [RM.get([KC, 512]), carve(RW.base, 32768, F32, "p (a b) -> p a b", a=KC)]
            SQ = RM.get([KC, 512], BF16)
            RS = RM.get([512])
            TMP = [RM.get([512]), RM.get([512])]

            def norm_stats(src, srckey, sqkey="sq", rskey="rs", psi=1):
                ACT(SQ, src, AF.Square, [srckey], [sqkey])
                MM(PS[psi][:, :], [(ONES11, SQ[:, kc, :]) for kc in range(KC)], ["ones11", sqkey], psk(psi))
                ACT(RS, PS[psi][:, :], AF.Ln, [psk(psi)], [rskey], bias=EPS)
                ACT(RS, RS, AF.Exp, [rskey], [rskey], scale=-0.5)

            for tb in range(NTB):
                c = 0 if tb < 4 else 1
                xb = XB[tb % 2]
                xk = ("xb", tb % 2)
                S.dma("sp", "xb%d" % (tb % 2), xb, XSRC[:, tb * 512:(tb + 1) * 512].rearrange("(kc p) t -> p kc t", p=128),
                      reads=xt_reads(tb), writes=[xk])
                norm_stats(xb, xk)
                for kc in range(KC):
                    tmp = TMP[kc % 2]
                    tk = ("tmp", kc % 2)
                    STT(tmp, xb[:, kc, :], A1[:, kc, c:c + 1], RS, ALU.mult, ALU.mult, [xk, ("a1", c), "rs"], [tk])
                    ACT(HALL[:, kc, tb * 512:(tb + 1) * 512], tmp, AF.Identity, [tk, ("mod", c)], [("hall", tb)],
                        bias=B1[:, kc, c:c + 1])
            S.barrier()
            if stop == "P1":
                break

            RM.reset()
            COS = RM.get([TSMP]); SIN = RM.get([TSMP])
            S.dma("sp", "cos", COS, cosT[:, :], writes=["cos"])
            S.dma("sp", "sin", SIN, sinT[:, :], writes=["sin"])
            SQ1 = [RM.get([512], BF16) for _ in range(2)]
            RS1 = [RM.get([512]) for _ in range(2)]
            QN = [RM.get([512]) for _ in range(2)]
            QNB = [RM.get([512], BF16) for _ in range(2)]
            T1 = [RM.get([512]) for _ in range(2)]
            T2 = [RM.get([512]) for _ in range(2)]
            QO = [RM.get([512], BF16) for _ in range(2)]
            VTb = [RM.get([512], BF16) for _ in range(2)]
            VF = [RM.get([512]) for _ in range(2)]
            specs = [[(w_in[l][:, OQ + g * 512:OQ + (g + 1) * 512], KC, 512)] for g in range(4)]
            specs.append([(w_in[l][:, OK_:OK_ + 512], KC, 512)])
            specs.append([(w_in[l][:, OV:OV + 512], KC, 512)])
            ws = WStream(specs)
            it = 0
            lim = stop.split(":")[1] if (stop and stop.startswith("P2a:")) else None
            for g in range({None: 5, "q1": 1, "q": 4, "qk": 5, "norope": 1, "nodma": 1}[lim]):
                wv, wk = ws.get(g)
                isk = (g == 4)
                gcol = pc + (P_GK if isk else P_GQ)
                for tb in range(1 if lim in ("q1", "norope", "nodma") else NTB):
                    hk_ = ("hall", tb)
                    for j in range(4):
                        r = it % 2
                        it += 1
                        pz, psn, pw = 2 + r, 4 + r, 6 + r
                        MM(PS[pz][:, :], [(wv[:, kc, j * 128:(j + 1) * 128], HALL[:, kc, tb * 512:(tb + 1) * 512]) for kc in range(KC)],
                           [wk, hk_], psk(pz))
                        ACT(SQ1[r], PS[pz][:, :], AF.Square, [psk(pz)], [("sq1", r)])
                        MM(PS[psn][:, :], [(ONES7, SQ1[r])], ["ones7", ("sq1", r)], psk(psn))
                        ACT(RS1[r], PS[psn][:, :], AF.Ln, [psk(psn)], [("rs1", r)], bias=EPS)
                        ACT(RS1[r], RS1[r], AF.Exp, [("rs1", r)], [("rs1", r)], scale=-0.5)
                        STT(QN[r], PS[pz][:, :], PTt[:, gcol:gcol + 1], RS1[r], ALU.mult, ALU.mult,
                            [psk(pz), "pt", ("rs1", r)], [("qn", r)])
                        if tb < 4 and lim != "norope":
                            ACT(QNB[r], QN[r], AF.Copy, [("qn", r)], [("qnb", r)])
                            MM(PS[pw][:, :], [(PERM, QNB[r])], ["perm", ("qnb", r)], psk(pw))
                            TT(T1[r], QN[r], COS[:, tb * 512:(tb + 1) * 512], ALU.mult, [("qn", r), "cos"], [("t1", r)])
                            TT(T2[r], PS[pw][:, :], SIN[:, tb * 512:(tb + 1) * 512], ALU.mult, [psk(pw), "sin"], [("t2", r)])
                            TT(QO[r], T1[r], T2[r], ALU.add, [("t1", r), ("t2", r)], [("qo", r)])
                        else:
                            ACT(QO[r], QN[r], AF.Copy, [("qn", r)], [("qo", r)])
                        if lim == "nodma":
                            pass
                        elif isk:
                            S.dma("sp", "qo%d" % r, KT[j * 128:(j + 1) * 128, tb * 512:(tb + 1) * 512], QO[r],
                                  reads=[("qo", r)], writes=[("kt", tb)])
                            if tb >= 4:
                                S.dma("sp", "kf%d" % r, knew[l][j * 128:(j + 1) * 128, (tb - 4) * 512:(tb - 3) * 512], QN[r],
                                      reads=[("qn", r)], writes=[])
                        else:
                            h = g * 4 + j
                            S.dma("sp", "qo%d" % r, QT[h * 128:(h + 1) * 128, tb * 512:(tb + 1) * 512], QO[r],
                                  reads=[("qo", r)], writes=[("qt", tb)])
            wv, wk = ws.get(5) if lim is None else (None, None)
            for tt in range(24 if lim is None else 0):
                r = tt % 2
                pv = 0 + r
                MM(PS[pv][:, :], [(HALL[:, kc, tt * 128:(tt + 1) * 128], wv[:, kc, :]) for kc in range(KC)],
                   [wk, ("hall", tt // 4)], psk(pv))
                ACT(VTb[r], PS[pv][:, :], AF.Copy, [psk(pv)], [("vt", r)])
                S.dma("sp", "vt%d" % r, VS[tt * 128:(tt + 1) * 128, :], VTb[r], reads=[("vt", r)], writes=[("vs", tt // 4)])
                if tt >= 16:
                    ACT(VF[r], PS[pv][:, :], AF.Copy, [psk(pv)], [("vf", r)])
                    S.dma("sp", "vf%d" % r, vnew[l][(tt - 16) * 128:(tt - 15) * 128, :], VF[r], reads=[("vf", r)], writes=[])
            S.barrier()
            if stop and stop.startswith("P2a"):
                break

            RM.reset()
            XPS = RM.get([2052])
            XPP = RM.get([4, 259])
            GLG = RM.get([T], BF16)
            XC = RM.get([TSMP])
            XCB = RM.get([TSMP], BF16)
            HF = RM.get([TSMP])
            G1 = RM.get([512]); AA = RM.get([512]); G2 = RM.get([512])
            HBB = [RM.get([512]), RM.get([512])]
            SM = RM.get([512])
            GX2 = [RM.get([512]) for _ in range(2)]
            GXS = [RM.get([512]) for _ in range(2)]
            GW = [RM.get([512]) for _ in range(2)]
            LW = [RM.get([4, 128], BF16) for _ in range(2)]
            MSET(XPS, 0.0, ["xp"])
            MSET(XPP, 0.0, ["xp"])
            ws = WStream([[(w_in[l][:, OLX + n * 128:OLX + (n + 1) * 128], KC, 128),
                           (w_in[l][:, OLG + n * 128:OLG + (n + 1) * 128], KC, 128)] for n in range(16)])

            GOZ = [RM.get([512], BF16) for _ in range(2)]
            wsg = WStream([[(w_in[l][:, OG + s_ * 256:OG + (s_ + 1) * 256], KC, 256)] for s_ in range(24)],
                          off=4096, tag="wg")

            def gg_units():
                itg = 0
                for s_ in range(24):
                    wvg, wkg = wsg.get(s_)
                    for tbg in range(NTB):
                        for jg in range(2):
                            rg = itg % 2
                            itg += 1
                            pzg = 6 + rg
                            MM(PS[pzg][:, :], [(wvg[:, kc, jg * 128:(jg + 1) * 128], HALL[:, kc, tbg * 512:(tbg + 1) * 512])
                                               for kc in range(KC)], [wkg, ("hall", tbg)], psk(pzg))
                            CP(GOZ[rg], PS[pzg][:, :], [psk(pzg)], [("goz", rg)])
                            fcg = s_ * 2 + jg
                            S.dma("sp", "goz%d" % rg, GG[fcg * 128:(fcg + 1) * 128, tbg * 512:(tbg + 1) * 512], GOZ[rg],
                                  reads=[("goz", rg)], writes=[("gg", tbg)])
                            yield

            ggen = gg_units()

            def gg_step():
                next(ggen, None)

            def gelu6(ps_ap, pkey, out_ap, outkeys, r):
                x2, xs, w_ = GX2[r], GXS[r], GW[r]
                ACT(x2, ps_ap, AF.Square, [pkey], [("gx2", r)])
                ACT(xs, ps_ap, AF.Copy, [pkey], [("gxs", r)])
                TSC(w_, x2, GC1, 1.0, ALU.mult, ALU.add, [("gx2", r)], [("gw", r)])
                TT(w_, w_, xs, ALU.mult, [("gw", r), ("gxs", r)], [("gw", r)])
                ACT(x2, w_, AF.Exp, [("gw", r)], [("gx2", r)], scale=-GC2)
                ACT(x2, x2, AF.Ln, [("gx2", r)], [("gx2", r)], bias=1.0)
                ACT(x2, x2, AF.Exp, [("gx2", r)], [("gx2", r)], scale=-1.0)
                TT(out_ap, x2, xs, ALU.mult, [("gx2", r), ("gxs", r)], outkeys)

            def lru_gates(n, d, xcb_blk, xc_blk, lw, lwk):
                idx = d * 16 + n
                MM(PS[2][:, :], [(lw[:, d, :], xcb_blk)], [lwk, "xcb"], psk(2))
                MM(PS[3][:, :], [(lw[:, 2 + d, :], xcb_blk)], [lwk, "xcb"], psk(3))
                ACT(G1, PS[2][:, :], AF.Exp, [psk(2), "nba"], ["g1"], scale=-1.0, bias=NBA[:, idx:idx + 1])
                ACT(G2, PS[3][:, :], AF.Exp, [psk(3), "nbx"], ["g2"], scale=-1.0, bias=NBX[:, idx:idx + 1])
                ACT(G1, G1, AF.Ln, ["g1"], ["g1"], bias=1.0)
                ACT(G2, G2, AF.Ln, ["g2"], ["g2"], bias=1.0)
                ACT(G1, G1, AF.Exp, ["g1"], ["g1"], scale=-1.0)
                ACT(G2, G2, AF.Exp, ["g2"], ["g2"], scale=-1.0)
                ACT(AA, G1, AF.Exp, ["g1", "cc"], ["aa"], scale=CC[:, idx:idx + 1])
                TT(G2, G2, xc_blk, ALU.mult, ["g2", "xc"], ["g2"])
                ACT(G1, G1, AF.Exp, ["g1", "c2"], ["g1"], scale=C2[:, idx:idx + 1])
                ACT(G1, G1, AF.Ln, ["g1"], ["g1"], scale=-0.9999999, bias=1.0)
                ACT(G1, G1, AF.Exp, ["g1"], ["g1"], scale=0.5)
                TT(G2, G1, G2, ALU.mult, ["g1", "g2"], ["g2"])

            def conv(dst, src_at, n):
                cw = pc + P_CW
                TSC(dst, src_at(0), PTt[:, cw + n:cw + n + 1], PTt[:, pc + P_CB + n:pc + P_CB + n + 1],
                    ALU.mult, ALU.add, ["xp", "pt"], ["xc"])
                for j in range(1, 4):
                    STT(dst, src_at(j), PTt[:, cw + j * 16 + n:cw + j * 16 + n + 1], dst, ALU.mult, ALU.add,
                        ["xp", "pt", "xc"], ["xc"])

            git = 0
            for n in range(16):
                wv, wk = ws.get(n)
                lw = LW[n % 2]
                lwk = ("lw", n % 2)
                S.dma("pool", "lwa%d" % (n % 2), lw[:, 0:2, :], lru_wa[l][:, n].rearrange("d c e -> c d e"), writes=[lwk])
                S.dma("pool", "lwx%d" % (n % 2), lw[:, 2:4, :], lru_wx[l][:, n].rearrange("d c e -> c d e"), writes=[lwk])
                for tb in range(NTB):
                    r = git % 2
                    git += 1
                    px, pg = 0 + r, 4 + r
                    MM(PS[px][:, :], [(wv[:, kc, 0:128], HALL[:, kc, tb * 512:(tb + 1) * 512]) for kc in range(KC)],
                       [wk, ("hall", tb)], psk(px))
                    MM(PS[pg][:, :], [(wv[:, kc, 128:256], HALL[:, kc, tb * 512:(tb + 1) * 512]) for kc in range(KC)],
                       [wk, ("hall", tb)], psk(pg))
                    if tb < 4:
                        ACT(XPS[:, 2 + tb * 512:2 + (tb + 1) * 512], PS[px][:, :], AF.Copy, [psk(px)], ["xp"])
                    else:
                        ACT(XPP[:, (tb - 4) * 2:(tb - 3) * 2, 2:258], PS[px][:, :].rearrange("p (s t) -> p s t", s=2),
                            AF.Copy, [psk(px)], ["xp"])
                    gelu6(PS[pg][:, :], psk(pg), GLG[:, tb * 512:(tb + 1) * 512], ["glg"], r)
                    gg_step()
                conv(XC, lambda j: XPS[:, j:j + TSMP], n)
                ACT(XCB, XC, AF.Copy, ["xc"], ["xcb"])
                for tb in range(4):
                    sl = slice(tb * 512, (tb + 1) * 512)
                    lru_gates(n, 0, XCB[:, sl], XC[:, sl], lw, lwk)
                    gg_step()
                    init = H0[:, (l * 2 + 0) * 16 + n:(l * 2 + 0) * 16 + n + 1] if tb == 0 else HF[:, tb * 512 - 1:tb * 512]
                    S.op("dve", lambda e, sl=sl, init=init: e.tensor_tensor_scan(
                        out=HF[:, sl], data0=AA, data1=G2, initial=init, op0=ALU.mult, op1=ALU.add),
                        reads=["aa", "g2", "hf", "h0"], writes=["hf"])
                for tb in range(3, -1, -1):
                    sl = slice(tb * 512, (tb + 1) * 512)
                    hb = HBB[tb % 2]
                    lru_gates(n, 1, XCB[:, sl], XC[:, sl], lw, lwk)
                    gg_step()
                    init = H0[:, (l * 2 + 1) * 16 + n:(l * 2 + 1) * 16 + n + 1] if tb == 3 else HBB[(tb + 1) % 2][:, 0:1]
                    S.op("dve", lambda e, hb=hb, init=init: e.tensor_tensor_scan(
                        out=hb[:, ::-1], data0=AA[:, ::-1], data1=G2[:, ::-1], initial=init, op0=ALU.mult, op1=ALU.add),
                        reads=["aa", "g2", ("hbb", (tb + 1) % 2), "h0"], writes=[("hbb", tb % 2)])
                    TT(SM, hb, HF[:, sl], ALU.add, [("hbb", tb % 2), "hf"], ["sm"])
                    TT(GLG[:, sl], SM, GLG[:, sl], ALU.mult, ["sm", "glg"], ["glg"])
                S.dma("sp", "lruo", LRUO[n * 128:(n + 1) * 128, 0:TSMP], GLG[:, 0:TSMP], reads=["glg"], writes=[("lruo", n)])
                XCp = XC[:, 0:1024].rearrange("p (s t) -> p s t", s=4)
                conv(XCp, lambda j: XPP[:, :, j:j + 256], n)
                ACT(XCB[:, 0:1024], XC[:, 0:1024], AF.Copy, ["xc"], ["xcb"])
                for tb in range(2):
                    sl = slice(tb * 512, (tb + 1) * 512)
                    lru_gates(n, 0, XCB[:, sl], XC[:, sl], lw, lwk)
                    gg_step()
                    for s2 in range(2):
                        ss = slice(tb * 512 + s2 * 256, tb * 512 + (s2 + 1) * 256)
                        sb = slice(s2 * 256, (s2 + 1) * 256)
                        S.op("dve", lambda e, ss=ss, sb=sb: e.tensor_tensor_scan(
                            out=HF[:, ss], data0=AA[:, sb], data1=G2[:, sb], initial=0.0, op0=ALU.mult, op1=ALU.add),
                            reads=["aa", "g2"], writes=["hf"])
                for tb in range(2):
                    sl = slice(tb * 512, (tb + 1) * 512)
                    hb = HBB[tb % 2]
                    lru_gates(n, 1, XCB[:, sl], XC[:, sl], lw, lwk)
                    gg_step()
                    for s2 in range(2):
                        sb = slice(s2 * 256, (s2 + 1) * 256)
                        S.op("dve", lambda e, hb=hb, sb=sb: e.tensor_tensor_scan(
                            out=hb[:, sb][:, ::-1], data0=AA[:, sb][:, ::-1], data1=G2[:, sb][:, ::-1], initial=0.0,
                            op0=ALU.mult, op1=ALU.add),
                            reads=["aa", "g2"], writes=[("hbb", tb % 2)])
                    for s2 in range(2):
                        seq = tb * 2 + s2
                        o = ((seq * DEPTH + l) * 2) * 16 + n
                        CP(NS[:, o:o + 1], HF[:, seq * 256 + 255:seq * 256 + 256], ["hf"], ["ns"])
                        CP(NS[:, o + 16:o + 17], hb[:, s2 * 256:s2 * 256 + 1], [("hbb", tb % 2)], ["ns"])
                    TT(SM, hb, HF[:, sl], ALU.add, [("hbb", tb % 2), "hf"], ["sm"])
                    gs = slice(TSMP + tb * 512, TSMP + (tb + 1) * 512)
                    TT(GLG[:, gs], SM, GLG[:, gs], ALU.mult, ["sm", "glg"], ["glg"])
                S.dma("sp", "lruo2", LRUO[n * 128:(n + 1) * 128, TSMP:T], GLG[:, TSMP:T], reads=["glg"], writes=[("lruo", n)])
            for _ in ggen:
                pass
            S.barrier()
            if stop == "P2b":
                break

            RM.reset()
            GX2 = [RM.get([512]) for _ in range(2)]
            GXS = [RM.get([512]) for _ in range(2)]
            GW = [RM.get([512]) for _ in range(2)]
            GO = [RM.get([512], BF16) for _ in range(2)]
            JUNK = RM.get([512], BF16)
            MSET(SSQ, 0.0, ["ssq"])

            def gelut(ps_ap, pkey, out_ap, outkeys, r):
                x2, xh, w_ = GX2[r], GXS[r], GW[r]
                ACT(x2, ps_ap, AF.Square, [pkey], [("gx2", r)])
                ACT(xh, ps_ap, AF.Identity, [pkey], [("gxs", r)], scale=0.5)
                TSC(w_, x2, GC1, 1.0, ALU.mult, ALU.add, [("gx2", r)], [("gw", r)])
                TT(w_, w_, xh, ALU.mult, [("gw", r), ("gxs", r)], [("gw", r)])
                ACT(x2, w_, AF.Tanh, [("gw", r)], [("gx2", r)], scale=GC2)
                STT(out_ap, x2, 1.0, xh, ALU.add, ALU.mult, [("gx2", r), ("gxs", r)], outkeys)

            specs = [[(w_in[l][:, OCU + g * 512:OCU + (g + 1) * 512], KC, 512)] for g in range(4)]
            specs += [[(w_in[l][:, OCV + g * 512:OCV + (g + 1) * 512], KC, 512)] for g in range(4)]
            ws = WStream(specs)
            it = 0
            for g in range(4):
                wv, wk = ws.get(g)
                for tb in range(NTB):
                    for j in range(4):
                        r = it % 2
                        it += 1
                        pz = 0 + r
                        MM(PS[pz][:, :], [(wv[:, kc, j * 128:(j + 1) * 128], HALL[:, kc, tb * 512:(tb + 1) * 512]) for kc in range(KC)],
                           [wk, ("hall", tb)], psk(pz))
                        gelut(PS[pz][:, :], psk(pz), GO[r], [("go", r)], r)
                        fc = g * 4 + j
                        S.dma("sp", "go%d" % r, GCU[fc * 128:(fc + 1) * 128, tb * 512:(tb + 1) * 512], GO[r],
                              reads=[("go", r)], writes=[("gcu", tb)])
            for g in range(4):
                wv, wk = ws.get(4 + g)
                for tt in range(24):
                    r = it % 2
                    it += 1
                    pz = 0 + r
                    MM(PS[pz][:, :], [(HALL[:, kc, tt * 128:(tt + 1) * 128], wv[:, kc, :]) for kc in range(KC)],
                       [wk, ("hall", tt // 4)], psk(pz))
                    gelut(PS[pz][:, :], psk(pz), GO[r], [("go", r)], r)
                    ACT(JUNK, GO[r], AF.Square, [("go", r)], ["junk", "ssq"], accum=SSQ[:, tt * 4 + g:tt * 4 + g + 1])
                    S.dma("sp", "go%d" % r, GCV[tt * 128:(tt + 1) * 128, g * 512:(g + 1) * 512], GO[r],
                          reads=[("go", r)], writes=[("gcv", tt)])
            S.barrier()
            if stop == "P2c":
                break

            RM.reset()
            RA.reset()
            GCUB = [RA.get([KC, 512], BF16) for _ in range(2)]
            CMB = [RA.get([KC, 512], BF16) for _ in range(2)]
            BSB = RA.get([D])
            CMG = RA.get([D])
            WST = RM.get([16, 128], BF16)
            GCVT = [RM.get([D], BF16) for _ in range(2)]
            VCM = [RM.get([D], BF16) for _ in range(2)]
            TMPX = [RM.get([512]) for _ in range(2)]
            SS = RM.get([24])
            S.op("dve", lambda e: e.tensor_reduce(out=SS, in_=SSQ.rearrange("p (t g) -> p t g", g=4),
                                                  axis=mybir.AxisListType.X, op=ALU.add), reads=["ssq"], writes=["ss"])
            ACT(RCV, SS, AF.Ln, ["ss"], ["rcv"], scale=1.0 / D, bias=EPS)
            ACT(RCV, RCV, AF.Exp, ["rcv"], ["rcv"], scale=-0.5)
            S.dma("pool", "wst", WST, wsT[l], writes=["wst"])
            S.dma("sp", "bsb", BSB, cm_bs[l][0:1, :].partition_broadcast(128), writes=["bsb"])
            S.dma("sp", "cmg", CMG, cm_g[l][0:1, :].partition_broadcast(128), writes=["cmg"])
            it = 0
            for tb in range(NTB):
                rb = tb % 2
                S.dma("sp", "gcub%d" % rb, GCUB[rb], GCU[:, tb * 512:(tb + 1) * 512].rearrange("(kc p) t -> p kc t", p=128),
                      reads=[("gcu", tb)], writes=[("gcub", rb)])
                for t4 in range(4):
                    tt = tb * 4 + t4
                    rt = tt % 2
                    S.dma("sp", "gcvt%d" % rt, GCVT[rt], GCV[tt * 128:(tt + 1) * 128, :], reads=[("gcv", tt)], writes=[("gcvt", rt)])
                    STT(VCM[rt], GCVT[rt], RCV[:, tt:tt + 1], CMG, ALU.mult, ALU.mult, [("gcvt", rt), "rcv", "cmg"], [("vcm", rt)])
                    for g4 in range(4):
                        r = it % 2
                        it += 1
                        px = 0 + r

                        def f(e, px=px, rt=rt, g4=g4):
                            ins = None
                            for gi in range(4):
                                g = g4 * 4 + gi
                                ins = e.matmul(PS[px][:, gi * 128:(gi + 1) * 128], lhsT=VCM[rt][:, g * 128:(g + 1) * 128],
                                               rhs=WST[:, g, :], start=True, stop=True)
                            return ins
                        S.op("pe", f, reads=[("vcm", rt), "wst"], writes=[psk(px)])
                        TT(TMPX[r], PS[px][:, :], BSB[:, g4 * 512:(g4 + 1) * 512], ALU.add, [psk(px), "bsb"], [("tmpx", r)])
                        TT(CMB[rb][:, g4 * 4:(g4 + 1) * 4, t4 * 128:(t4 + 1) * 128],
                           TMPX[r].rearrange("p (g q) -> p g q", g=4),
                           GCUB[rb][:, g4 * 4:(g4 + 1) * 4, t4 * 128:(t4 + 1) * 128], ALU.mult,
                           [("tmpx", r), ("gcub", rb)], [("cmb", rb)])
                S.dma("sp", "cmb%d" % rb, CM[:, tb * 512:(tb + 1) * 512].rearrange("(kc p) t -> p kc t", p=128), CMB[rb],
                      reads=[("cmb", rb)], writes=[("cm", tb)])
            S.barrier()
            if stop == "P2d":
                break

            RM.reset()
            RA.reset()
            KTS = RA.get([4, 2304], BF16)
            VSS = RA.get([18, 512], BF16)
            KTP = RA.get([4, 1024], BF16)
            VSP = RA.get([8, 512], BF16)
            QB = [RA.get([16, 512], BF16) for _ in range(2)]
            ATTB = [RM.get([16, 512], BF16) for _ in range(2)]
            PTL = [RM.get([512], BF16) for _ in range(4)]
            RD = [RM.get([512]) for _ in range(2)]
            S.dma("sp", "kts", KTS[:, :, 0:TSMP], KT[:, 0:TSMP].rearrange("(h d) t -> d h t", d=128),
                  reads=[("kt", tb) for tb in range(4)], writes=["kts"])
            S.dma("pool", "ktsc", KTS[:, :, TSMP:2304], kctx[l].rearrange("(h d) t -> d h t", d=128), writes=["kts"])
            S.dma("sp", "vss", VSS[:, 0:16, :], VS[0:TSMP, :].rearrange("(tt p) c -> p tt c", p=128),
                  reads=[("vs", tb) for tb in range(4)], writes=["vss"])
            S.dma("pool", "vssc", VSS[:, 16:18, :], vctx[l].rearrange("(tt p) c -> p tt c", p=128), writes=["vss"])
            S.dma("sp", "ktp", KTP, KT[:, TSMP:T].rearrange("(h d) t -> d h t", d=128),
                  reads=[("kt", 4), ("kt", 5)], writes=["ktp"])
            S.dma("sp", "vsp", VSP, VS[TSMP:T, :].rearrange("(tt p) c -> p tt c", p=128),
                  reads=[("vs", 4), ("vs", 5)], writes=["vsp"])
            ai = 0
            pi_ = 0
            for tb in range(NTB):
                rb = tb % 2
                S.dma("sp", "qb%d" % rb, QB[rb], QT[:, tb * 512:(tb + 1) * 512].rearrange("(h d) t -> d h t", d=128),
                      reads=[("qt", tb)], writes=[("qb", rb)])
                for hk in range(4):
                    for qs in range(4):
                        ra = ai % 2
                        ai += 1
                        po, pd = 4 + ra, 6 + ra
                        rhs = QB[rb][:, hk * 4:(hk + 1) * 4, qs * 128:(qs + 1) * 128]
                        if tb < 4:
                            tiles = [(KTS[:, hk, kt * 128:(kt + 1) * 128], VSS[:, kt, hk * 128:(hk + 1) * 128]) for kt in range(18)]
                            kv = ["kts", "vss"]
                        else:
                            seq = (tb - 4) * 2 + qs // 2
                            tiles = [(KTP[:, hk, seq * 256 + kt * 128:seq * 256 + (kt + 1) * 128],
                                      VSP[:, seq * 2 + kt, hk * 128:(hk + 1) * 128]) for kt in range(2)]
                            kv = ["ktp", "vsp"]
                        nk = len(tiles)
                        slots = []

                        def qk(i):
                            ps_i = pi_ % 4
                            MM(PS[ps_i][:, :], [(tiles[i][0], rhs)], [kv[0], ("qb", rb)], psk(ps_i))
                            return ps_i
                        ps_i = qk(0)
                        pi_ += 1
                        for kt in range(nk):
                            cur = ps_i
                            if kt + 1 < nk:
                                ps_i = qk(kt + 1)
                                pi_ += 1
                            pt = PTL[cur]
                            ACT(pt, PS[cur][:, :], AF.Exp, [psk(cur)], [("ptl", cur)], scale=SCALE)
                            st_, sp_ = (kt == 0), (kt == nk - 1)
                            S.op("pe", lambda e, po=po, pd=pd, v=tiles[kt][1], pt=pt, st_=st_, sp_=sp_: (
                                e.matmul(PS[po][:, :], lhsT=v, rhs=pt, start=st_, stop=sp_),
                                e.matmul(PS[pd][:, :], lhsT=ONES1, rhs=pt, start=st_, stop=sp_))[1],
                                reads=[kv[1], ("ptl", cur), "ones1"], writes=[psk(po), psk(pd)])
                        S.op("dve", lambda e, ra=ra, pd=pd: e.reciprocal(out=RD[ra], in_=PS[pd][:, :]), reads=[psk(pd)], writes=[("rd", ra)])
                        TT(ATTB[rb][:, hk * 4:(hk + 1) * 4, qs * 128:(qs + 1) * 128],
                           PS[po][:, :].rearrange("p (g q) -> p g q", g=4), RD[ra].rearrange("p (g q) -> p g q", g=4),
                           ALU.mult, [psk(po), ("rd", ra)], [("attb", rb)])
                S.dma("sp", "attb%d" % rb, ATT[:, tb * 512:(tb + 1) * 512].rearrange("(h d) t -> d h t", d=128), ATTB[rb],
                      reads=[("attb", rb)], writes=[("att", tb)])
            S.barrier()
            if stop == "P3":
                break

            RM.reset()
            RA.reset()
            A3 = [RA.get([KC, 768], BF16) for _ in range(3)]
            MRG = RA.get([KC, 768], BF16)
            ACC = RM.get([4, 768])
            GT = [RM.get([4, 768], BF16) for _ in range(2)]
            TM = [RM.get([384]) for _ in range(2)]
            OTL = [RM.get([384]) for _ in range(2)]
            srcs = [ATT, LRUO, CM]
            wouts = [w_attn_out, w_lru_out, w_cm_out]
            it = 0
            gi_ = 0
            for tg in range(4):
                c0 = tg * 768
                allk = [("att", tb) for tb in range(NTB)] + [("lruo", n) for n in range(16)] + [("cm", tb) for tb in range(NTB)]
                for b in range(3):
                    S.dma("sp", "a3%d" % b, A3[b], srcs[b][:, c0:c0 + 768].rearrange("(kc p) t -> p kc t", p=128),
                          reads=allk, writes=[("a3", b)])
                specs = []
                for cg in range(4):
                    for b in range(3):
                        specs.append([(wouts[b][l][:, cg * 512:(cg + 1) * 512], KC, 512)])
                for cg in range(4):
                    specs.append([(w_out[l][:, cg * 512:(cg + 1) * 512], KC, 512)])
                ws = WStream(specs)
                for cg in range(4):
                    for b in range(3):
                        wv, wk = ws.get(cg * 3 + b)
                        gr = gi_ % 2
                        gi_ += 1
                        r0 = b * D + cg * 512
                        S.dma("sp", "gt%d" % gr, GT[gr], GG[r0:r0 + 512, c0:c0 + 768].rearrange("(j p) t -> p j t", p=128),
                              reads=[("gg", tb) for tb in range(NTB)], writes=[("gt", gr)])
                        ACT(GT[gr], GT[gr], AF.Tanh, [("gt", gr)], [("gt", gr)], scale=0.5)
                        for sub in range(2):
                            ss = slice(sub * 384, (sub + 1) * 384)
                            for j in range(4):
                                r = it % 2
                                it += 1
                                pz = 0 + r
                                MM(PS[pz][:, 0:384], [(wv[:, kc, j * 128:(j + 1) * 128], A3[b][:, kc, ss]) for kc in range(KC)],
                                   [wk, ("a3", b)], psk(pz))
                                if b == 0:
                                    STT(ACC[:, j, ss], GT[gr][:, j, ss], 1.0, PS[pz][:, 0:384], ALU.add, ALU.mult,
                                        [("gt", gr), psk(pz)], [("acc", j, sub)])
                                else:
                                    STT(TM[r], GT[gr][:, j, ss], 1.0, PS[pz][:, 0:384], ALU.add, ALU.mult,
                                        [("gt", gr), psk(pz)], [("tm", r)])
                                    TT(ACC[:, j, ss], ACC[:, j, ss], TM[r], ALU.add, [("acc", j, sub), ("tm", r)], [("acc", j, sub)])
                                if b == 2:
                                    ACT(MRG[:, cg * 4 + j, ss], ACC[:, j, ss], AF.Copy, [("acc", j, sub)], ["mrg"])
                for cg in range(4):
                    wv, wk = ws.get(12 + cg)
                    for sub in range(2):
                        ss = slice(sub * 384, (sub + 1) * 384)
                        for j in range(4):
                            r = it % 2
                            it += 1
                            pz = 0 + r
                            MM(PS[pz][:, 0:384], [(wv[:, kc, j * 128:(j + 1) * 128], MRG[:, kc, ss]) for kc in range(KC)],
                               [wk, "mrg"], psk(pz))
                            ACT(OTL[r], PS[pz][:, 0:384], AF.Identity, [psk(pz)], [("otl", r)], scale=0.5)
                            fc = cg * 4 + j
                            S.dma("sp", "otl%d" % r, OT[fc * 128:(fc + 1) * 128, c0 + sub * 384:c0 + (sub + 1) * 384], OTL[r],
                                  reads=[("otl", r)], writes=["ot"])
            S.barrier()
            if stop == "P4a":
                break

            RM.reset()
            RA.reset()
            OB = RA.get([KC, 512])
            XBr = RA.get([KC, 512])
            SQ = RA.get([KC, 512], BF16)
            RS = RM.get([512])
            TMP = [RM.get([512]), RM.get([512])]
            H2O = RM.get([KC, 512], BF16)
            for tb in range(NTB):
                c = 0 if tb < 4 else 1
                S.dma("sp", "ob", OB, OT[:, tb * 512:(tb + 1) * 512].rearrange("(kc p) t -> p kc t", p=128), reads=["ot"], writes=["ob"])
                S.dma("sp", "xbr", XBr, XSRC[:, tb * 512:(tb + 1) * 512].rearrange("(kc p) t -> p kc t", p=128),
                      reads=xt_reads(tb), writes=["xbr"])
                norm_stats(OB, "ob")
                for kc in range(KC):
                    tmp, tk = TMP[kc % 2], ("tmp", kc % 2)
                    STT(tmp, OB[:, kc, :], G1P[:, kc, c:c + 1], RS, ALU.mult, ALU.mult, ["ob", ("g1p", c), "rs"], [tk])
                    TT(XBr[:, kc, :], XBr[:, kc, :], tmp, ALU.add, ["xbr", tk], ["xbr"])
                S.dma("sp", "xst", yT[:, tb * 512:(tb + 1) * 512].rearrange("(kc p) t -> p kc t", p=128), XBr,
                      reads=["xbr"], writes=[("xt", tb)])
                norm_stats(XBr, "xbr")
                for kc in range(KC):
                    tmp, tk = TMP[kc % 2], ("tmp", kc % 2)
                    STT(tmp, XBr[:, kc, :], A2[:, kc, c:c + 1], RS, ALU.mult, ALU.mult, ["xbr", ("a2", c), "rs"], [tk])
                    ACT(H2O[:, kc, :], tmp, AF.Identity, [tk, ("mod", c)], ["h2o"], bias=B2[:, kc, c:c + 1])
                S.dma("sp", "h2o", H2T[:, tb * 512:(tb + 1) * 512].rearrange("(kc p) t -> p kc t", p=128), H2O,
                      reads=["h2o"], writes=["h2t"])
            S.barrier()
            if stop == "P4b":
                break

            RM.reset()
            RA.reset()
            F1 = RA.get([64, 768], BF16)
            H2G = RM.get([KC, 768], BF16)
            SQX = [RM.get([384]) for _ in range(2)]
            FO = [RM.get([384]) for _ in range(2)]
            it = 0
            for tg in range(4):
                c0 = tg * 768
                S.dma("sp", "h2g", H2G, H2T[:, c0:c0 + 768].rearrange("(kc p) t -> p kc t", p=128), reads=["h2t"], writes=["h2g"])
                specs = [[(w_ff1[l][:, cg * 512:(cg + 1) * 512], KC, 512)] for cg in range(16)]
                specs += [[(w_ff2[l][:, og * 128:(og + 1) * 128], 64, 128)] for og in range(16)]
                ws = WStream(specs)
                for cg in range(16):
                    wv, wk = ws.get(cg)
                    for sub in range(2):
                        ss = slice(sub * 384, (sub + 1) * 384)
                        for j in range(4):
                            r = it % 2
                            it += 1
                            pz = 0 + r
                            MM(PS[pz][:, 0:384], [(wv[:, kc, j * 128:(j + 1) * 128], H2G[:, kc, ss]) for kc in range(KC)],
                               [wk, "h2g"], psk(pz))
                            ACT(SQX[r], PS[pz][:, 0:384], AF.Square, [psk(pz)], [("sqx", r)])
                            STT(F1[:, cg * 4 + j, ss], PS[pz][:, 0:384], 0.0, SQX[r], ALU.is_gt, ALU.mult,
                                [psk(pz), ("sqx", r)], ["f1"])
                for og in range(16):
                    wv, wk = ws.get(16 + og)
                    for sub in range(2):
                        ss = slice(sub * 384, (sub + 1) * 384)
                        r = it % 2
                        it += 1
                        pz = 0 + r
                        MM(PS[pz][:, 0:384], [(wv[:, kc, :], F1[:, kc, ss]) for kc in range(64)], [wk, "f1"], psk(pz))
                        ACT(FO[r], PS[pz][:, 0:384], AF.Copy, [psk(pz)], [("fo", r)])
                        S.dma("sp", "fo%d" % r, FT[og * 128:(og + 1) * 128, c0 + sub * 384:c0 + (sub + 1) * 384], FO[r],
                              reads=[("fo", r)], writes=["ft"])
            S.barrier()
            if stop == "P5":
                break

            RM.reset()
            RA.reset()
            OB = RA.get([KC, 512])
            XBr = RA.get([KC, 512])
            SQ = RA.get([KC, 512], BF16)
            RS = RM.get([512])
            TMP = [RM.get([512]), RM.get([512])]
            for tb in range(NTB):
                c = 0 if tb < 4 else 1
                S.dma("sp", "ob", OB, FT[:, tb * 512:(tb + 1) * 512].rearrange("(kc p) t -> p kc t", p=128), reads=["ft"], writes=["ob"])
                S.dma("sp", "xbr", XBr, yT[:, tb * 512:(tb + 1) * 512].rearrange("(kc p) t -> p kc t", p=128),
                      reads=[("xt", tb)], writes=["xbr"])
                norm_stats(OB, "ob")
                for kc in range(KC):
                    tmp, tk = TMP[kc % 2], ("tmp", kc % 2)
                    STT(tmp, OB[:, kc, :], G2P[:, kc, c:c + 1], RS, ALU.mult, ALU.mult, ["ob", ("g2p", c), "rs"], [tk])
                    TT(XBr[:, kc, :], XBr[:, kc, :], tmp, ALU.add, ["xbr", tk], ["xbr"])
                S.dma("sp", "xst", yT[:, tb * 512:(tb + 1) * 512].rearrange("(kc p) t -> p kc t", p=128), XBr,
                      reads=["xbr"], writes=[("xt", tb)])
            S.barrier()
            if stop == "P7":
                break

        S.dma("sp", "nsout", nsT[:, :], NS, reads=["ns"], writes=[])
        S.barrier()
        S.emit(blk)
    return nc


def _fm(v):
    v = np.asarray(v, np.float32)
    lead = v.shape[:-1]
    return np.ascontiguousarray(np.moveaxis(v.reshape(*lead, 16, 128), -1, 0))


def _host_tables():
    d = np.arange(128)
    f = d % 32
    axis = d // 64
    inv = (10000.0 ** (-np.arange(32, dtype=np.float32) / 32)).astype(np.float32)
    t = np.arange(TSMP)
    pos = np.stack([(t // 64).astype(np.float32), (t % 64).astype(np.float32)], 0)
    ang = pos[axis, :] * inv[f][:, None]
    cos = np.cos(ang).astype(np.float32)
    sin = np.sin(ang).astype(np.float32)
    isb = (d % 64) >= 32
    sgn = np.where(isb, 1.0, -1.0).astype(np.float32)
    sinS = sin * sgn[:, None]
    partner = np.where(isb, d - 32, d + 32)
    perm = np.zeros((128, 128), np.float32)
    perm[partner, d] = 1.0
    return cos, sinS, perm


def _prep(inputs):
    I = {k: np.asarray(v) for k, v in inputs.items()}
    cos, sinS, perm = _host_tables()
    pt = np.zeros((128, DEPTH, NPL), np.float32)
    for l in range(DEPTH):
        pt[:, l, P_BMOD:P_BMOD + 96] = I["b_mod"][l].reshape(96, 128).T
        pt[:, l, P_GPM:P_GPM + 16] = I["g_pre_mix"][l].reshape(16, 128).T
        pt[:, l, P_GPO:P_GPO + 16] = I["g_post_mix"][l].reshape(16, 128).T
        pt[:, l, P_GPF:P_GPF + 16] = I["g_pre_ff"][l].reshape(16, 128).T
        pt[:, l, P_GOF:P_GOF + 16] = I["g_post_ff"][l].reshape(16, 128).T
        pt[:, l, P_CW:P_CW + 64] = I["conv_w"][l].reshape(4, 16, 128).transpose(2, 0, 1).reshape(128, 64)
        pt[:, l, P_CB:P_CB + 16] = I["conv_b"][l].reshape(16, 128).T
        pt[:, l, P_BA:P_BA + 32] = I["lru_ba"][l].reshape(2, 16, 128).transpose(2, 0, 1).reshape(128, 32)
        pt[:, l, P_BX:P_BX + 32] = I["lru_bx"][l].reshape(2, 16, 128).transpose(2, 0, 1).reshape(128, 32)
        pt[:, l, P_LAM:P_LAM + 32] = I["lru_lam"][l].reshape(2, 16, 128).transpose(2, 0, 1).reshape(128, 32)
        pt[:, l, P_GQ] = I["g_q"][l]
        pt[:, l, P_GK] = I["g_k"][l]
    shared = {
        "ptab": np.ascontiguousarray(pt.reshape(128, DEPTH * NPL)),
        "cosT": cos, "sinT": sinS, "permd": perm,
        "w_mod": I["w_mod"], "w_in": I["w_in"], "w_attn_out": I["w_attn_out"], "w_lru_out": I["w_lru_out"],
        "w_cm_out": I["w_cm_out"], "w_out": I["w_out"], "w_ff1": I["w_ff1"], "w_ff2": I["w_ff2"],
        "lru_wa": I["lru_wa"], "lru_wx": I["lru_wx"],
        "wsT": np.ascontiguousarray(I["cm_ws"].transpose(0, 3, 1, 2)),
        "cm_bs": np.ascontiguousarray(I["cm_bs"].reshape(DEPTH, 1, D)),
        "cm_g": np.ascontiguousarray(I["cm_g"].reshape(DEPTH, 1, D)),
    }
    in_maps = []
    for i in range(8):
        xs = I["x_sample"][i]
        xp = I["x_prompt"][4 * i:4 * i + 4].reshape(1024, D)
        xT = np.ascontiguousarray(np.concatenate([xs, xp], 0).T)
        cond = np.stack([I["c"][i], I["c_ctx"]], -1)
        condT = np.ascontiguousarray(cond.reshape(16, 128, 2).transpose(1, 0, 2))
        kc_ = np.ascontiguousarray(I["cache_k"][i].transpose(0, 2, 3, 1).reshape(DEPTH, 512, 256))
        vc_ = np.ascontiguousarray(I["cache_v"][i].reshape(DEPTH, 256, 512))
        h0 = np.ascontiguousarray(I["state_lru"][i].reshape(DEPTH, 2, 16, 128).transpose(3, 0, 1, 2).reshape(128, DEPTH * 32))
        m = dict(shared)
        m.update({"xT0": xT, "condT": condT, "kctx": kc_, "vctx": vc_, "h0T": h0})
        in_maps.append(m)
    return in_maps


def _gather(results):
    y_p = np.zeros((32, 256, D), np.float32)
    y_s = np.zeros((8, 2048, D), np.float32)
    nk = np.zeros((32, DEPTH, 256, 4, 128), np.float32)
    nv = np.zeros((32, DEPTH, 256, 4, 128), np.float32)
    ns = np.zeros((32, DEPTH, 2, D), np.float32)
    for i, r in enumerate(results):
        y = np.asarray(r["yT"]).T
        y_s[i] = y[:2048]
        y_p[4 * i:4 * i + 4] = y[2048:].reshape(4, 256, D)
        k = np.asarray(r["knew"]).reshape(DEPTH, 4, 128, 4, 256)
        nk[4 * i:4 * i + 4] = k.transpose(3, 0, 4, 1, 2)
        v = np.asarray(r["vnew"]).reshape(DEPTH, 4, 256, 4, 128)
        nv[4 * i:4 * i + 4] = v.transpose(1, 0, 2, 3, 4)
        s = np.asarray(r["nsT"]).reshape(128, 4, DEPTH, 2, 16)
        ns[4 * i:4 * i + 4] = s.transpose(1, 2, 3, 4, 0).reshape(4, DEPTH, 2, D)
    return y_p, y_s, nk, nv, ns


def kernel(**inputs):
    in_maps = _prep(inputs)
    nc = build()
    res = run_bass_kernel_spmd(nc, in_maps, core_ids=list(range(8)))
    return _gather(res.results)
```

```python
import numpy as np
from contextlib import ExitStack
import concourse.bass as bass
import concourse.mybir as mybir
from concourse.bass_utils import run_bass_kernel_spmd

F32 = mybir.dt.float32
BF16 = mybir.dt.bfloat16
ALU = mybir.AluOpType
AF = mybir.ActivationFunctionType
ENGS = ("pe", "act", "dve", "pool", "sp")

D = 2048
KC = 16
T = 3072
TSMP = 2048
NTB = 6
DEPTH = 4
INW = 17408
DFF = 8192
OQ, OK_, OV, OLX, OLG, OCU, OCV, OG = 0, 2048, 2560, 3072, 5120, 7168, 9216, 11264
EPS = 1e-6
SCALE = 128 ** -0.5
GC1 = 0.044715
GC2 = 1.5957691216057308
P_BMOD, P_GPM, P_GPO, P_GPF, P_GOF, P_CW, P_CB, P_BA, P_BX, P_LAM, P_GQ, P_GK, NPL = (
    0, 96, 112, 128, 144, 160, 224, 240, 272, 304, 336, 337, 338)


class Chan:
    def __init__(self, sem):
        self.sem = sem
        self.cum = 0


class Sched:
    def __init__(self, nc, stack):
        self.nc = nc
        self.q = {e: [] for e in ENGS}
        self.cnt = {e: 0 for e in ENGS}
        self.esem = {e: stack.enter_context(nc.semaphore("prog_" + e)) for e in ENGS}
        self.seen = {e: {} for e in ENGS}
        self.lastw = {}
        self.readers = {}
        self.stack = stack
        self.chans = {}

    def chan(self, name):
        if name not in self.chans:
            self.chans[name] = Chan(self.stack.enter_context(self.nc.semaphore("ch_" + name)))
        return self.chans[name]

    def _need(self, eng, tok):
        sem, val, _ = tok
        k = id(sem)
        if self.seen[eng].get(k, 0) >= val:
            return
        self.seen[eng][k] = val
        self.q[eng].append(("wait", sem, val))

    def _deps(self, eng, reads, writes):
        for k in reads:
            t = self.lastw.get(k)
            if t is not None:
                self._need(eng, t)
        for k in writes:
            t = self.lastw.get(k)
            if t is not None and t[2] != eng:
                self._need(eng, t)
            for t in self.readers.get(k, {}).values():
                if t[2] != eng:
                    self._need(eng, t)

    def _commit(self, tok, reads, writes):
        for k in writes:
            self.lastw[k] = tok
            self.readers[k] = {}
        for k in reads:
            self.readers.setdefault(k, {})[id(tok[0])] = tok

    def op(self, eng, fn, reads=(), writes=()):
        self._deps(eng, reads, writes)
        self.cnt[eng] += 1
        tok = (self.esem[eng], self.cnt[eng], eng)
        self.q[eng].append(("op", fn))
        self._commit(tok, reads, writes)

    def dma(self, eng, chname, out, in_, reads=(), writes=()):
        ch = self.chan(chname)
        self._deps(eng, reads, writes)
        if ch.cum:
            self._need(eng, (ch.sem, ch.cum, None))
        ch.cum += 16
        tok = (ch.sem, ch.cum, None)
        self.q[eng].append(("dma", out, in_, ch.sem))
        self._commit(tok, reads, writes)

    def barrier(self):
        toks = [(self.esem[e], self.cnt[e], None) for e in ENGS if self.cnt[e]]
        toks += [(c.sem, c.cum, None) for c in self.chans.values() if c.cum]
        for e in ENGS:
            for t in toks:
                self._need(e, t)
        self.lastw = {}
        self.readers = {}

    def emit(self, block):
        engobj = {"pe": "tensor", "act": "scalar", "dve": "vector", "pool": "gpsimd", "sp": "sync"}

        def runner(e):
            def f(eng):
                sem = self.esem[e]
                for item in self.q[e]:
                    if item[0] == "wait":
                        eng.wait_ge(item[1], item[2])
                    elif item[0] == "op":
                        item[1](eng).then_inc(sem, 1)
                    else:
                        eng.dma_start(out=item[1], in_=item[2]).then_inc(item[3], 16)
            return f

        for e in ENGS:
            getattr(block, engobj[e])(runner(e))


def build(nlayers=DEPTH, debug=False, stop=None):
    nc = bass.Bass("TRN2", target_bir_lowering=False)
    st = ExitStack()
    with st:
        def din(name, shape, dt=F32):
            return nc.dram_tensor(name, list(shape), dt, kind="ExternalInput").ap()

        def dout(name, shape, dt=F32):
            return nc.dram_tensor(name, list(shape), dt, kind="ExternalOutput").ap()

        def dscr(name, shape, dt):
            kind = "ExternalOutput" if debug else "Internal"
            return nc.dram_tensor(name, list(shape), dt, kind=kind).ap()

        xT0 = din("xT0", [D, T])
        condT = din("condT", [128, KC, 2])
        kctx = din("kctx", [DEPTH, 512, 256])
        vctx = din("vctx", [DEPTH, 256, 512])
        h0T = din("h0T", [128, DEPTH * 2 * 16])
        ptab = din("ptab", [128, DEPTH * NPL])
        cosT = din("cosT", [128, TSMP])
        sinT = din("sinT", [128, TSMP])
        permd = din("permd", [128, 128])
        w_mod = din("w_mod", [nlayers, D, 6 * D])
        w_in = din("w_in", [nlayers, D, INW])
        w_attn_out = din("w_attn_out", [nlayers, D, D])
        w_lru_out = din("w_lru_out", [nlayers, D, D])
        w_cm_out = din("w_cm_out", [nlayers, D, D])
        w_out = din("w_out", [nlayers, D, D])
        w_ff1 = din("w_ff1", [nlayers, D, DFF])
        w_ff2 = din("w_ff2", [nlayers, DFF, D])
        lru_wa = din("lru_wa", [nlayers, 2, 16, 128, 128])
        lru_wx = din("lru_wx", [nlayers, 2, 16, 128, 128])
        wsT = din("wsT", [nlayers, 128, 16, 128])
        cm_bs = din("cm_bs", [nlayers, 1, D])
        cm_g = din("cm_g", [nlayers, 1, D])

        yT = dout("yT", [D, T])
        knew = dout("knew", [DEPTH, 512, 1024])
        vnew = dout("vnew", [DEPTH, 1024, 512])
        nsT = dout("nsT", [128, 4 * DEPTH * 2 * 16])

        QT = dscr("QT", [D, T], BF16)
        KT = dscr("KT", [512, T], BF16)
        VS = dscr("VS", [T, 512], BF16)
        LRUO = dscr("LRUO", [D, T], BF16)
        GCU = dscr("GCU", [D, T], BF16)
        GCV = dscr("GCV", [T, D], BF16)
        CM = dscr("CM", [D, T], BF16)
        GG = dscr("GG", [3 * D, T], BF16)
        ATT = dscr("ATT", [D, T], BF16)
        OT = dscr("OT", [D, T], F32)
        H2T = dscr("H2T", [D, T], BF16)
        FT = dscr("FT", [D, T], F32)

        S = Sched(nc, st)
        ARENA_W = 53000
        arena = st.enter_context(nc.sbuf_tensor("arena", [128, ARENA_W], F32))
        PS = [st.enter_context(nc.psum_tensor("ps%d" % i, [128, 512], F32)) for i in range(8)]

        def carve(off, nbytes, dt, pat=None, **kw):
            assert off % 4 == 0 and nbytes % 4 == 0 and off + nbytes <= ARENA_W * 4, (off, nbytes)
            a = arena[:, off // 4:(off + nbytes) // 4]
            if dt != F32:
                a = a.bitcast(dt)
            if pat is not None:
                a = a.rearrange(pat, **kw)
            return a

        class Region:
            def __init__(self, base, size):
                self.base, self.size, self.cur = base, size, 0

            def reset(self):
                self.cur = 0

            def get(self, shape, dt=F32):
                n = 1
                for s_ in shape:
                    n *= s_
                nb = n * (4 if dt == F32 else 2)
                nb = (nb + 31) // 32 * 32
                assert self.cur + nb <= self.size, (self.cur, nb, self.size)
                off = self.base + self.cur
                self.cur += nb
                if len(shape) == 1:
                    return carve(off, nb, dt)[:, 0:shape[0]]
                if len(shape) == 2:
                    return carve(off, nb, dt)[:, 0:n].rearrange("p (a b) -> p a b", a=shape[0])
                return carve(off, nb, dt)[:, 0:n].rearrange("p (a b c) -> p a b c", a=shape[0], b=shape[1])

        RC = Region(0, 12288)
        RA = Region(12288, 98304)
        RW = Region(12288 + 98304, 32768)
        RM = Region(12288 + 98304 + 32768, ARENA_W * 4 - (12288 + 98304 + 32768))

        ONES11 = RC.get([128], BF16)
        ONES7 = RC.get([128], BF16)
        ONES1 = RC.get([128], BF16)
        PERM = RC.get([128], BF16)
        PTt = RC.get([DEPTH * NPL])
        CONDB = RC.get([KC, 2], BF16)
        MOD = RC.get([96, 2])
        A1 = RC.get([16, 2]); B1 = MOD[:, 0:16, :]
        G1P = RC.get([16, 2])
        A2 = RC.get([16, 2]); B2 = MOD[:, 48:64, :]
        G2P = RC.get([16, 2])
        NBA = RC.get([32]); NBX = RC.get([32]); CC = RC.get([32]); C2 = RC.get([32])
        NS = RC.get([4 * DEPTH * 2 * 16])
        H0 = RC.get([DEPTH * 2 * 16])
        SSQ = RC.get([96])
        RCV = RC.get([24])
        CTMP = RC.get([KC, 2])

        blk = st.enter_context(nc.Block())

        def ACT(out, in_, func, reads, writes, bias=None, scale=None, accum=None):
            kw = {}
            if bias is not None:
                kw["bias"] = bias
            if scale is not None:
                kw["scale"] = scale
            if accum is not None:
                kw["accum_out"] = accum
            S.op("act", lambda e: e.activation(out=out, in_=in_, func=func, **kw), reads=reads, writes=writes)

        def TSC(out, in0, s1, s2, op0, op1, reads, writes, eng="dve"):
            if op1 is None:
                S.op(eng, lambda e: e.tensor_scalar(out=out, in0=in0, scalar1=s1, scalar2=None, op0=op0), reads=reads, writes=writes)
            else:
                S.op(eng, lambda e: e.tensor_scalar(out=out, in0=in0, scalar1=s1, scalar2=s2, op0=op0, op1=op1), reads=reads, writes=writes)

        def TT(out, in0, in1, op, reads, writes, eng="dve"):
            S.op(eng, lambda e: e.tensor_tensor(out=out, in0=in0, in1=in1, op=op), reads=reads, writes=writes)

        def STT(out, in0, scalar, in1, op0, op1, reads, writes):
            S.op("dve", lambda e: e.scalar_tensor_tensor(out=out, in0=in0, scalar=scalar, in1=in1, op0=op0, op1=op1), reads=reads, writes=writes)

        def CP(out, in_, reads, writes, eng="dve"):
            S.op(eng, lambda e: e.tensor_copy(out=out, in_=in_), reads=reads, writes=writes)

        def MM(ps_ap, pairs, reads, pskey):
            def f(e):
                n = len(pairs)
                ins = None
                for i, (l, r) in enumerate(pairs):
                    ins = e.matmul(ps_ap, lhsT=l, rhs=r, start=(i == 0), stop=(i == n - 1))
                return ins
            S.op("pe", f, reads=reads, writes=[pskey])

        def MSET(ap, val, writes, eng="dve"):
            S.op(eng, lambda e: e.memset(ap, val), writes=writes)

        WSLOT = [RW.get([8192], BF16), RW.get([8192], BF16)]
        wctr = [0]

        class WStream:
            def __init__(self, specs, off=0, tag="w"):
                self.specs = specs
                self.issued = 0
                self.info = {}
                self.off, self.tag = off, tag
                self.ctr = wctr if tag == "w" else [0]

            def _issue(self, i):
                slot = self.ctr[0] % 2
                self.ctr[0] += 1
                parts = self.specs[i]
                kcn = parts[0][1]
                ntot = sum(p[2] for p in parts)
                assert self.off + kcn * ntot <= 8192
                view = WSLOT[slot][:, self.off:self.off + kcn * ntot].rearrange("p (kc n) -> p kc n", kc=kcn)
                c0 = 0
                for (src, kcn_, n) in parts:
                    srcv = src.rearrange("(kc p) n -> p kc n", p=128)
                    for k0 in range(0, kcn_, 16):
                        S.dma("pool", "%s%d" % (self.tag, slot), view[:, k0:k0 + 16, c0:c0 + n],
                              srcv[:, k0:k0 + 16, :], writes=[(self.tag, slot)])
                    c0 += n
                self.info[i] = (view, (self.tag, slot))

            def get(self, i):
                while self.issued <= min(i + 1, len(self.specs) - 1):
                    self._issue(self.issued)
                    self.issued += 1
                return self.info[i]

        def psk(i):
            return "ps%d" % i

        S.dma("sp", "c0", PTt, ptab[:, :], writes=["pt"])
        S.dma("sp", "c1", H0, h0T[:, :], writes=["h0"])
        CONDF = RM.get([KC, 2])
        S.dma("sp", "c2", CONDF, condT[:, :, :], writes=["condf"])
        S.dma("pool", "c3", PERM, permd[:, :], writes=["perm"])
        MSET(ONES11, 2.0 ** -11, ["ones11"])
        MSET(ONES7, 2.0 ** -7, ["ones7"])
        MSET(ONES1, 1.0, ["ones1"])
        MSET(NS, 0.0, ["ns"])
        ACT(CTMP, CONDF, AF.Exp, ["condf"], ["ctmp"], scale=-1.0)
        ACT(CTMP, CTMP, AF.Ln, ["ctmp"], ["ctmp"], bias=1.0)
        ACT(CTMP, CTMP, AF.Exp, ["ctmp"], ["ctmp"], scale=-1.0)
        TT(CONDB, CTMP, CONDF, ALU.mult, ["ctmp", "condf"], ["condb"])
        S.barrier()

        for l in range(nlayers):
            pc = l * NPL
            XSRC = xT0 if l == 0 else yT

            def xt_reads(tb, l=l):
                return [] if l == 0 else [("xt", tb)]

            RM.reset()
            ws = WStream([[(w_mod[l][:, cg * 512:(cg + 1) * 512], KC, 512)] for cg in range(24)])
            PSM = PS[0][:, 0:192]
            for cg in range(24):
                wv, wk = ws.get(cg)

                def f(e, wv=wv, cg=cg):
                    ins = None
                    for j in range(4):
                        c0 = (cg * 4 + j) * 2
                        for kc in range(KC):
                            ins = e.matmul(PSM[:, c0:c0 + 2], lhsT=wv[:, kc, j * 128:(j + 1) * 128],
                                           rhs=CONDB[:, kc, :], start=(kc == 0), stop=(kc == KC - 1))
                    return ins
                S.op("pe", f, reads=[wk, "condb"], writes=[psk(0)])
            PSMv = PSM.rearrange("p (a c) -> p a c", c=2)
            for c in range(2):
                TT(MOD[:, :, c], PSMv[:, :, c], PTt[:, pc + P_BMOD:pc + P_BMOD + 96], ALU.add,
                   [psk(0), "pt"], [("mod", c)])
            for c in range(2):
                STT(A1[:, :, c], MOD[:, 16:32, c], 1.0, PTt[:, pc + P_GPM:pc + P_GPM + 16], ALU.add, ALU.mult,
                    [("mod", c), "pt"], [("a1", c)])
                TT(G1P[:, :, c], MOD[:, 32:48, c], PTt[:, pc + P_GPO:pc + P_GPO + 16], ALU.mult,
                   [("mod", c), "pt"], [("g1p", c)])
                STT(A2[:, :, c], MOD[:, 64:80, c], 1.0, PTt[:, pc + P_GPF:pc + P_GPF + 16], ALU.add, ALU.mult,
                    [("mod", c), "pt"], [("a2", c)])
                TT(G2P[:, :, c], MOD[:, 80:96, c], PTt[:, pc + P_GOF:pc + P_GOF + 16], ALU.mult,
                   [("mod", c), "pt"], [("g2p", c)])
            TSC(NBA, PTt[:, pc + P_BA:pc + P_BA + 32], -1.0, None, ALU.mult, None, ["pt"], ["nba"])
            TSC(NBX, PTt[:, pc + P_BX:pc + P_BX + 32], -1.0, None, ALU.mult, None, ["pt"], ["nbx"])
            ACT(CC, PTt[:, pc + P_LAM:pc + P_LAM + 32], AF.Exp, ["pt"], ["cc"], scale=-1.0)
            ACT(CC, CC, AF.Ln, ["cc"], ["cc"], bias=1.0)
            TSC(C2, CC, -16.0, None, ALU.mult, None, ["cc"], ["c2"])
            TSC(CC, CC, -8.0, None, ALU.mult, None, ["cc"], ["cc"])
            S.barrier()
            if stop == "P0":
                break

            RM.reset()
            RA.reset()
            HALL = RA.get([KC, T], BF16)
            XB = [RM.get([KC, 512]), carve(RW.base, 32768, F32, "p (a b) -> p a b", a=KC)]
            SQ = RM.get([KC, 512], BF16)
            RS = RM.get([512])
            TMP = [RM.get([512]), RM.get([512])]

            def norm_stats(src, srckey, sqkey="sq", rskey="rs", psi=1):
                ACT(SQ, src, AF.Square, [srckey], [sqkey])
                MM(PS[psi][:, :], [(ONES11, SQ[:, kc, :]) for kc in range(KC)], ["ones11", sqkey], psk(psi))
                ACT(RS, PS[psi][:, :], AF.Ln, [psk(psi)], [rskey], bias=EPS)
                ACT(RS, RS, AF.Exp, [rskey], [rskey], scale=-0.5)

            for tb in range(NTB):
                c = 0 if tb < 4 else 1
                xb = XB[tb % 2]
                xk = ("xb", tb % 2)
                S.dma("sp", "xb%d" % (tb % 2), xb, XSRC[:, tb * 512:(tb + 1) * 512].rearrange("(kc p) t -> p kc t", p=128),
                      reads=xt_reads(tb), writes=[xk])
                norm_stats(xb, xk)
                for kc in range(KC):
                    tmp = TMP[kc % 2]
                    tk = ("tmp", kc % 2)
                    STT(tmp, xb[:, kc, :], A1[:, kc, c:c + 1], RS, ALU.mult, ALU.mult, [xk, ("a1", c), "rs"], [tk])
                    ACT(HALL[:, kc, tb * 512:(tb + 1) * 512], tmp, AF.Identity, [tk, ("mod", c)], [("hall", tb)],
                        bias=B1[:, kc, c:c + 1])
            S.barrier()
            if stop == "P1":
                break

            RM.reset()
            COS = RM.get([TSMP]); SIN = RM.get([TSMP])
            S.dma("sp", "cos", COS, cosT[:, :], writes=["cos"])
            S.dma("sp", "sin", SIN, sinT[:, :], writes=["sin"])
            SQ1 = [RM.get([512], BF16) for _ in range(2)]
            RS1 = [RM.get([512]) for _ in range(2)]
            QN = [RM.get([512]) for _ in range(2)]
            QNB = [RM.get([512], BF16) for _ in range(2)]
            T1 = [RM.get([512]) for _ in range(2)]
            T2 = [RM.get([512]) for _ in range(2)]
            QO = [RM.get([512], BF16) for _ in range(2)]
            VTb = [RM.get([512], BF16) for _ in range(2)]
            VF = [RM.get([512]) for _ in range(2)]
            specs = [[(w_in[l][:, OQ + g * 512:OQ + (g + 1) * 512], KC, 512)] for g in range(4)]
            specs.append([(w_in[l][:, OK_:OK_ + 512], KC, 512)])
            specs.append([(w_in[l][:, OV:OV + 512], KC, 512)])
            ws = WStream(specs)
            it = 0
            lim = stop.split(":")[1] if (stop and stop.startswith("P2a:")) else None
            for g in range({None: 5, "q1": 1, "q": 4, "qk": 5, "norope": 1, "nodma": 1}[lim]):
                wv, wk = ws.get(g)
                isk = (g == 4)
                gcol = pc + (P_GK if isk else P_GQ)
                for tb in range(1 if lim in ("q1", "norope", "nodma") else NTB):
                    hk_ = ("hall", tb)
                    for j in range(4):
                        r = it % 2
                        it += 1
                        pz, psn, pw = 2 + r, 4 + r, 6 + r
                        MM(PS[pz][:, :], [(wv[:, kc, j * 128:(j + 1) * 128], HALL[:, kc, tb * 512:(tb + 1) * 512]) for kc in range(KC)],
                           [wk, hk_], psk(pz))
                        ACT(SQ1[r], PS[pz][:, :], AF.Square, [psk(pz)], [("sq1", r)])
                        MM(PS[psn][:, :], [(ONES7, SQ1[r])], ["ones7", ("sq1", r)], psk(psn))
                        ACT(RS1[r], PS[psn][:, :], AF.Ln, [psk(psn)], [("rs1", r)], bias=EPS)
                        ACT(RS1[r], RS1[r], AF.Exp, [("rs1", r)], [("rs1", r)], scale=-0.5)
                        STT(QN[r], PS[pz][:, :], PTt[:, gcol:gcol + 1], RS1[r], ALU.mult, ALU.mult,
                            [psk(pz), "pt", ("rs1", r)], [("qn", r)])
                        if tb < 4 and lim != "norope":
                            ACT(QNB[r], QN[r], AF.Copy, [("qn", r)], [("qnb", r)])
                            MM(PS[pw][:, :], [(PERM, QNB[r])], ["perm", ("qnb", r)], psk(pw))
                            TT(T1[r], QN[r], COS[:, tb * 512:(tb + 1) * 512], ALU.mult, [("qn", r), "cos"], [("t1", r)])
                            TT(T2[r], PS[pw][:, :], SIN[:, tb * 512:(tb + 1) * 512], ALU.mult, [psk(pw), "sin"], [("t2", r)])
                            TT(QO[r], T1[r], T2[r], ALU.add, [("t1", r), ("t2", r)], [("qo", r)])
                        else:
                            ACT(QO[r], QN[r], AF.Copy, [("qn", r)], [("qo", r)])
                        if lim == "nodma":
                            pass
                        elif isk:
                            S.dma("sp", "qo%d" % r, KT[j * 128:(j + 1) * 128, tb * 512:(tb + 1) * 512], QO[r],
                                  reads=[("qo", r)], writes=[("kt", tb)])
                            if tb >= 4:
                                S.dma("sp", "kf%d" % r, knew[l][j * 128:(j + 1) * 128, (tb - 4) * 512:(tb - 3) * 512], QN[r],
                                      reads=[("qn", r)], writes=[])
                        else:
                            h = g * 4 + j
                            S.dma("sp", "qo%d" % r, QT[h * 128:(h + 1) * 128, tb * 512:(tb + 1) * 512], QO[r],
                                  reads=[("qo", r)], writes=[("qt", tb)])
            wv, wk = ws.get(5) if lim is None else (None, None)
            for tt in range(24 if lim is None else 0):
                r = tt % 2
                pv = 0 + r
                MM(PS[pv][:, :], [(HALL[:, kc, tt * 128:(tt + 1) * 128], wv[:, kc, :]) for kc in range(KC)],
                   [wk, ("hall", tt // 4)], psk(pv))
                ACT(VTb[r], PS[pv][:, :], AF.Copy, [psk(pv)], [("vt", r)])
                S.dma("sp", "vt%d" % r, VS[tt * 128:(tt + 1) * 128, :], VTb[r], reads=[("vt", r)], writes=[("vs", tt // 4)])
                if tt >= 16:
                    ACT(VF[r], PS[pv][:, :], AF.Copy, [psk(pv)], [("vf", r)])
                    S.dma("sp", "vf%d" % r, vnew[l][(tt - 16) * 128:(tt - 15) * 128, :], VF[r], reads=[("vf", r)], writes=[])
            S.barrier()
            if stop and stop.startswith("P2a"):
                break

            RM.reset()
            XPS = RM.get([2052])
            XPP = RM.get([4, 259])
            GLG = RM.get([T], BF16)
            XC = RM.get([TSMP])
            XCB = RM.get([TSMP], BF16)
            HF = RM.get([TSMP])
            G1 = RM.get([512]); AA = RM.get([512]); G2 = RM.get([512])
            HBB = [RM.get([512]), RM.get([512])]
            SM = RM.get([512])
            GX2 = [RM.get([512]) for _ in range(2)]
            GXS = [RM.get([512]) for _ in range(2)]
            GW = [RM.get([512]) for _ in range(2)]
            LW = [RM.get([4, 128], BF16) for _ in range(2)]
            MSET(XPS, 0.0, ["xp"])
            MSET(XPP, 0.0, ["xp"])
            ws = WStream([[(w_in[l][:, OLX + n * 128:OLX + (n + 1) * 128], KC, 128),
                           (w_in[l][:, OLG + n * 128:OLG + (n + 1) * 128], KC, 128)] for n in range(16)])

            GOZ = [RM.get([512], BF16) for _ in range(2)]
            wsg = WStream([[(w_in[l][:, OG + s_ * 256:OG + (s_ + 1) * 256], KC, 256)] for s_ in range(24)],
                          off=4096, tag="wg")

            def gg_units():
                itg = 0
                for s_ in range(24):
                    wvg, wkg = wsg.get(s_)
                    for tbg in range(NTB):
                        for jg in range(2):
                            rg = itg % 2
                            itg += 1
                            pzg = 6 + rg
                            MM(PS[pzg][:, :], [(wvg[:, kc, jg * 128:(jg + 1) * 128], HALL[:, kc, tbg * 512:(tbg + 1) * 512])
                                               for kc in range(KC)], [wkg, ("hall", tbg)], psk(pzg))
                            CP(GOZ[rg], PS[pzg][:, :], [psk(pzg)], [("goz", rg)])
                            fcg = s_ * 2 + jg
                            S.dma("sp", "goz%d" % rg, GG[fcg * 128:(fcg + 1) * 128, tbg * 512:(tbg + 1) * 512], GOZ[rg],
                                  reads=[("goz", rg)], writes=[("gg", tbg)])
                            yield

            ggen = gg_units()

            def gg_step():
                next(ggen, None)

            def gelu6(ps_ap, pkey, out_ap, outkeys, r):
                x2, xs, w_ = GX2[r], GXS[r], GW[r]
                ACT(x2, ps_ap, AF.Square, [pkey], [("gx2", r)])
                ACT(xs, ps_ap, AF.Copy, [pkey], [("gxs", r)])
                TSC(w_, x2, GC1, 1.0, ALU.mult, ALU.add, [("gx2", r)], [("gw", r)])
                TT(w_, w_, xs, ALU.mult, [("gw", r), ("gxs", r)], [("gw", r)])
                ACT(x2, w_, AF.Exp, [("gw", r)], [("gx2", r)], scale=-GC2)
                ACT(x2, x2, AF.Ln, [("gx2", r)], [("gx2", r)], bias=1.0)
                ACT(x2, x2, AF.Exp, [("gx2", r)], [("gx2", r)], scale=-1.0)
                TT(out_ap, x2, xs, ALU.mult, [("gx2", r), ("gxs", r)], outkeys)

            def lru_gates(n, d, xcb_blk, xc_blk, lw, lwk):
                idx = d * 16 + n
                MM(PS[2][:, :], [(lw[:, d, :], xcb_blk)], [lwk, "xcb"], psk(2))
                MM(PS[3][:, :], [(lw[:, 2 + d, :], xcb_blk)], [lwk, "xcb"], psk(3))
                ACT(G1, PS[2][:, :], AF.Exp, [psk(2), "nba"], ["g1"], scale=-1.0, bias=NBA[:, idx:idx + 1])
                ACT(G2, PS[3][:, :], AF.Exp, [psk(3), "nbx"], ["g2"], scale=-1.0, bias=NBX[:, idx:idx + 1])
                ACT(G1, G1, AF.Ln, ["g1"], ["g1"], bias=1.0)
                ACT(G2, G2, AF.Ln, ["g2"], ["g2"], bias=1.0)
                ACT(G1, G1, AF.Exp, ["g1"], ["g1"], scale=-1.0)
                ACT(G2, G2, AF.Exp, ["g2"], ["g2"], scale=-1.0)
                ACT(AA, G1, AF.Exp, ["g1", "cc"], ["aa"], scale=CC[:, idx:idx + 1])
                TT(G2, G2, xc_blk, ALU.mult, ["g2", "xc"], ["g2"])
                ACT(G1, G1, AF.Exp, ["g1", "c2"], ["g1"], scale=C2[:, idx:idx + 1])
                ACT(G1, G1, AF.Ln, ["g1"], ["g1"], scale=-0.9999999, bias=1.0)
                ACT(G1, G1, AF.Exp, ["g1"], ["g1"], scale=0.5)
                TT(G2, G1, G2, ALU.mult, ["g1", "g2"], ["g2"])

            def conv(dst, src_at, n):
                cw = pc + P_CW
                TSC(dst, src_at(0), PTt[:, cw + n:cw + n + 1], PTt[:, pc + P_CB + n:pc + P_CB + n + 1],
                    ALU.mult, ALU.add, ["xp", "pt"], ["xc"])
                for j in range(1, 4):
                    STT(dst, src_at(j), PTt[:, cw + j * 16 + n:cw + j * 16 + n + 1], dst, ALU.mult, ALU.add,
                        ["xp", "pt", "xc"], ["xc"])

            git = 0
            for n in range(16):
                wv, wk = ws.get(n)
                lw = LW[n % 2]
                lwk = ("lw", n % 2)
                S.dma("pool", "lwa%d" % (n % 2), lw[:, 0:2, :], lru_wa[l][:, n].rearrange("d c e -> c d e"), writes=[lwk])
                S.dma("pool", "lwx%d" % (n % 2), lw[:, 2:4, :], lru_wx[l][:, n].rearrange("d c e -> c d e"), writes=[lwk])
                for tb in range(NTB):
                    r = git % 2
                    git += 1
                    px, pg = 0 + r, 4 + r
                    MM(PS[px][:, :], [(wv[:, kc, 0:128], HALL[:, kc, tb * 512:(tb + 1) * 512]) for kc in range(KC)],
                       [wk, ("hall", tb)], psk(px))
                    MM(PS[pg][:, :], [(wv[:, kc, 128:256], HALL[:, kc, tb * 512:(tb + 1) * 512]) for kc in range(KC)],
                       [wk, ("hall", tb)], psk(pg))
                    if tb < 4:
                        ACT(XPS[:, 2 + tb * 512:2 + (tb + 1) * 512], PS[px][:, :], AF.Copy, [psk(px)], ["xp"])
                    else:
                        ACT(XPP[:, (tb - 4) * 2:(tb - 3) * 2, 2:258], PS[px][:, :].rearrange("p (s t) -> p s t", s=2),
                            AF.Copy, [psk(px)], ["xp"])
                    gelu6(PS[pg][:, :], psk(pg), GLG[:, tb * 512:(tb + 1) * 512], ["glg"], r)
                    gg_step()
                conv(XC, lambda j: XPS[:, j:j + TSMP], n)
                ACT(XCB, XC, AF.Copy, ["xc"], ["xcb"])
                for tb in range(4):
                    sl = slice(tb * 512, (tb + 1) * 512)
                    lru_gates(n, 0, XCB[:, sl], XC[:, sl], lw, lwk)
                    gg_step()
                    init = H0[:, (l * 2 + 0) * 16 + n:(l * 2 + 0) * 16 + n + 1] if tb == 0 else HF[:, tb * 512 - 1:tb * 512]
                    S.op("dve", lambda e, sl=sl, init=init: e.tensor_tensor_scan(
                        out=HF[:, sl], data0=AA, data1=G2, initial=init, op0=ALU.mult, op1=ALU.add),
                        reads=["aa", "g2", "hf", "h0"], writes=["hf"])
                for tb in range(3, -1, -1):
                    sl = slice(tb * 512, (tb + 1) * 512)
                    hb = HBB[tb % 2]
                    lru_gates(n, 1, XCB[:, sl], XC[:, sl], lw, lwk)
                    gg_step()
                    init = H0[:, (l * 2 + 1) * 16 + n:(l * 2 + 1) * 16 + n + 1] if tb == 3 else HBB[(tb + 1) % 2][:, 0:1]
                    S.op("dve", lambda e, hb=hb, init=init: e.tensor_tensor_scan(
                        out=hb[:, ::-1], data0=AA[:, ::-1], data1=G2[:, ::-1], initial=init, op0=ALU.mult, op1=ALU.add),
                        reads=["aa", "g2", ("hbb", (tb + 1) % 2), "h0"], writes=[("hbb", tb % 2)])
                    TT(SM, hb, HF[:, sl], ALU.add, [("hbb", tb % 2), "hf"], ["sm"])
                    TT(GLG[:, sl], SM, GLG[:, sl], ALU.mult, ["sm", "glg"], ["glg"])
                S.dma("sp", "lruo", LRUO[n * 128:(n + 1) * 128, 0:TSMP], GLG[:, 0:TSMP], reads=["glg"], writes=[("lruo", n)])
                XCp = XC[:, 0:1024].rearrange("p (s t) -> p s t", s=4)
                conv(XCp, lambda j: XPP[:, :, j:j + 256], n)
                ACT(XCB[:, 0:1024], XC[:, 0:1024], AF.Copy, ["xc"], ["xcb"])
                for tb in range(2):
                    sl = slice(tb * 512, (tb + 1) * 512)
                    lru_gates(n, 0, XCB[:, sl], XC[:, sl], lw, lwk)
                    gg_step()
                    for s2 in range(2):
                        ss = slice(tb * 512 + s2 * 256, tb * 512 + (s2 + 1) * 256)
                        sb = slice(s2 * 256, (s2 + 1) * 256)
                        S.op("dve", lambda e, ss=ss, sb=sb: e.tensor_tensor_scan(
                            out=HF[:, ss], data0=AA[:, sb], data1=G2[:, sb], initial=0.0, op0=ALU.mult, op1=ALU.add),
                            reads=["aa", "g2"], writes=["hf"])
                for tb in range(2):
                    sl = slice(tb * 512, (tb + 1) * 512)
                    hb = HBB[tb % 2]
                    lru_gates(n, 1, XCB[:, sl], XC[:, sl], lw, lwk)
                    gg_step()
                    for s2 in range(2):
                        sb = slice(s2 * 256, (s2 + 1) * 256)
                        S.op("dve", lambda e, hb=hb, sb=sb: e.tensor_tensor_scan(
                            out=hb[:, sb][:, ::-1], data0=AA[:, sb][:, ::-1], data1=G2[:, sb][:, ::-1], initial=0.0,
                            op0=ALU.mult, op1=ALU.add),
                            reads=["aa", "g2"], writes=[("hbb", tb % 2)])
                    for s2 in range(2):
                        seq = tb * 2 + s2
                        o = ((seq * DEPTH + l) * 2) * 16 + n
                        CP(NS[:, o:o + 1], HF[:, seq * 256 + 255:seq * 256 + 256], ["hf"], ["ns"])
                        CP(NS[:, o + 16:o + 17], hb[:, s2 * 256:s2 * 256 + 1], [("hbb", tb % 2)], ["ns"])
                    TT(SM, hb, HF[:, sl], ALU.add, [("hbb", tb % 2), "hf"], ["sm"])
                    gs = slice(TSMP + tb * 512, TSMP + (tb + 1) * 512)
                    TT(GLG[:, gs], SM, GLG[:, gs], ALU.mult, ["sm", "glg"], ["glg"])
                S.dma("sp", "lruo2", LRUO[n * 128:(n + 1) * 128, TSMP:T], GLG[:, TSMP:T], reads=["glg"], writes=[("lruo", n)])
            for _ in ggen:
                pass
            S.barrier()
            if stop == "P2b":
                break

            RM.reset()
            GX2 = [RM.get([512]) for _ in range(2)]
            GXS = [RM.get([512]) for _ in range(2)]
            GW = [RM.get([512]) for _ in range(2)]
            GO = [RM.get([512], BF16) for _ in range(2)]
            JUNK = RM.get([512], BF16)
            MSET(SSQ, 0.0, ["ssq"])

            def gelut(ps_ap, pkey, out_ap, outkeys, r):
                x2, xh, w_ = GX2[r], GXS[r], GW[r]
                ACT(x2, ps_ap, AF.Square, [pkey], [("gx2", r)])
                ACT(xh, ps_ap, AF.Identity, [pkey], [("gxs", r)], scale=0.5)
                TSC(w_, x2, GC1, 1.0, ALU.mult, ALU.add, [("gx2", r)], [("gw", r)])
                TT(w_, w_, xh, ALU.mult, [("gw", r), ("gxs", r)], [("gw", r)])
                ACT(x2, w_, AF.Tanh, [("gw", r)], [("gx2", r)], scale=GC2)
                STT(out_ap, x2, 1.0, xh, ALU.add, ALU.mult, [("gx2", r), ("gxs", r)], outkeys)

            specs = [[(w_in[l][:, OCU + g * 512:OCU + (g + 1) * 512], KC, 512)] for g in range(4)]
            specs += [[(w_in[l][:, OCV + g * 512:OCV + (g + 1) * 512], KC, 512)] for g in range(4)]
            ws = WStream(specs)
            it = 0
            for g in range(4):
                wv, wk = ws.get(g)
                for tb in range(NTB):
                    for j in range(4):
                        r = it % 2
                        it += 1
                        pz = 0 + r
                        MM(PS[pz][:, :], [(wv[:, kc, j * 128:(j + 1) * 128], HALL[:, kc, tb * 512:(tb + 1) * 512]) for kc in range(KC)],
                           [wk, ("hall", tb)], psk(pz))
                        gelut(PS[pz][:, :], psk(pz), GO[r], [("go", r)], r)
                        fc = g * 4 + j
                        S.dma("sp", "go%d" % r, GCU[fc * 128:(fc + 1) * 128, tb * 512:(tb + 1) * 512], GO[r],
                              reads=[("go", r)], writes=[("gcu", tb)])
            for g in range(4):
                wv, wk = ws.get(4 + g)
                for tt in range(24):
                    r = it % 2
                    it += 1
                    pz = 0 + r
                    MM(PS[pz][:, :], [(HALL[:, kc, tt * 128:(tt + 1) * 128], wv[:, kc, :]) for kc in range(KC)],
                       [wk, ("hall", tt // 4)], psk(pz))
                    gelut(PS[pz][:, :], psk(pz), GO[r], [("go", r)], r)
                    ACT(JUNK, GO[r], AF.Square, [("go", r)], ["junk", "ssq"], accum=SSQ[:, tt * 4 + g:tt * 4 + g + 1])
                    S.dma("sp", "go%d" % r, GCV[tt * 128:(tt + 1) * 128, g * 512:(g + 1) * 512], GO[r],
                          reads=[("go", r)], writes=[("gcv", tt)])
            S.barrier()
            if stop == "P2c":
                break

            RM.reset()
            RA.reset()
            GCUB = [RA.get([KC, 512], BF16) for _ in range(2)]
            CMB = [RA.get([KC, 512], BF16) for _ in range(2)]
            BSB = RA.get([D])
            CMG = RA.get([D])
            WST = RM.get([16, 128], BF16)
            GCVT = [RM.get([D], BF16) for _ in range(2)]
            VCM = [RM.get([D], BF16) for _ in range(2)]
            TMPX = [RM.get([512]) for _ in range(2)]
            SS = RM.get([24])
            S.op("dve", lambda e: e.tensor_reduce(out=SS, in_=SSQ.rearrange("p (t g) -> p t g", g=4),
                                                  axis=mybir.AxisListType.X, op=ALU.add), reads=["ssq"], writes=["ss"])
            ACT(RCV, SS, AF.Ln, ["ss"], ["rcv"], scale=1.0 / D, bias=EPS)
            ACT(RCV, RCV, AF.Exp, ["rcv"], ["rcv"], scale=-0.5)
            S.dma("pool", "wst", WST, wsT[l], writes=["wst"])
            S.dma("sp", "bsb", BSB, cm_bs[l][0:1, :].partition_broadcast(128), writes=["bsb"])
            S.dma("sp", "cmg", CMG, cm_g[l][0:1, :].partition_broadcast(128), writes=["cmg"])
            it = 0
            for tb in range(NTB):
                rb = tb % 2
                S.dma("sp", "gcub%d" % rb, GCUB[rb], GCU[:, tb * 512:(tb + 1) * 512].rearrange("(kc p) t -> p kc t", p=128),
                      reads=[("gcu", tb)], writes=[("gcub", rb)])
                for t4 in range(4):
                    tt = tb * 4 + t4
                    rt = tt % 2
                    S.dma("sp", "gcvt%d" % rt, GCVT[rt], GCV[tt * 128:(tt + 1) * 128, :], reads=[("gcv", tt)], writes=[("gcvt", rt)])
                    STT(VCM[rt], GCVT[rt], RCV[:, tt:tt + 1], CMG, ALU.mult, ALU.mult, [("gcvt", rt), "rcv", "cmg"], [("vcm", rt)])
                    for g4 in range(4):
                        r = it % 2
                        it += 1
                        px = 0 + r

                        def f(e, px=px, rt=rt, g4=g4):
                            ins = None
                            for gi in range(4):
                                g = g4 * 4 + gi
                                ins = e.matmul(PS[px][:, gi * 128:(gi + 1) * 128], lhsT=VCM[rt][:, g * 128:(g + 1) * 128],
                                               rhs=WST[:, g, :], start=True, stop=True)
                            return ins
                        S.op("pe", f, reads=[("vcm", rt), "wst"], writes=[psk(px)])
                        TT(TMPX[r], PS[px][:, :], BSB[:, g4 * 512:(g4 + 1) * 512], ALU.add, [psk(px), "bsb"], [("tmpx", r)])
                        TT(CMB[rb][:, g4 * 4:(g4 + 1) * 4, t4 * 128:(t4 + 1) * 128],
                           TMPX[r].rearrange("p (g q) -> p g q", g=4),
                           GCUB[rb][:, g4 * 4:(g4 + 1) * 4, t4 * 128:(t4 + 1) * 128], ALU.mult,
                           [("tmpx", r), ("gcub", rb)], [("cmb", rb)])
                S.dma("sp", "cmb%d" % rb, CM[:, tb * 512:(tb + 1) * 512].rearrange("(kc p) t -> p kc t", p=128), CMB[rb],
                      reads=[("cmb", rb)], writes=[("cm", tb)])
            S.barrier()
            if stop == "P2d":
                break

            RM.reset()
            RA.reset()
            KTS = RA.get([4, 2304], BF16)
            VSS = RA.get([18, 512], BF16)
            KTP = RA.get([4, 1024], BF16)
            VSP = RA.get([8, 512], BF16)
            QB = [RA.get([16, 512], BF16) for _ in range(2)]
            ATTB = [RM.get([16, 512], BF16) for _ in range(2)]
            PTL = [RM.get([512], BF16) for _ in range(4)]
            RD = [RM.get([512]) for _ in range(2)]
            S.dma("sp", "kts", KTS[:, :, 0:TSMP], KT[:, 0:TSMP].rearrange("(h d) t -> d h t", d=128),
                  reads=[("kt", tb) for tb in range(4)], writes=["kts"])
            S.dma("pool", "ktsc", KTS[:, :, TSMP:2304], kctx[l].rearrange("(h d) t -> d h t", d=128), writes=["kts"])
            S.dma("sp", "vss", VSS[:, 0:16, :], VS[0:TSMP, :].rearrange("(tt p) c -> p tt c", p=128),
                  reads=[("vs", tb) for tb in range(4)], writes=["vss"])
            S.dma("pool", "vssc", VSS[:, 16:18, :], vctx[l].rearrange("(tt p) c -> p tt c", p=128), writes=["vss"])
            S.dma("sp", "ktp", KTP, KT[:, TSMP:T].rearrange("(h d) t -> d h t", d=128),
                  reads=[("kt", 4), ("kt", 5)], writes=["ktp"])
            S.dma("sp", "vsp", VSP, VS[TSMP:T, :].rearrange("(tt p) c -> p tt c", p=128),
                  reads=[("vs", 4), ("vs", 5)], writes=["vsp"])
            ai = 0
            pi_ = 0
            for tb in range(NTB):
                rb = tb % 2
                S.dma("sp", "qb%d" % rb, QB[rb], QT[:, tb * 512:(tb + 1) * 512].rearrange("(h d) t -> d h t", d=128),
                      reads=[("qt", tb)], writes=[("qb", rb)])
                for hk in range(4):
                    for qs in range(4):
                        ra = ai % 2
                        ai += 1
                        po, pd = 4 + ra, 6 + ra
                        rhs = QB[rb][:, hk * 4:(hk + 1) * 4, qs * 128:(qs + 1) * 128]
                        if tb < 4:
                            tiles = [(KTS[:, hk, kt * 128:(kt + 1) * 128], VSS[:, kt, hk * 128:(hk + 1) * 128]) for kt in range(18)]
                            kv = ["kts", "vss"]
                        else:
                            seq = (tb - 4) * 2 + qs // 2
                            tiles = [(KTP[:, hk, seq * 256 + kt * 128:seq * 256 + (kt + 1) * 128],
                                      VSP[:, seq * 2 + kt, hk * 128:(hk + 1) * 128]) for kt in range(2)]
                            kv = ["ktp", "vsp"]
                        nk = len(tiles)
                        slots = []

                        def qk(i):
                            ps_i = pi_ % 4
                            MM(PS[ps_i][:, :], [(tiles[i][0], rhs)], [kv[0], ("qb", rb)], psk(ps_i))
                            return ps_i
                        pend = []
                        nxt = 0
                        while nxt < min(2, nk):
                            pend.append(qk(nxt))
                            pi_ += 1
                            nxt += 1
                        for kt in range(nk):
                            cur = pend.pop(0)
                            if nxt < nk:
                                pend.append(qk(nxt))
                                pi_ += 1
                                nxt += 1
                            pt = PTL[cur]
                            ACT(pt, PS[cur][:, :], AF.Exp, [psk(cur)], [("ptl", cur)], scale=SCALE)
                            st_, sp_ = (kt == 0), (kt == nk - 1)
                            S.op("pe", lambda e, po=po, pd=pd, v=tiles[kt][1], pt=pt, st_=st_, sp_=sp_: (
                                e.matmul(PS[po][:, :], lhsT=v, rhs=pt, start=st_, stop=sp_),
                                e.matmul(PS[pd][:, :], lhsT=ONES1, rhs=pt, start=st_, stop=sp_))[1],
                                reads=[kv[1], ("ptl", cur), "ones1"], writes=[psk(po), psk(pd)])
                        S.op("dve", lambda e, ra=ra, pd=pd: e.reciprocal(out=RD[ra], in_=PS[pd][:, :]), reads=[psk(pd)], writes=[("rd", ra)])
                        TT(ATTB[rb][:, hk * 4:(hk + 1) * 4, qs * 128:(qs + 1) * 128],
                           PS[po][:, :].rearrange("p (g q) -> p g q", g=4), RD[ra].rearrange("p (g q) -> p g q", g=4),
                           ALU.mult, [psk(po), ("rd", ra)], [("attb", rb)])
                S.dma("sp", "attb%d" % rb, ATT[:, tb * 512:(tb + 1) * 512].rearrange("(h d) t -> d h t", d=128), ATTB[rb],
                      reads=[("attb", rb)], writes=[("att", tb)])
            S.barrier()
            if stop == "P3":
                break

            RM.reset()
            RA.reset()
            A3 = [RA.get([KC, 768], BF16) for _ in range(3)]
            MRG = RA.get([KC, 768], BF16)
            ACC = RM.get([4, 768])
            GT = [RM.get([4, 768], BF16) for _ in range(2)]
            TM = [RM.get([384]) for _ in range(2)]
            OTL = [RM.get([384]) for _ in range(2)]
            srcs = [ATT, LRUO, CM]
            wouts = [w_attn_out, w_lru_out, w_cm_out]
            it = 0
            gi_ = 0
            allk = [("att", tb) for tb in range(NTB)] + [("lruo", n) for n in range(16)] + [("cm", tb) for tb in range(NTB)]

            def load_a3(tg_):
                for b_ in range(3):
                    S.dma("sp", "a3%d" % b_, A3[b_], srcs[b_][:, tg_ * 768:(tg_ + 1) * 768].rearrange("(kc p) t -> p kc t", p=128),
                          reads=allk, writes=[("a3", b_)])

            load_a3(0)
            for tg in range(4):
                c0 = tg * 768
                specs = []
                for cg in range(4):
                    for b in range(3):
                        specs.append([(wouts[b][l][:, cg * 512:(cg + 1) * 512], KC, 512)])
                for cg in range(4):
                    specs.append([(w_out[l][:, cg * 512:(cg + 1) * 512], KC, 512)])
                ws = WStream(specs)
                for cg in range(4):
                    for b in range(3):
                        wv, wk = ws.get(cg * 3 + b)
                        gr = gi_ % 2
                        gi_ += 1
                        r0 = b * D + cg * 512
                        S.dma("sp", "gt%d" % gr, GT[gr], GG[r0:r0 + 512, c0:c0 + 768].rearrange("(j p) t -> p j t", p=128),
                              reads=[("gg", tb) for tb in range(NTB)], writes=[("gt", gr)])
                        ACT(GT[gr], GT[gr], AF.Tanh, [("gt", gr)], [("gt", gr)], scale=0.5)
                        for sub in range(2):
                            ss = slice(sub * 384, (sub + 1) * 384)
                            for j in range(4):
                                r = it % 2
                                it += 1
                                pz = 0 + r
                                MM(PS[pz][:, 0:384], [(wv[:, kc, j * 128:(j + 1) * 128], A3[b][:, kc, ss]) for kc in range(KC)],
                                   [wk, ("a3", b)], psk(pz))
                                if b == 0:
                                    STT(ACC[:, j, ss], GT[gr][:, j, ss], 1.0, PS[pz][:, 0:384], ALU.add, ALU.mult,
                                        [("gt", gr), psk(pz)], [("acc", j, sub)])
                                else:
                                    STT(TM[r], GT[gr][:, j, ss], 1.0, PS[pz][:, 0:384], ALU.add, ALU.mult,
                                        [("gt", gr), psk(pz)], [("tm", r)])
                                    TT(ACC[:, j, ss], ACC[:, j, ss], TM[r], ALU.add, [("acc", j, sub), ("tm", r)], [("acc", j, sub)])
                                if b == 2:
                                    ACT(MRG[:, cg * 4 + j, ss], ACC[:, j, ss], AF.Copy, [("acc", j, sub)], ["mrg"])
                if tg + 1 < 4:
                    load_a3(tg + 1)
                for cg in range(4):
                    wv, wk = ws.get(12 + cg)
                    for sub in range(2):
                        ss = slice(sub * 384, (sub + 1) * 384)
                        for j in range(4):
                            r = it % 2
                            it += 1
                            pz = 0 + r
                            MM(PS[pz][:, 0:384], [(wv[:, kc, j * 128:(j + 1) * 128], MRG[:, kc, ss]) for kc in range(KC)],
                               [wk, "mrg"], psk(pz))
                            ACT(OTL[r], PS[pz][:, 0:384], AF.Identity, [psk(pz)], [("otl", r)], scale=0.5)
                            fc = cg * 4 + j
                            S.dma("sp", "otl%d" % r, OT[fc * 128:(fc + 1) * 128, c0 + sub * 384:c0 + (sub + 1) * 384], OTL[r],
                                  reads=[("otl", r)], writes=["ot"])
            S.barrier()
            if stop == "P4a":
                break

            RM.reset()
            RA.reset()
            OB = RA.get([KC, 512])
            SQ = RA.get([KC, 512], BF16)
            RS = RM.get([512])
            TMP = [RM.get([512]), RM.get([512])]
            H2O = RM.get([KC, 512], BF16)
            XBRS = [RA.get([KC, 512]), RM.get([KC, 512])]
            for tb in range(NTB):
                c = 0 if tb < 4 else 1
                XBr, xk = XBRS[tb % 2], ("xbr", tb % 2)
                S.dma("pool", "ob", OB, OT[:, tb * 512:(tb + 1) * 512].rearrange("(kc p) t -> p kc t", p=128), reads=["ot"], writes=["ob"])
                S.dma("pool", "xbr%d" % (tb % 2), XBr, XSRC[:, tb * 512:(tb + 1) * 512].rearrange("(kc p) t -> p kc t", p=128),
                      reads=xt_reads(tb), writes=[xk])
                norm_stats(OB, "ob")
                for kc in range(KC):
                    tmp, tk = TMP[kc % 2], ("tmp", kc % 2)
                    STT(tmp, OB[:, kc, :], G1P[:, kc, c:c + 1], RS, ALU.mult, ALU.mult, ["ob", ("g1p", c), "rs"], [tk])
                    TT(XBr[:, kc, :], XBr[:, kc, :], tmp, ALU.add, [xk, tk], [xk])
                S.dma("sp", "xst%d" % (tb % 2), yT[:, tb * 512:(tb + 1) * 512].rearrange("(kc p) t -> p kc t", p=128), XBr,
                      reads=[xk], writes=[("xt", tb)])
                norm_stats(XBr, xk)
                for kc in range(KC):
                    tmp, tk = TMP[kc % 2], ("tmp", kc % 2)
                    STT(tmp, XBr[:, kc, :], A2[:, kc, c:c + 1], RS, ALU.mult, ALU.mult, [xk, ("a2", c), "rs"], [tk])
                    ACT(H2O[:, kc, :], tmp, AF.Identity, [tk, ("mod", c)], ["h2o"], bias=B2[:, kc, c:c + 1])
                S.dma("sp", "h2o", H2T[:, tb * 512:(tb + 1) * 512].rearrange("(kc p) t -> p kc t", p=128), H2O,
                      reads=["h2o"], writes=["h2t"])
            S.barrier()
            if stop == "P4b":
                break

            RM.reset()
            RA.reset()
            F1 = RA.get([64, 768], BF16)
            H2G = RM.get([KC, 768], BF16)
            SQX = [RM.get([384]) for _ in range(2)]
            FO = [RM.get([384]) for _ in range(2)]
            it = 0
            for tg in range(4):
                c0 = tg * 768
                S.dma("sp", "h2g", H2G, H2T[:, c0:c0 + 768].rearrange("(kc p) t -> p kc t", p=128), reads=["h2t"], writes=["h2g"])
                specs = [[(w_ff1[l][:, cg * 512:(cg + 1) * 512], KC, 512)] for cg in range(16)]
                specs += [[(w_ff2[l][kh * 4096:(kh + 1) * 4096, og2 * 256:(og2 + 1) * 256], 32, 256)]
                          for og2 in range(8) for kh in range(2)]
                ws = WStream(specs)
                for cg in range(16):
                    wv, wk = ws.get(cg)
                    for sub in range(2):
                        ss = slice(sub * 384, (sub + 1) * 384)
                        for j in range(4):
                            r = it % 2
                            it += 1
                            pz = 0 + r
                            MM(PS[pz][:, 0:384], [(wv[:, kc, j * 128:(j + 1) * 128], H2G[:, kc, ss]) for kc in range(KC)],
                               [wk, "h2g"], psk(pz))
                            ACT(SQX[r], PS[pz][:, 0:384], AF.Square, [psk(pz)], [("sqx", r)])
                            STT(F1[:, cg * 4 + j, ss], PS[pz][:, 0:384], 0.0, SQX[r], ALU.is_gt, ALU.mult,
                                [psk(pz), ("sqx", r)], ["f1"])
                for og2 in range(8):
                    pb = 4 * (og2 % 2)
                    for kh in range(2):
                        wv, wk = ws.get(16 + og2 * 2 + kh)
                        for j in range(2):
                            for sub in range(2):
                                ss = slice(sub * 384, (sub + 1) * 384)
                                pz = pb + j * 2 + sub

                                def f(e, wv=wv, j=j, ss=ss, pz=pz, kh=kh):
                                    ins = None
                                    for kc in range(32):
                                        ins = e.matmul(PS[pz][:, 0:384], lhsT=wv[:, kc, j * 128:(j + 1) * 128],
                                                       rhs=F1[:, kh * 32 + kc, ss],
                                                       start=(kh == 0 and kc == 0), stop=(kh == 1 and kc == 31))
                                    return ins
                                S.op("pe", f, reads=[wk, "f1"], writes=[psk(pz)])
                    for j in range(2):
                        for sub in range(2):
                            pz = pb + j * 2 + sub
                            r = it % 2
                            it += 1
                            og = og2 * 2 + j
                            ACT(FO[r], PS[pz][:, 0:384], AF.Copy, [psk(pz)], [("fo", r)])
                            S.dma("sp", "fo%d" % r, FT[og * 128:(og + 1) * 128, c0 + sub * 384:c0 + (sub + 1) * 384], FO[r],
                                  reads=[("fo", r)], writes=["ft"])
            S.barrier()
            if stop == "P5":
                break

            RM.reset()
            RA.reset()
            OB = RA.get([KC, 512])
            SQ = RA.get([KC, 512], BF16)
            RS = RM.get([512])
            TMP = [RM.get([512]), RM.get([512])]
            XBRS = [RA.get([KC, 512]), RM.get([KC, 512])]
            for tb in range(NTB):
                c = 0 if tb < 4 else 1
                XBr, xk = XBRS[tb % 2], ("xbr", tb % 2)
                S.dma("pool", "ob", OB, FT[:, tb * 512:(tb + 1) * 512].rearrange("(kc p) t -> p kc t", p=128), reads=["ft"], writes=["ob"])
                S.dma("pool", "xbr%d" % (tb % 2), XBr, yT[:, tb * 512:(tb + 1) * 512].rearrange("(kc p) t -> p kc t", p=128),
                      reads=[("xt", tb)], writes=[xk])
                norm_stats(OB, "ob")
                for kc in range(KC):
                    tmp, tk = TMP[kc % 2], ("tmp", kc % 2)
                    STT(tmp, OB[:, kc, :], G2P[:, kc, c:c + 1], RS, ALU.mult, ALU.mult, ["ob", ("g2p", c), "rs"], [tk])
                    TT(XBr[:, kc, :], XBr[:, kc, :], tmp, ALU.add, [xk, tk], [xk])
                S.dma("sp", "xst%d" % (tb % 2), yT[:, tb * 512:(tb + 1) * 512].rearrange("(kc p) t -> p kc t", p=128), XBr,
                      reads=[xk], writes=[("xt", tb)])
            S.barrier()
            if stop == "P7":
                break

        S.dma("sp", "nsout", nsT[:, :], NS, reads=["ns"], writes=[])
        S.barrier()
        S.emit(blk)
    return nc


def _fm(v):
    v = np.asarray(v, np.float32)
    lead = v.shape[:-1]
    return np.ascontiguousarray(np.moveaxis(v.reshape(*lead, 16, 128), -1, 0))


def _host_tables():
    d = np.arange(128)
    f = d % 32
    axis = d // 64
    inv = (10000.0 ** (-np.arange(32, dtype=np.float32) / 32)).astype(np.float32)
    t = np.arange(TSMP)
    pos = np.stack([(t // 64).astype(np.float32), (t % 64).astype(np.float32)], 0)
    ang = pos[axis, :] * inv[f][:, None]
    cos = np.cos(ang).astype(np.float32)
    sin = np.sin(ang).astype(np.float32)
    isb = (d % 64) >= 32
    sgn = np.where(isb, 1.0, -1.0).astype(np.float32)
    sinS = sin * sgn[:, None]
    partner = np.where(isb, d - 32, d + 32)
    perm = np.zeros((128, 128), np.float32)
    perm[partner, d] = 1.0
    return cos, sinS, perm


def _prep(inputs):
    I = {k: np.asarray(v) for k, v in inputs.items()}
    cos, sinS, perm = _host_tables()
    pt = np.zeros((128, DEPTH, NPL), np.float32)
    for l in range(DEPTH):
        pt[:, l, P_BMOD:P_BMOD + 96] = I["b_mod"][l].reshape(96, 128).T
        pt[:, l, P_GPM:P_GPM + 16] = I["g_pre_mix"][l].reshape(16, 128).T
        pt[:, l, P_GPO:P_GPO + 16] = I["g_post_mix"][l].reshape(16, 128).T
        pt[:, l, P_GPF:P_GPF + 16] = I["g_pre_ff"][l].reshape(16, 128).T
        pt[:, l, P_GOF:P_GOF + 16] = I["g_post_ff"][l].reshape(16, 128).T
        pt[:, l, P_CW:P_CW + 64] = I["conv_w"][l].reshape(4, 16, 128).transpose(2, 0, 1).reshape(128, 64)
        pt[:, l, P_CB:P_CB + 16] = I["conv_b"][l].reshape(16, 128).T
        pt[:, l, P_BA:P_BA + 32] = I["lru_ba"][l].reshape(2, 16, 128).transpose(2, 0, 1).reshape(128, 32)
        pt[:, l, P_BX:P_BX + 32] = I["lru_bx"][l].reshape(2, 16, 128).transpose(2, 0, 1).reshape(128, 32)
        pt[:, l, P_LAM:P_LAM + 32] = I["lru_lam"][l].reshape(2, 16, 128).transpose(2, 0, 1).reshape(128, 32)
        pt[:, l, P_GQ] = I["g_q"][l]
        pt[:, l, P_GK] = I["g_k"][l]
    shared = {
        "ptab": np.ascontiguousarray(pt.reshape(128, DEPTH * NPL)),
        "cosT": cos, "sinT": sinS, "permd": perm,
        "w_mod": I["w_mod"], "w_in": I["w_in"], "w_attn_out": I["w_attn_out"], "w_lru_out": I["w_lru_out"],
        "w_cm_out": I["w_cm_out"], "w_out": I["w_out"], "w_ff1": I["w_ff1"], "w_ff2": I["w_ff2"],
        "lru_wa": I["lru_wa"], "lru_wx": I["lru_wx"],
        "wsT": np.ascontiguousarray(I["cm_ws"].transpose(0, 3, 1, 2)),
        "cm_bs": np.ascontiguousarray(I["cm_bs"].reshape(DEPTH, 1, D)),
        "cm_g": np.ascontiguousarray(I["cm_g"].reshape(DEPTH, 1, D)),
    }
    in_maps = []
    for i in range(8):
        xs = I["x_sample"][i]
        xp = I["x_prompt"][4 * i:4 * i + 4].reshape(1024, D)
        xT = np.ascontiguousarray(np.concatenate([xs, xp], 0).T)
        cond = np.stack([I["c"][i], I["c_ctx"]], -1)
        condT = np.ascontiguousarray(cond.reshape(16, 128, 2).transpose(1, 0, 2))
        kc_ = np.ascontiguousarray(I["cache_k"][i].transpose(0, 2, 3, 1).reshape(DEPTH, 512, 256))
        vc_ = np.ascontiguousarray(I["cache_v"][i].reshape(DEPTH, 256, 512))
        h0 = np.ascontiguousarray(I["state_lru"][i].reshape(DEPTH, 2, 16, 128).transpose(3, 0, 1, 2).reshape(128, DEPTH * 32))
        m = dict(shared)
        m.update({"xT0": xT, "condT": condT, "kctx": kc_, "vctx": vc_, "h0T": h0})
        in_maps.append(m)
    return in_maps


def _gather(results):
    y_p = np.zeros((32, 256, D), np.float32)
    y_s = np.zeros((8, 2048, D), np.float32)
    nk = np.zeros((32, DEPTH, 256, 4, 128), np.float32)
    nv = np.zeros((32, DEPTH, 256, 4, 128), np.float32)
    ns = np.zeros((32, DEPTH, 2, D), np.float32)
    for i, r in enumerate(results):
        y = np.asarray(r["yT"]).T
        y_s[i] = y[:2048]
        y_p[4 * i:4 * i + 4] = y[2048:].reshape(4, 256, D)
        k = np.asarray(r["knew"]).reshape(DEPTH, 4, 128, 4, 256)
        nk[4 * i:4 * i + 4] = k.transpose(3, 0, 4, 1, 2)
        v = np.asarray(r["vnew"]).reshape(DEPTH, 4, 256, 4, 128)
        nv[4 * i:4 * i + 4] = v.transpose(1, 0, 2, 3, 4)
        s = np.asarray(r["nsT"]).reshape(128, 4, DEPTH, 2, 16)
        ns[4 * i:4 * i + 4] = s.transpose(1, 2, 3, 4, 0).reshape(4, DEPTH, 2, D)
    return y_p, y_s, nk, nv, ns


def kernel(**inputs):
    in_maps = _prep(inputs)
    nc = build()
    res = run_bass_kernel_spmd(nc, in_maps, core_ids=list(range(8)))
    return _gather(res.results)
```

```python
import numpy as np
from contextlib import ExitStack
import concourse.bass as bass
import concourse.mybir as mybir
from concourse.bass_utils import run_bass_kernel_spmd

F32 = mybir.dt.float32
BF16 = mybir.dt.bfloat16
ALU = mybir.AluOpType
AF = mybir.ActivationFunctionType
ENGS = ("pe", "act", "dve", "pool", "sp")

D = 2048
KC = 16
T = 3072
TSMP = 2048
NTB = 6
DEPTH = 4
INW = 17408
DFF = 8192
OQ, OK_, OV, OLX, OLG, OCU, OCV, OG = 0, 2048, 2560, 3072, 5120, 7168, 9216, 11264
EPS = 1e-6
SCALE = 128 ** -0.5
GC1 = 0.044715
GC2 = 1.5957691216057308
P_BMOD, P_GPM, P_GPO, P_GPF, P_GOF, P_CW, P_CB, P_BA, P_BX, P_LAM, P_GQ, P_GK, NPL = (
    0, 96, 112, 128, 144, 160, 224, 240, 272, 304, 336, 337, 338)


class Chan:
    def __init__(self, sem):
        self.sem = sem
        self.cum = 0


class Sched:
    def __init__(self, nc, stack):
        self.nc = nc
        self.q = {e: [] for e in ENGS}
        self.cnt = {e: 0 for e in ENGS}
        self.esem = {e: stack.enter_context(nc.semaphore("prog_" + e)) for e in ENGS}
        self.seen = {e: {} for e in ENGS}
        self.lastw = {}
        self.readers = {}
        self.stack = stack
        self.chans = {}

    def chan(self, name):
        if name not in self.chans:
            self.chans[name] = Chan(self.stack.enter_context(self.nc.semaphore("ch_" + name)))
        return self.chans[name]

    def _need(self, eng, tok):
        sem, val, _ = tok
        k = id(sem)
        if self.seen[eng].get(k, 0) >= val:
            return
        self.seen[eng][k] = val
        self.q[eng].append(("wait", sem, val))

    def _deps(self, eng, reads, writes):
        for k in reads:
            t = self.lastw.get(k)
            if t is not None:
                self._need(eng, t)
        for k in writes:
            t = self.lastw.get(k)
            if t is not None and t[2] != eng:
                self._need(eng, t)
            for t in self.readers.get(k, {}).values():
                if t[2] != eng:
                    self._need(eng, t)

    def _commit(self, tok, reads, writes):
        for k in writes:
            self.lastw[k] = tok
            self.readers[k] = {}
        for k in reads:
            self.readers.setdefault(k, {})[id(tok[0])] = tok

    def op(self, eng, fn, reads=(), writes=()):
        self._deps(eng, reads, writes)
        self.cnt[eng] += 1
        tok = (self.esem[eng], self.cnt[eng], eng)
        self.q[eng].append(("op", fn))
        self._commit(tok, reads, writes)

    def dma(self, eng, chname, out, in_, reads=(), writes=()):
        ch = self.chan(chname)
        self._deps(eng, reads, writes)
        if ch.cum:
            self._need(eng, (ch.sem, ch.cum, None))
        ch.cum += 16
        tok = (ch.sem, ch.cum, None)
        self.q[eng].append(("dma", out, in_, ch.sem))
        self._commit(tok, reads, writes)

    def barrier(self):
        toks = [(self.esem[e], self.cnt[e], None) for e in ENGS if self.cnt[e]]
        toks += [(c.sem, c.cum, None) for c in self.chans.values() if c.cum]
        for e in ENGS:
            for t in toks:
                self._need(e, t)
        self.lastw = {}
        self.readers = {}

    def emit(self, block):
        engobj = {"pe": "tensor", "act": "scalar", "dve": "vector", "pool": "gpsimd", "sp": "sync"}

        def runner(e):
            def f(eng):
                sem = self.esem[e]
                for item in self.q[e]:
                    if item[0] == "wait":
                        eng.wait_ge(item[1], item[2])
                    elif item[0] == "op":
                        item[1](eng).then_inc(sem, 1)
                    else:
                        eng.dma_start(out=item[1], in_=item[2]).then_inc(item[3], 16)
            return f

        for e in ENGS:
            getattr(block, engobj[e])(runner(e))


def build(nlayers=DEPTH, debug=False, stop=None):
    nc = bass.Bass("TRN2", target_bir_lowering=False)
    st = ExitStack()
    with st:
        def din(name, shape, dt=F32):
            return nc.dram_tensor(name, list(shape), dt, kind="ExternalInput").ap()

        def dout(name, shape, dt=F32):
            return nc.dram_tensor(name, list(shape), dt, kind="ExternalOutput").ap()

        def dscr(name, shape, dt):
            kind = "ExternalOutput" if debug else "Internal"
            return nc.dram_tensor(name, list(shape), dt, kind=kind).ap()

        xT0 = din("xT0", [D, T])
        condT = din("condT", [128, KC, 2])
        kctx = din("kctx", [DEPTH, 512, 256])
        vctx = din("vctx", [DEPTH, 256, 512])
        h0T = din("h0T", [128, DEPTH * 2 * 16])
        ptab = din("ptab", [128, DEPTH * NPL])
        cosT = din("cosT", [128, TSMP])
        sinT = din("sinT", [128, TSMP])
        permd = din("permd", [128, 128])
        w_mod = din("w_mod", [nlayers, D, 6 * D])
        w_in = din("w_in", [nlayers, D, INW])
        w_attn_out = din("w_attn_out", [nlayers, D, D])
        w_lru_out = din("w_lru_out", [nlayers, D, D])
        w_cm_out = din("w_cm_out", [nlayers, D, D])
        w_out = din("w_out", [nlayers, D, D])
        w_ff1 = din("w_ff1", [nlayers, D, DFF])
        w_ff2 = din("w_ff2", [nlayers, DFF, D])
        lru_wa = din("lru_wa", [nlayers, 2, 16, 128, 128])
        lru_wx = din("lru_wx", [nlayers, 2, 16, 128, 128])
        wsT = din("wsT", [nlayers, 128, 16, 128])
        cm_bs = din("cm_bs", [nlayers, 1, D])
        cm_g = din("cm_g", [nlayers, 1, D])

        yT = dout("yT", [D, T])
        knew = dout("knew", [DEPTH, 512, 1024])
        vnew = dout("vnew", [DEPTH, 1024, 512])
        nsT = dout("nsT", [128, 4 * DEPTH * 2 * 16])

        QT = dscr("QT", [D, T], BF16)
        KT = dscr("KT", [512, T], BF16)
        VS = dscr("VS", [T, 512], BF16)
        LRUO = dscr("LRUO", [D, T], BF16)
        GCU = dscr("GCU", [D, T], BF16)
        GCV = dscr("GCV", [T, D], BF16)
        CM = dscr("CM", [D, T], BF16)
        GG = dscr("GG", [3 * D, T], BF16)
        ATT = dscr("ATT", [D, T], BF16)
        OT = dscr("OT", [D, T], F32)
        H2T = dscr("H2T", [D, T], BF16)
        FT = dscr("FT", [D, T], F32)

        S = Sched(nc, st)
        ARENA_W = 53000
        arena = st.enter_context(nc.sbuf_tensor("arena", [128, ARENA_W], F32))
        PS = [st.enter_context(nc.psum_tensor("ps%d" % i, [128, 512], F32)) for i in range(8)]

        def carve(off, nbytes, dt, pat=None, **kw):
            assert off % 4 == 0 and nbytes % 4 == 0 and off + nbytes <= ARENA_W * 4, (off, nbytes)
            a = arena[:, off // 4:(off + nbytes) // 4]
            if dt != F32:
                a = a.bitcast(dt)
            if pat is not None:
                a = a.rearrange(pat, **kw)
            return a

        class Region:
            def __init__(self, base, size):
                self.base, self.size, self.cur = base, size, 0

            def reset(self):
                self.cur = 0

            def get(self, shape, dt=F32):
                n = 1
                for s_ in shape:
                    n *= s_
                nb = n * (4 if dt == F32 else 2)
                nb = (nb + 31) // 32 * 32
                assert self.cur + nb <= self.size, (self.cur, nb, self.size)
                off = self.base + self.cur
                self.cur += nb
                if len(shape) == 1:
                    return carve(off, nb, dt)[:, 0:shape[0]]
                if len(shape) == 2:
                    return carve(off, nb, dt)[:, 0:n].rearrange("p (a b) -> p a b", a=shape[0])
                return carve(off, nb, dt)[:, 0:n].rearrange("p (a b c) -> p a b c", a=shape[0], b=shape[1])

        RC = Region(0, 12288)
        RA = Region(12288, 98304)
        RW = Region(12288 + 98304, 32768)
        RM = Region(12288 + 98304 + 32768, ARENA_W * 4 - (12288 + 98304 + 32768))

        ONES11 = RC.get([128], BF16)
        ONES7 = RC.get([128], BF16)
        ONES1 = RC.get([128], BF16)
        PERM = RC.get([128], BF16)
        PTt = RC.get([DEPTH * NPL])
        CONDB = RC.get([KC, 2], BF16)
        MOD = RC.get([96, 2])
        A1 = RC.get([16, 2]); B1 = MOD[:, 0:16, :]
        G1P = RC.get([16, 2])
        A2 = RC.get([16, 2]); B2 = MOD[:, 48:64, :]
        G2P = RC.get([16, 2])
        NBA = RC.get([32]); NBX = RC.get([32]); CC = RC.get([32]); C2 = RC.get([32])
        NS = RC.get([4 * DEPTH * 2 * 16])
        H0 = RC.get([DEPTH * 2 * 16])
        SSQ = RC.get([96])
        RCV = RC.get([24])
        CTMP = RC.get([KC, 2])

        blk = st.enter_context(nc.Block())

        def ACT(out, in_, func, reads, writes, bias=None, scale=None, accum=None):
            kw = {}
            if bias is not None:
                kw["bias"] = bias
            if scale is not None:
                kw["scale"] = scale
            if accum is not None:
                kw["accum_out"] = accum
            S.op("act", lambda e: e.activation(out=out, in_=in_, func=func, **kw), reads=reads, writes=writes)

        def TSC(out, in0, s1, s2, op0, op1, reads, writes, eng="dve"):
            if op1 is None:
                S.op(eng, lambda e: e.tensor_scalar(out=out, in0=in0, scalar1=s1, scalar2=None, op0=op0), reads=reads, writes=writes)
            else:
                S.op(eng, lambda e: e.tensor_scalar(out=out, in0=in0, scalar1=s1, scalar2=s2, op0=op0, op1=op1), reads=reads, writes=writes)

        def TT(out, in0, in1, op, reads, writes, eng="dve"):
            S.op(eng, lambda e: e.tensor_tensor(out=out, in0=in0, in1=in1, op=op), reads=reads, writes=writes)

        def STT(out, in0, scalar, in1, op0, op1, reads, writes):
            S.op("dve", lambda e: e.scalar_tensor_tensor(out=out, in0=in0, scalar=scalar, in1=in1, op0=op0, op1=op1), reads=reads, writes=writes)

        def CP(out, in_, reads, writes, eng="dve"):
            S.op(eng, lambda e: e.tensor_copy(out=out, in_=in_), reads=reads, writes=writes)

        def MM(ps_ap, pairs, reads, pskey):
            def f(e):
                n = len(pairs)
                ins = None
                for i, (l, r) in enumerate(pairs):
                    ins = e.matmul(ps_ap, lhsT=l, rhs=r, start=(i == 0), stop=(i == n - 1))
                return ins
            S.op("pe", f, reads=reads, writes=[pskey])

        def MSET(ap, val, writes, eng="dve"):
            S.op(eng, lambda e: e.memset(ap, val), writes=writes)

        WSLOT = [RW.get([8192], BF16), RW.get([8192], BF16)]
        wctr = [0]

        class WStream:
            def __init__(self, specs, off=0, tag="w"):
                self.specs = specs
                self.issued = 0
                self.info = {}
                self.off, self.tag = off, tag
                self.ctr = wctr if tag == "w" else [0]

            def _issue(self, i):
                slot = self.ctr[0] % 2
                self.ctr[0] += 1
                parts = self.specs[i]
                kcn = parts[0][1]
                ntot = sum(p[2] for p in parts)
                assert self.off + kcn * ntot <= 8192
                view = WSLOT[slot][:, self.off:self.off + kcn * ntot].rearrange("p (kc n) -> p kc n", kc=kcn)
                c0 = 0
                for (src, kcn_, n) in parts:
                    srcv = src.rearrange("(kc p) n -> p kc n", p=128)
                    for k0 in range(0, kcn_, 16):
                        S.dma("pool", "%s%d" % (self.tag, slot), view[:, k0:k0 + 16, c0:c0 + n],
                              srcv[:, k0:k0 + 16, :], writes=[(self.tag, slot)])
                    c0 += n
                self.info[i] = (view, (self.tag, slot))

            def get(self, i):
                while self.issued <= min(i + 1, len(self.specs) - 1):
                    self._issue(self.issued)
                    self.issued += 1
                return self.info[i]

        def psk(i):
            return "ps%d" % i

        S.dma("sp", "c0", PTt, ptab[:, :], writes=["pt"])
        S.dma("sp", "c1", H0, h0T[:, :], writes=["h0"])
        CONDF = RM.get([KC, 2])
        S.dma("sp", "c2", CONDF, condT[:, :, :], writes=["condf"])
        S.dma("pool", "c3", PERM, permd[:, :], writes=["perm"])
        MSET(ONES11, 2.0 ** -11, ["ones11"])
        MSET(ONES7, 2.0 ** -7, ["ones7"])
        MSET(ONES1, 1.0, ["ones1"])
        MSET(NS, 0.0, ["ns"])
        ACT(CTMP, CONDF, AF.Exp, ["condf"], ["ctmp"], scale=-1.0)
        ACT(CTMP, CTMP, AF.Ln, ["ctmp"], ["ctmp"], bias=1.0)
        ACT(CTMP, CTMP, AF.Exp, ["ctmp"], ["ctmp"], scale=-1.0)
        TT(CONDB, CTMP, CONDF, ALU.mult, ["ctmp", "condf"], ["condb"])
        S.barrier()

        for l in range(nlayers):
            pc = l * NPL
            XSRC = xT0 if l == 0 else yT

            def xt_reads(tb, l=l):
                return [] if l == 0 else [("xt", tb)]

            RM.reset()
            ws = WStream([[(w_mod[l][:, cg * 512:(cg + 1) * 512], KC, 512)] for cg in range(24)])
            PSM = PS[0][:, 0:192]
            for cg in range(24):
                wv, wk = ws.get(cg)

                def f(e, wv=wv, cg=cg):
                    ins = None
                    for j in range(4):
                        c0 = (cg * 4 + j) * 2
                        for kc in range(KC):
                            ins = e.matmul(PSM[:, c0:c0 + 2], lhsT=wv[:, kc, j * 128:(j + 1) * 128],
                                           rhs=CONDB[:, kc, :], start=(kc == 0), stop=(kc == KC - 1))
                    return ins
                S.op("pe", f, reads=[wk, "condb"], writes=[psk(0)])
            PSMv = PSM.rearrange("p (a c) -> p a c", c=2)
            for c in range(2):
                TT(MOD[:, :, c], PSMv[:, :, c], PTt[:, pc + P_BMOD:pc + P_BMOD + 96], ALU.add,
                   [psk(0), "pt"], [("mod", c)])
            for c in range(2):
                STT(A1[:, :, c], MOD[:, 16:32, c], 1.0, PTt[:, pc + P_GPM:pc + P_GPM + 16], ALU.add, ALU.mult,
                    [("mod", c), "pt"], [("a1", c)])
                TT(G1P[:, :, c], MOD[:, 32:48, c], PTt[:, pc + P_GPO:pc + P_GPO + 16], ALU.mult,
                   [("mod", c), "pt"], [("g1p", c)])
                STT(A2[:, :, c], MOD[:, 64:80, c], 1.0, PTt[:, pc + P_GPF:pc + P_GPF + 16], ALU.add, ALU.mult,
                    [("mod", c), "pt"], [("a2", c)])
                TT(G2P[:, :, c], MOD[:, 80:96, c], PTt[:, pc + P_GOF:pc + P_GOF + 16], ALU.mult,
                   [("mod", c), "pt"], [("g2p", c)])
            TSC(NBA, PTt[:, pc + P_BA:pc + P_BA + 32], -1.0, None, ALU.mult, None, ["pt"], ["nba"])
            TSC(NBX, PTt[:, pc + P_BX:pc + P_BX + 32], -1.0, None, ALU.mult, None, ["pt"], ["nbx"])
            ACT(CC, PTt[:, pc + P_LAM:pc + P_LAM + 32], AF.Exp, ["pt"], ["cc"], scale=-1.0)
            ACT(CC, CC, AF.Ln, ["cc"], ["cc"], bias=1.0)
            TSC(C2, CC, -16.0, None, ALU.mult, None, ["cc"], ["c2"])
            TSC(CC, CC, -8.0, None, ALU.mult, None, ["cc"], ["cc"])
            S.barrier()
            if stop == "P0":
                break

            RM.reset()
            RA.reset()
            HALL = RA.get([KC, T], BF16)
            XB = [RM.get([KC, 512]), carve(RW.base, 32768, F32, "p (a b) -> p a b", a=KC)]
            SQ = RM.get([KC, 512], BF16)
            RS = RM.get([512])
            TMP = [RM.get([512]), RM.get([512])]

            def norm_stats(src, srckey, sqkey="sq", rskey="rs", psi=1):
                ACT(SQ, src, AF.Square, [srckey], [sqkey])
                MM(PS[psi][:, :], [(ONES11, SQ[:, kc, :]) for kc in range(KC)], ["ones11", sqkey], psk(psi))
                ACT(RS, PS[psi][:, :], AF.Ln, [psk(psi)], [rskey], bias=EPS)
                ACT(RS, RS, AF.Exp, [rskey], [rskey], scale=-0.5)

            for tb in range(NTB):
                c = 0 if tb < 4 else 1
                xb = XB[tb % 2]
                xk = ("xb", tb % 2)
                S.dma("sp", "xb%d" % (tb % 2), xb, XSRC[:, tb * 512:(tb + 1) * 512].rearrange("(kc p) t -> p kc t", p=128),
                      reads=xt_reads(tb), writes=[xk])
                norm_stats(xb, xk)
                for kc in range(KC):
                    tmp = TMP[kc % 2]
                    tk = ("tmp", kc % 2)
                    STT(tmp, xb[:, kc, :], A1[:, kc, c:c + 1], RS, ALU.mult, ALU.mult, [xk, ("a1", c), "rs"], [tk])
                    ACT(HALL[:, kc, tb * 512:(tb + 1) * 512], tmp, AF.Identity, [tk, ("mod", c)], [("hall", tb)],
                        bias=B1[:, kc, c:c + 1])
            S.barrier()
            if stop == "P1":
                break

            RM.reset()
            COS = RM.get([TSMP]); SIN = RM.get([TSMP])
            S.dma("sp", "cos", COS, cosT[:, :], writes=["cos"])
            S.dma("sp", "sin", SIN, sinT[:, :], writes=["sin"])
            SQ1 = [RM.get([512], BF16) for _ in range(2)]
            RS1 = [RM.get([512]) for _ in range(2)]
            QN = [RM.get([512]) for _ in range(2)]
            QNB = [RM.get([512], BF16) for _ in range(2)]
            T1 = [RM.get([512]) for _ in range(2)]
            T2 = [RM.get([512]) for _ in range(2)]
            QO = [RM.get([512], BF16) for _ in range(2)]
            VTb = [RM.get([512], BF16) for _ in range(2)]
            VF = [RM.get([512]) for _ in range(2)]
            specs = [[(w_in[l][:, OQ + g * 512:OQ + (g + 1) * 512], KC, 512)] for g in range(4)]
            specs.append([(w_in[l][:, OK_:OK_ + 512], KC, 512)])
            specs.append([(w_in[l][:, OV:OV + 512], KC, 512)])
            ws = WStream(specs)
            it = 0
            lim = stop.split(":")[1] if (stop and stop.startswith("P2a:")) else None
            units = [(g, tb, j) for g in range(5) for tb in range(NTB) for j in range(4)]

            def qk_main(idx):
                g, tb, j = units[idx]
                wv, wk = ws.get(g)
                pz = 2 + idx % 2
                MM(PS[pz][:, :], [(wv[:, kc, j * 128:(j + 1) * 128], HALL[:, kc, tb * 512:(tb + 1) * 512]) for kc in range(KC)],
                   [wk, ("hall", tb)], psk(pz))

            def qk_rest(idx):
                g, tb, j = units[idx]
                isk = (g == 4)
                gcol = pc + (P_GK if isk else P_GQ)
                r = idx % 2
                pz, psn, pw = 2 + r, 4 + r, 6 + r
                ACT(SQ1[r], PS[pz][:, :], AF.Square, [psk(pz)], [("sq1", r)])
                MM(PS[psn][:, :], [(ONES7, SQ1[r])], ["ones7", ("sq1", r)], psk(psn))
                ACT(RS1[r], PS[psn][:, :], AF.Ln, [psk(psn)], [("rs1", r)], bias=EPS)
                ACT(RS1[r], RS1[r], AF.Exp, [("rs1", r)], [("rs1", r)], scale=-0.5)
                STT(QN[r], PS[pz][:, :], PTt[:, gcol:gcol + 1], RS1[r], ALU.mult, ALU.mult,
                    [psk(pz), "pt", ("rs1", r)], [("qn", r)])
                if tb < 4:
                    ACT(QNB[r], QN[r], AF.Copy, [("qn", r)], [("qnb", r)])
                    MM(PS[pw][:, :], [(PERM, QNB[r])], ["perm", ("qnb", r)], psk(pw))
                    TT(T1[r], QN[r], COS[:, tb * 512:(tb + 1) * 512], ALU.mult, [("qn", r), "cos"], [("t1", r)])
                    TT(T2[r], PS[pw][:, :], SIN[:, tb * 512:(tb + 1) * 512], ALU.mult, [psk(pw), "sin"], [("t2", r)])
                    TT(QO[r], T1[r], T2[r], ALU.add, [("t1", r), ("t2", r)], [("qo", r)])
                else:
                    ACT(QO[r], QN[r], AF.Copy, [("qn", r)], [("qo", r)])
                if isk:
                    S.dma("sp", "qo%d" % r, KT[j * 128:(j + 1) * 128, tb * 512:(tb + 1) * 512], QO[r],
                          reads=[("qo", r)], writes=[("kt", tb)])
                    if tb >= 4:
                        S.dma("sp", "kf%d" % r, knew[l][j * 128:(j + 1) * 128, (tb - 4) * 512:(tb - 3) * 512], QN[r],
                              reads=[("qn", r)], writes=[])
                else:
                    h = g * 4 + j
                    S.dma("sp", "qo%d" % r, QT[h * 128:(h + 1) * 128, tb * 512:(tb + 1) * 512], QO[r],
                          reads=[("qo", r)], writes=[("qt", tb)])

            qk_main(0)
            for idx in range(len(units)):
                if idx + 1 < len(units):
                    qk_main(idx + 1)
                qk_rest(idx)
            wv, wk = ws.get(5) if lim is None else (None, None)
            for tt in range(24 if lim is None else 0):
                r = tt % 2
                pv = 0 + r
                MM(PS[pv][:, :], [(HALL[:, kc, tt * 128:(tt + 1) * 128], wv[:, kc, :]) for kc in range(KC)],
                   [wk, ("hall", tt // 4)], psk(pv))
                ACT(VTb[r], PS[pv][:, :], AF.Copy, [psk(pv)], [("vt", r)])
                S.dma("sp", "vt%d" % r, VS[tt * 128:(tt + 1) * 128, :], VTb[r], reads=[("vt", r)], writes=[("vs", tt // 4)])
                if tt >= 16:
                    ACT(VF[r], PS[pv][:, :], AF.Copy, [psk(pv)], [("vf", r)])
                    S.dma("sp", "vf%d" % r, vnew[l][(tt - 16) * 128:(tt - 15) * 128, :], VF[r], reads=[("vf", r)], writes=[])
            S.barrier()
            if stop and stop.startswith("P2a"):
                break

            RM.reset()
            XPS = RM.get([2052])
            XPP = RM.get([4, 259])
            GLG = RM.get([T], BF16)
            XC = RM.get([TSMP])
            XCB = RM.get([TSMP], BF16)
            HF = RM.get([TSMP])
            G1 = RM.get([512]); AA = RM.get([512]); G2 = RM.get([512])
            HBB = [RM.get([512]), RM.get([512])]
            SM = RM.get([512])
            GX2 = [RM.get([512]) for _ in range(2)]
            GXS = [RM.get([512]) for _ in range(2)]
            GW = [RM.get([512]) for _ in range(2)]
            LW = [RM.get([4, 128], BF16) for _ in range(2)]
            MSET(XPS, 0.0, ["xp"])
            MSET(XPP, 0.0, ["xp"])
            ws = WStream([[(w_in[l][:, OLX + n * 128:OLX + (n + 1) * 128], KC, 128),
                           (w_in[l][:, OLG + n * 128:OLG + (n + 1) * 128], KC, 128)] for n in range(16)])

            GOZ = [RM.get([512], BF16) for _ in range(2)]
            wsg = WStream([[(w_in[l][:, OG + s_ * 256:OG + (s_ + 1) * 256], KC, 256)] for s_ in range(24)],
                          off=4096, tag="wg")

            def gg_units():
                itg = 0
                for s_ in range(24):
                    wvg, wkg = wsg.get(s_)
                    for tbg in range(NTB):
                        for jg in range(2):
                            rg = itg % 2
                            itg += 1
                            pzg = 6 + rg
                            MM(PS[pzg][:, :], [(wvg[:, kc, jg * 128:(jg + 1) * 128], HALL[:, kc, tbg * 512:(tbg + 1) * 512])
                                               for kc in range(KC)], [wkg, ("hall", tbg)], psk(pzg))
                            CP(GOZ[rg], PS[pzg][:, :], [psk(pzg)], [("goz", rg)])
                            fcg = s_ * 2 + jg
                            S.dma("sp", "goz%d" % rg, GG[fcg * 128:(fcg + 1) * 128, tbg * 512:(tbg + 1) * 512], GOZ[rg],
                                  reads=[("goz", rg)], writes=[("gg", tbg)])
                            yield

            ggen = gg_units()

            def gg_step():
                next(ggen, None)

            def gelu6(ps_ap, pkey, out_ap, outkeys, r):
                x2, xs, w_ = GX2[r], GXS[r], GW[r]
                ACT(x2, ps_ap, AF.Square, [pkey], [("gx2", r)])
                ACT(xs, ps_ap, AF.Copy, [pkey], [("gxs", r)])
                TSC(w_, x2, GC1, 1.0, ALU.mult, ALU.add, [("gx2", r)], [("gw", r)])
                TT(w_, w_, xs, ALU.mult, [("gw", r), ("gxs", r)], [("gw", r)])
                ACT(x2, w_, AF.Exp, [("gw", r)], [("gx2", r)], scale=-GC2)
                ACT(x2, x2, AF.Ln, [("gx2", r)], [("gx2", r)], bias=1.0)
                ACT(x2, x2, AF.Exp, [("gx2", r)], [("gx2", r)], scale=-1.0)
                TT(out_ap, x2, xs, ALU.mult, [("gx2", r), ("gxs", r)], outkeys)

            def lru_gates(n, d, xcb_blk, xc_blk, lw, lwk):
                idx = d * 16 + n
                MM(PS[2][:, :], [(lw[:, d, :], xcb_blk)], [lwk, "xcb"], psk(2))
                MM(PS[3][:, :], [(lw[:, 2 + d, :], xcb_blk)], [lwk, "xcb"], psk(3))
                ACT(G1, PS[2][:, :], AF.Exp, [psk(2), "nba"], ["g1"], scale=-1.0, bias=NBA[:, idx:idx + 1])
                ACT(G2, PS[3][:, :], AF.Exp, [psk(3), "nbx"], ["g2"], scale=-1.0, bias=NBX[:, idx:idx + 1])
                ACT(G1, G1, AF.Ln, ["g1"], ["g1"], bias=1.0)
                ACT(G2, G2, AF.Ln, ["g2"], ["g2"], bias=1.0)
                ACT(G1, G1, AF.Exp, ["g1"], ["g1"], scale=-1.0)
                ACT(G2, G2, AF.Exp, ["g2"], ["g2"], scale=-1.0)
                ACT(AA, G1, AF.Exp, ["g1", "cc"], ["aa"], scale=CC[:, idx:idx + 1])
                TT(G2, G2, xc_blk, ALU.mult, ["g2", "xc"], ["g2"])
                ACT(G1, G1, AF.Exp, ["g1", "c2"], ["g1"], scale=C2[:, idx:idx + 1])
                ACT(G1, G1, AF.Ln, ["g1"], ["g1"], scale=-0.9999999, bias=1.0)
                ACT(G1, G1, AF.Exp, ["g1"], ["g1"], scale=0.5)
                TT(G2, G1, G2, ALU.mult, ["g1", "g2"], ["g2"])

            def conv(dst, src_at, n):
                cw = pc + P_CW
                TSC(dst, src_at(0), PTt[:, cw + n:cw + n + 1], PTt[:, pc + P_CB + n:pc + P_CB + n + 1],
                    ALU.mult, ALU.add, ["xp", "pt"], ["xc"])
                for j in range(1, 4):
                    STT(dst, src_at(j), PTt[:, cw + j * 16 + n:cw + j * 16 + n + 1], dst, ALU.mult, ALU.add,
                        ["xp", "pt", "xc"], ["xc"])

            git = 0
            for n in range(16):
                wv, wk = ws.get(n)
                lw = LW[n % 2]
                lwk = ("lw", n % 2)
                S.dma("pool", "lwa%d" % (n % 2), lw[:, 0:2, :], lru_wa[l][:, n].rearrange("d c e -> c d e"), writes=[lwk])
                S.dma("pool", "lwx%d" % (n % 2), lw[:, 2:4, :], lru_wx[l][:, n].rearrange("d c e -> c d e"), writes=[lwk])
                for tb in range(NTB):
                    r = git % 2
                    git += 1
                    px, pg = 0 + r, 4 + r
                    MM(PS[px][:, :], [(wv[:, kc, 0:128], HALL[:, kc, tb * 512:(tb + 1) * 512]) for kc in range(KC)],
                       [wk, ("hall", tb)], psk(px))
                    MM(PS[pg][:, :], [(wv[:, kc, 128:256], HALL[:, kc, tb * 512:(tb + 1) * 512]) for kc in range(KC)],
                       [wk, ("hall", tb)], psk(pg))
                    if tb < 4:
                        ACT(XPS[:, 2 + tb * 512:2 + (tb + 1) * 512], PS[px][:, :], AF.Copy, [psk(px)], ["xp"])
                    else:
                        ACT(XPP[:, (tb - 4) * 2:(tb - 3) * 2, 2:258], PS[px][:, :].rearrange("p (s t) -> p s t", s=2),
                            AF.Copy, [psk(px)], ["xp"])
                    gelu6(PS[pg][:, :], psk(pg), GLG[:, tb * 512:(tb + 1) * 512], ["glg"], r)
                    gg_step()
                conv(XC, lambda j: XPS[:, j:j + TSMP], n)
                ACT(XCB, XC, AF.Copy, ["xc"], ["xcb"])
                for tb in range(4):
                    sl = slice(tb * 512, (tb + 1) * 512)
                    lru_gates(n, 0, XCB[:, sl], XC[:, sl], lw, lwk)
                    gg_step()
                    init = H0[:, (l * 2 + 0) * 16 + n:(l * 2 + 0) * 16 + n + 1] if tb == 0 else HF[:, tb * 512 - 1:tb * 512]
                    S.op("dve", lambda e, sl=sl, init=init: e.tensor_tensor_scan(
                        out=HF[:, sl], data0=AA, data1=G2, initial=init, op0=ALU.mult, op1=ALU.add),
                        reads=["aa", "g2", "hf", "h0"], writes=["hf"])
                for tb in range(3, -1, -1):
                    sl = slice(tb * 512, (tb + 1) * 512)
                    hb = HBB[tb % 2]
                    lru_gates(n, 1, XCB[:, sl], XC[:, sl], lw, lwk)
                    gg_step()
                    init = H0[:, (l * 2 + 1) * 16 + n:(l * 2 + 1) * 16 + n + 1] if tb == 3 else HBB[(tb + 1) % 2][:, 0:1]
                    S.op("dve", lambda e, hb=hb, init=init: e.tensor_tensor_scan(
                        out=hb[:, ::-1], data0=AA[:, ::-1], data1=G2[:, ::-1], initial=init, op0=ALU.mult, op1=ALU.add),
                        reads=["aa", "g2", ("hbb", (tb + 1) % 2), "h0"], writes=[("hbb", tb % 2)])
                    TT(SM, hb, HF[:, sl], ALU.add, [("hbb", tb % 2), "hf"], ["sm"])
                    TT(GLG[:, sl], SM, GLG[:, sl], ALU.mult, ["sm", "glg"], ["glg"])
                S.dma("sp", "lruo", LRUO[n * 128:(n + 1) * 128, 0:TSMP], GLG[:, 0:TSMP], reads=["glg"], writes=[("lruo", n)])
                XCp = XC[:, 0:1024].rearrange("p (s t) -> p s t", s=4)
                conv(XCp, lambda j: XPP[:, :, j:j + 256], n)
                ACT(XCB[:, 0:1024], XC[:, 0:1024], AF.Copy, ["xc"], ["xcb"])
                for tb in range(2):
                    sl = slice(tb * 512, (tb + 1) * 512)
                    lru_gates(n, 0, XCB[:, sl], XC[:, sl], lw, lwk)
                    gg_step()
                    for s2 in range(2):
                        ss = slice(tb * 512 + s2 * 256, tb * 512 + (s2 + 1) * 256)
                        sb = slice(s2 * 256, (s2 + 1) * 256)
                        S.op("dve", lambda e, ss=ss, sb=sb: e.tensor_tensor_scan(
                            out=HF[:, ss], data0=AA[:, sb], data1=G2[:, sb], initial=0.0, op0=ALU.mult, op1=ALU.add),
                            reads=["aa", "g2"], writes=["hf"])
                for tb in range(2):
                    sl = slice(tb * 512, (tb + 1) * 512)
                    hb = HBB[tb % 2]
                    lru_gates(n, 1, XCB[:, sl], XC[:, sl], lw, lwk)
                    gg_step()
                    for s2 in range(2):
                        sb = slice(s2 * 256, (s2 + 1) * 256)
                        S.op("dve", lambda e, hb=hb, sb=sb: e.tensor_tensor_scan(
                            out=hb[:, sb][:, ::-1], data0=AA[:, sb][:, ::-1], data1=G2[:, sb][:, ::-1], initial=0.0,
                            op0=ALU.mult, op1=ALU.add),
                            reads=["aa", "g2"], writes=[("hbb", tb % 2)])
                    for s2 in range(2):
                        seq = tb * 2 + s2
                        o = ((seq * DEPTH + l) * 2) * 16 + n
                        CP(NS[:, o:o + 1], HF[:, seq * 256 + 255:seq * 256 + 256], ["hf"], ["ns"])
                        CP(NS[:, o + 16:o + 17], hb[:, s2 * 256:s2 * 256 + 1], [("hbb", tb % 2)], ["ns"])
                    TT(SM, hb, HF[:, sl], ALU.add, [("hbb", tb % 2), "hf"], ["sm"])
                    gs = slice(TSMP + tb * 512, TSMP + (tb + 1) * 512)
                    TT(GLG[:, gs], SM, GLG[:, gs], ALU.mult, ["sm", "glg"], ["glg"])
                S.dma("sp", "lruo2", LRUO[n * 128:(n + 1) * 128, TSMP:T], GLG[:, TSMP:T], reads=["glg"], writes=[("lruo", n)])
            for _ in ggen:
                pass
            S.barrier()
            if stop == "P2b":
                break

            RM.reset()
            GX2 = [RM.get([512]) for _ in range(2)]
            GXS = [RM.get([512]) for _ in range(2)]
            GW = [RM.get([512]) for _ in range(2)]
            GO = [RM.get([512], BF16) for _ in range(2)]
            JUNK = RM.get([512], BF16)
            MSET(SSQ, 0.0, ["ssq"])

            def gelut(ps_ap, pkey, out_ap, outkeys, r):
                x2, xh, w_ = GX2[r], GXS[r], GW[r]
                ACT(x2, ps_ap, AF.Square, [pkey], [("gx2", r)])
                ACT(xh, ps_ap, AF.Identity, [pkey], [("gxs", r)], scale=0.5)
                TSC(w_, x2, GC1, 1.0, ALU.mult, ALU.add, [("gx2", r)], [("gw", r)])
                TT(w_, w_, xh, ALU.mult, [("gw", r), ("gxs", r)], [("gw", r)])
                ACT(x2, w_, AF.Tanh, [("gw", r)], [("gx2", r)], scale=GC2)
                STT(out_ap, x2, 1.0, xh, ALU.add, ALU.mult, [("gx2", r), ("gxs", r)], outkeys)

            specs = [[(w_in[l][:, OCU + g * 512:OCU + (g + 1) * 512], KC, 512)] for g in range(4)]
            specs += [[(w_in[l][:, OCV + g * 512:OCV + (g + 1) * 512], KC, 512)] for g in range(4)]
            ws = WStream(specs)
            it = 0
            for g in range(4):
                wv, wk = ws.get(g)
                for tb in range(NTB):
                    for j in range(4):
                        r = it % 2
                        it += 1
                        pz = 0 + r
                        MM(PS[pz][:, :], [(wv[:, kc, j * 128:(j + 1) * 128], HALL[:, kc, tb * 512:(tb + 1) * 512]) for kc in range(KC)],
                           [wk, ("hall", tb)], psk(pz))
                        gelut(PS[pz][:, :], psk(pz), GO[r], [("go", r)], r)
                        fc = g * 4 + j
                        S.dma("sp", "go%d" % r, GCU[fc * 128:(fc + 1) * 128, tb * 512:(tb + 1) * 512], GO[r],
                              reads=[("go", r)], writes=[("gcu", tb)])
            for g in range(4):
                wv, wk = ws.get(4 + g)
                for tt in range(24):
                    r = it % 2
                    it += 1
                    pz = 0 + r
                    MM(PS[pz][:, :], [(HALL[:, kc, tt * 128:(tt + 1) * 128], wv[:, kc, :]) for kc in range(KC)],
                       [wk, ("hall", tt // 4)], psk(pz))
                    gelut(PS[pz][:, :], psk(pz), GO[r], [("go", r)], r)
                    ACT(JUNK, GO[r], AF.Square, [("go", r)], ["junk", "ssq"], accum=SSQ[:, tt * 4 + g:tt * 4 + g + 1])
                    S.dma("sp", "go%d" % r, GCV[tt * 128:(tt + 1) * 128, g * 512:(g + 1) * 512], GO[r],
                          reads=[("go", r)], writes=[("gcv", tt)])
            S.barrier()
            if stop == "P2c":
                break

            RM.reset()
            RA.reset()
            GCUB = [RA.get([KC, 512], BF16) for _ in range(2)]
            CMB = [RA.get([KC, 512], BF16) for _ in range(2)]
            BSB = RA.get([D])
            CMG = RA.get([D])
            WST = RM.get([16, 128], BF16)
            GCVT = [RM.get([D], BF16) for _ in range(2)]
            VCM = [RM.get([D], BF16) for _ in range(2)]
            TMPX = [RM.get([512]) for _ in range(2)]
            SS = RM.get([24])
            S.op("dve", lambda e: e.tensor_reduce(out=SS, in_=SSQ.rearrange("p (t g) -> p t g", g=4),
                                                  axis=mybir.AxisListType.X, op=ALU.add), reads=["ssq"], writes=["ss"])
            ACT(RCV, SS, AF.Ln, ["ss"], ["rcv"], scale=1.0 / D, bias=EPS)
            ACT(RCV, RCV, AF.Exp, ["rcv"], ["rcv"], scale=-0.5)
            S.dma("pool", "wst", WST, wsT[l], writes=["wst"])
            S.dma("sp", "bsb", BSB, cm_bs[l][0:1, :].partition_broadcast(128), writes=["bsb"])
            S.dma("sp", "cmg", CMG, cm_g[l][0:1, :].partition_broadcast(128), writes=["cmg"])
            it = 0
            for tb in range(NTB):
                rb = tb % 2
                S.dma("sp", "gcub%d" % rb, GCUB[rb], GCU[:, tb * 512:(tb + 1) * 512].rearrange("(kc p) t -> p kc t", p=128),
                      reads=[("gcu", tb)], writes=[("gcub", rb)])
                for t4 in range(4):
                    tt = tb * 4 + t4
                    rt = tt % 2
                    S.dma("sp", "gcvt%d" % rt, GCVT[rt], GCV[tt * 128:(tt + 1) * 128, :], reads=[("gcv", tt)], writes=[("gcvt", rt)])
                    STT(VCM[rt], GCVT[rt], RCV[:, tt:tt + 1], CMG, ALU.mult, ALU.mult, [("gcvt", rt), "rcv", "cmg"], [("vcm", rt)])
                    for g4 in range(4):
                        r = it % 2
                        it += 1
                        px = 0 + r

                        def f(e, px=px, rt=rt, g4=g4):
                            ins = None
                            for gi in range(4):
                                g = g4 * 4 + gi
                                ins = e.matmul(PS[px][:, gi * 128:(gi + 1) * 128], lhsT=VCM[rt][:, g * 128:(g + 1) * 128],
                                               rhs=WST[:, g, :], start=True, stop=True)
                            return ins
                        S.op("pe", f, reads=[("vcm", rt), "wst"], writes=[psk(px)])
                        TT(TMPX[r], PS[px][:, :], BSB[:, g4 * 512:(g4 + 1) * 512], ALU.add, [psk(px), "bsb"], [("tmpx", r)])
                        TT(CMB[rb][:, g4 * 4:(g4 + 1) * 4, t4 * 128:(t4 + 1) * 128],
                           TMPX[r].rearrange("p (g q) -> p g q", g=4),
                           GCUB[rb][:, g4 * 4:(g4 + 1) * 4, t4 * 128:(t4 + 1) * 128], ALU.mult,
                           [("tmpx", r), ("gcub", rb)], [("cmb", rb)])
                S.dma("sp", "cmb%d" % rb, CM[:, tb * 512:(tb + 1) * 512].rearrange("(kc p) t -> p kc t", p=128), CMB[rb],
                      reads=[("cmb", rb)], writes=[("cm", tb)])
            S.barrier()
            if stop == "P2d":
                break

            RM.reset()
            RA.reset()
            KTS = RA.get([4, 2304], BF16)
            VSS = RA.get([18, 512], BF16)
            KTP = RA.get([4, 1024], BF16)
            VSP = RA.get([8, 512], BF16)
            QB = [RA.get([16, 512], BF16) for _ in range(2)]
            ATTB = [RM.get([16, 512], BF16) for _ in range(2)]
            PTL = [RM.get([512], BF16) for _ in range(4)]
            RD = [RM.get([512]) for _ in range(2)]
            S.dma("sp", "kts", KTS[:, :, 0:TSMP], KT[:, 0:TSMP].rearrange("(h d) t -> d h t", d=128),
                  reads=[("kt", tb) for tb in range(4)], writes=["kts"])
            S.dma("pool", "ktsc", KTS[:, :, TSMP:2304], kctx[l].rearrange("(h d) t -> d h t", d=128), writes=["kts"])
            S.dma("sp", "vss", VSS[:, 0:16, :], VS[0:TSMP, :].rearrange("(tt p) c -> p tt c", p=128),
                  reads=[("vs", tb) for tb in range(4)], writes=["vss"])
            S.dma("pool", "vssc", VSS[:, 16:18, :], vctx[l].rearrange("(tt p) c -> p tt c", p=128), writes=["vss"])
            S.dma("sp", "ktp", KTP, KT[:, TSMP:T].rearrange("(h d) t -> d h t", d=128),
                  reads=[("kt", 4), ("kt", 5)], writes=["ktp"])
            S.dma("sp", "vsp", VSP, VS[TSMP:T, :].rearrange("(tt p) c -> p tt c", p=128),
                  reads=[("vs", 4), ("vs", 5)], writes=["vsp"])
            ai = 0
            pi_ = 0
            for tb in range(NTB):
                rb = tb % 2
                S.dma("sp", "qb%d" % rb, QB[rb], QT[:, tb * 512:(tb + 1) * 512].rearrange("(h d) t -> d h t", d=128),
                      reads=[("qt", tb)], writes=[("qb", rb)])
                for hk in range(4):
                    for qs in range(4):
                        ra = ai % 2
                        ai += 1
                        po, pd = 4 + ra, 6 + ra
                        rhs = QB[rb][:, hk * 4:(hk + 1) * 4, qs * 128:(qs + 1) * 128]
                        if tb < 4:
                            tiles = [(KTS[:, hk, kt * 128:(kt + 1) * 128], VSS[:, kt, hk * 128:(hk + 1) * 128]) for kt in range(18)]
                            kv = ["kts", "vss"]
                        else:
                            seq = (tb - 4) * 2 + qs // 2
                            tiles = [(KTP[:, hk, seq * 256 + kt * 128:seq * 256 + (kt + 1) * 128],
                                      VSP[:, seq * 2 + kt, hk * 128:(hk + 1) * 128]) for kt in range(2)]
                            kv = ["ktp", "vsp"]
                        nk = len(tiles)
                        slots = []

                        def qk(i):
                            ps_i = pi_ % 4
                            MM(PS[ps_i][:, :], [(tiles[i][0], rhs)], [kv[0], ("qb", rb)], psk(ps_i))
                            return ps_i
                        pend = []
                        nxt = 0
                        while nxt < min(2, nk):
                            pend.append(qk(nxt))
                            pi_ += 1
                            nxt += 1
                        for kt in range(nk):
                            cur = pend.pop(0)
                            if nxt < nk:
                                pend.append(qk(nxt))
                                pi_ += 1
                                nxt += 1
                            pt = PTL[cur]
                            ACT(pt, PS[cur][:, :], AF.Exp, [psk(cur)], [("ptl", cur)], scale=SCALE)
                            st_, sp_ = (kt == 0), (kt == nk - 1)
                            S.op("pe", lambda e, po=po, pd=pd, v=tiles[kt][1], pt=pt, st_=st_, sp_=sp_: (
                                e.matmul(PS[po][:, :], lhsT=v, rhs=pt, start=st_, stop=sp_),
                                e.matmul(PS[pd][:, :], lhsT=ONES1, rhs=pt, start=st_, stop=sp_))[1],
                                reads=[kv[1], ("ptl", cur), "ones1"], writes=[psk(po), psk(pd)])
                        S.op("dve", lambda e, ra=ra, pd=pd: e.reciprocal(out=RD[ra], in_=PS[pd][:, :]), reads=[psk(pd)], writes=[("rd", ra)])
                        TT(ATTB[rb][:, hk * 4:(hk + 1) * 4, qs * 128:(qs + 1) * 128],
                           PS[po][:, :].rearrange("p (g q) -> p g q", g=4), RD[ra].rearrange("p (g q) -> p g q", g=4),
                           ALU.mult, [psk(po), ("rd", ra)], [("attb", rb)])
                S.dma("sp", "attb%d" % rb, ATT[:, tb * 512:(tb + 1) * 512].rearrange("(h d) t -> d h t", d=128), ATTB[rb],
                      reads=[("attb", rb)], writes=[("att", tb)])
            S.barrier()
            if stop == "P3":
                break

            RM.reset()
            RA.reset()
            A3 = [RA.get([KC, 768], BF16) for _ in range(3)]
            MRG = RA.get([KC, 768], BF16)
            ACC = RM.get([4, 768])
            GT = [RM.get([4, 768], BF16) for _ in range(2)]
            TM = [RM.get([384]) for _ in range(2)]
            OTL = [RM.get([384]) for _ in range(2)]
            srcs = [ATT, LRUO, CM]
            wouts = [w_attn_out, w_lru_out, w_cm_out]
            it = 0
            gi_ = 0
            allk = [("att", tb) for tb in range(NTB)] + [("lruo", n) for n in range(16)] + [("cm", tb) for tb in range(NTB)]

            def load_a3(tg_):
                for b_ in range(3):
                    S.dma("sp", "a3%d" % b_, A3[b_], srcs[b_][:, tg_ * 768:(tg_ + 1) * 768].rearrange("(kc p) t -> p kc t", p=128),
                          reads=allk, writes=[("a3", b_)])

            load_a3(0)
            for tg in range(4):
                c0 = tg * 768
                specs = []
                for cg in range(4):
                    for b in range(3):
                        specs.append([(wouts[b][l][:, cg * 512:(cg + 1) * 512], KC, 512)])
                for cg in range(4):
                    specs.append([(w_out[l][:, cg * 512:(cg + 1) * 512], KC, 512)])
                ws = WStream(specs)
                for cg in range(4):
                    for b in range(3):
                        wv, wk = ws.get(cg * 3 + b)
                        gr = gi_ % 2
                        gi_ += 1
                        r0 = b * D + cg * 512
                        S.dma("sp", "gt%d" % gr, GT[gr], GG[r0:r0 + 512, c0:c0 + 768].rearrange("(j p) t -> p j t", p=128),
                              reads=[("gg", tb) for tb in range(NTB)], writes=[("gt", gr)])
                        ACT(GT[gr], GT[gr], AF.Tanh, [("gt", gr)], [("gt", gr)], scale=0.5)
                        for sub in range(2):
                            ss = slice(sub * 384, (sub + 1) * 384)
                            for j in range(4):
                                r = it % 2
                                it += 1
                                pz = 0 + r
                                MM(PS[pz][:, 0:384], [(wv[:, kc, j * 128:(j + 1) * 128], A3[b][:, kc, ss]) for kc in range(KC)],
                                   [wk, ("a3", b)], psk(pz))
                                if b == 0:
                                    STT(ACC[:, j, ss], GT[gr][:, j, ss], 1.0, PS[pz][:, 0:384], ALU.add, ALU.mult,
                                        [("gt", gr), psk(pz)], [("acc", j, sub)])
                                else:
                                    STT(TM[r], GT[gr][:, j, ss], 1.0, PS[pz][:, 0:384], ALU.add, ALU.mult,
                                        [("gt", gr), psk(pz)], [("tm", r)])
                                    TT(ACC[:, j, ss], ACC[:, j, ss], TM[r], ALU.add, [("acc", j, sub), ("tm", r)], [("acc", j, sub)])
                                if b == 2:
                                    ACT(MRG[:, cg * 4 + j, ss], ACC[:, j, ss], AF.Copy, [("acc", j, sub)], ["mrg"])
                if tg + 1 < 4:
                    load_a3(tg + 1)
                for cg in range(4):
                    wv, wk = ws.get(12 + cg)
                    for sub in range(2):
                        ss = slice(sub * 384, (sub + 1) * 384)
                        for j in range(4):
                            r = it % 2
                            it += 1
                            pz = 0 + r
                            MM(PS[pz][:, 0:384], [(wv[:, kc, j * 128:(j + 1) * 128], MRG[:, kc, ss]) for kc in range(KC)],
                               [wk, "mrg"], psk(pz))
                            ACT(OTL[r], PS[pz][:, 0:384], AF.Identity, [psk(pz)], [("otl", r)], scale=0.5)
                            fc = cg * 4 + j
                            S.dma("sp", "otl%d" % r, OT[fc * 128:(fc + 1) * 128, c0 + sub * 384:c0 + (sub + 1) * 384], OTL[r],
                                  reads=[("otl", r)], writes=["ot"])
            S.barrier()
            if stop == "P4a":
                break

            RM.reset()
            RA.reset()
            OB = RA.get([KC, 512])
            SQ = RA.get([KC, 512], BF16)
            RS = RM.get([512])
            TMP = [RM.get([512]), RM.get([512])]
            H2O = RM.get([KC, 512], BF16)
            XBRS = [RA.get([KC, 512]), RM.get([KC, 512])]
            for tb in range(NTB):
                c = 0 if tb < 4 else 1
                XBr, xk = XBRS[tb % 2], ("xbr", tb % 2)
                S.dma("pool", "ob", OB, OT[:, tb * 512:(tb + 1) * 512].rearrange("(kc p) t -> p kc t", p=128), reads=["ot"], writes=["ob"])
                S.dma("pool", "xbr%d" % (tb % 2), XBr, XSRC[:, tb * 512:(tb + 1) * 512].rearrange("(kc p) t -> p kc t", p=128),
                      reads=xt_reads(tb), writes=[xk])
                norm_stats(OB, "ob")
                for kc in range(KC):
                    tmp, tk = TMP[kc % 2], ("tmp", kc % 2)
                    STT(tmp, OB[:, kc, :], G1P[:, kc, c:c + 1], RS, ALU.mult, ALU.mult, ["ob", ("g1p", c), "rs"], [tk])
                    TT(XBr[:, kc, :], XBr[:, kc, :], tmp, ALU.add, [xk, tk], [xk])
                S.dma("sp", "xst%d" % (tb % 2), yT[:, tb * 512:(tb + 1) * 512].rearrange("(kc p) t -> p kc t", p=128), XBr,
                      reads=[xk], writes=[("xt", tb)])
                norm_stats(XBr, xk)
                for kc in range(KC):
                    tmp, tk = TMP[kc % 2], ("tmp", kc % 2)
                    STT(tmp, XBr[:, kc, :], A2[:, kc, c:c + 1], RS, ALU.mult, ALU.mult, [xk, ("a2", c), "rs"], [tk])
                    ACT(H2O[:, kc, :], tmp, AF.Identity, [tk, ("mod", c)], ["h2o"], bias=B2[:, kc, c:c + 1])
                S.dma("sp", "h2o", H2T[:, tb * 512:(tb + 1) * 512].rearrange("(kc p) t -> p kc t", p=128), H2O,
                      reads=["h2o"], writes=["h2t"])
            S.barrier()
            if stop == "P4b":
                break

            RM.reset()
            RA.reset()
            F1 = RA.get([64, 768], BF16)
            H2G = RM.get([KC, 768], BF16)
            SQX = [RM.get([384]) for _ in range(2)]
            FO = [RM.get([384]) for _ in range(2)]
            it = 0
            for tg in range(4):
                c0 = tg * 768
                S.dma("sp", "h2g", H2G, H2T[:, c0:c0 + 768].rearrange("(kc p) t -> p kc t", p=128), reads=["h2t"], writes=["h2g"])
                specs = [[(w_ff1[l][:, cg * 512:(cg + 1) * 512], KC, 512)] for cg in range(16)]
                specs += [[(w_ff2[l][kh * 4096:(kh + 1) * 4096, og2 * 256:(og2 + 1) * 256], 32, 256)]
                          for og2 in range(8) for kh in range(2)]
                ws = WStream(specs)
                for cg in range(16):
                    wv, wk = ws.get(cg)
                    for sub in range(2):
                        ss = slice(sub * 384, (sub + 1) * 384)
                        for j in range(4):
                            r = it % 2
                            it += 1
                            pz = 0 + r
                            MM(PS[pz][:, 0:384], [(wv[:, kc, j * 128:(j + 1) * 128], H2G[:, kc, ss]) for kc in range(KC)],
                               [wk, "h2g"], psk(pz))
                            ACT(SQX[r], PS[pz][:, 0:384], AF.Square, [psk(pz)], [("sqx", r)])
                            STT(F1[:, cg * 4 + j, ss], PS[pz][:, 0:384], 0.0, SQX[r], ALU.is_gt, ALU.mult,
                                [psk(pz), ("sqx", r)], ["f1"])
                for og2 in range(8):
                    pb = 4 * (og2 % 2)
                    for kh in range(2):
                        wv, wk = ws.get(16 + og2 * 2 + kh)
                        for j in range(2):
                            for sub in range(2):
                                ss = slice(sub * 384, (sub + 1) * 384)
                                pz = pb + j * 2 + sub

                                def f(e, wv=wv, j=j, ss=ss, pz=pz, kh=kh):
                                    ins = None
                                    for kc in range(32):
                                        ins = e.matmul(PS[pz][:, 0:384], lhsT=wv[:, kc, j * 128:(j + 1) * 128],
                                                       rhs=F1[:, kh * 32 + kc, ss],
                                                       start=(kh == 0 and kc == 0), stop=(kh == 1 and kc == 31))
                                    return ins
                                S.op("pe", f, reads=[wk, "f1"], writes=[psk(pz)])
                    for j in range(2):
                        for sub in range(2):
                            pz = pb + j * 2 + sub
                            r = it % 2
                            it += 1
                            og = og2 * 2 + j
                            ACT(FO[r], PS[pz][:, 0:384], AF.Copy, [psk(pz)], [("fo", r)])
                            S.dma("sp", "fo%d" % r, FT[og * 128:(og + 1) * 128, c0 + sub * 384:c0 + (sub + 1) * 384], FO[r],
                                  reads=[("fo", r)], writes=["ft"])
            S.barrier()
            if stop == "P5":
                break

            RM.reset()
            RA.reset()
            OB = RA.get([KC, 512])
            SQ = RA.get([KC, 512], BF16)
            RS = RM.get([512])
            TMP = [RM.get([512]), RM.get([512])]
            XBRS = [RA.get([KC, 512]), RM.get([KC, 512])]
            for tb in range(NTB):
                c = 0 if tb < 4 else 1
                XBr, xk = XBRS[tb % 2], ("xbr", tb % 2)
                S.dma("pool", "ob", OB, FT[:, tb * 512:(tb + 1) * 512].rearrange("(kc p) t -> p kc t", p=128), reads=["ft"], writes=["ob"])
                S.dma("pool", "xbr%d" % (tb % 2), XBr, yT[:, tb * 512:(tb + 1) * 512].rearrange("(kc p) t -> p kc t", p=128),
                      reads=[("xt", tb)], writes=[xk])
                norm_stats(OB, "ob")
                for kc in range(KC):
                    tmp, tk = TMP[kc % 2], ("tmp", kc % 2)
                    STT(tmp, OB[:, kc, :], G2P[:, kc, c:c + 1], RS, ALU.mult, ALU.mult, ["ob", ("g2p", c), "rs"], [tk])
                    TT(XBr[:, kc, :], XBr[:, kc, :], tmp, ALU.add, [xk, tk], [xk])
                S.dma("sp", "xst%d" % (tb % 2), yT[:, tb * 512:(tb + 1) * 512].rearrange("(kc p) t -> p kc t", p=128), XBr,
                      reads=[xk], writes=[("xt", tb)])
            S.barrier()
            if stop == "P7":
                break

        S.dma("sp", "nsout", nsT[:, :], NS, reads=["ns"], writes=[])
        S.barrier()
        S.emit(blk)
    return nc


def _fm(v):
    v = np.asarray(v, np.float32)
    lead = v.shape[:-1]
    return np.ascontiguousarray(np.moveaxis(v.reshape(*lead, 16, 128), -1, 0))


def _host_tables():
    d = np.arange(128)
    f = d % 32
    axis = d // 64
    inv = (10000.0 ** (-np.arange(32, dtype=np.float32) / 32)).astype(np.float32)
    t = np.arange(TSMP)
    pos = np.stack([(t // 64).astype(np.float32), (t % 64).astype(np.float32)], 0)
    ang = pos[axis, :] * inv[f][:, None]
    cos = np.cos(ang).astype(np.float32)
    sin = np.sin(ang).astype(np.float32)
    isb = (d % 64) >= 32
    sgn = np.where(isb, 1.0, -1.0).astype(np.float32)
    sinS = sin * sgn[:, None]
    partner = np.where(isb, d - 32, d + 32)
    perm = np.zeros((128, 128), np.float32)
    perm[partner, d] = 1.0
    return cos, sinS, perm


def _prep(inputs):
    I = {k: np.asarray(v) for k, v in inputs.items()}
    cos, sinS, perm = _host_tables()
    pt = np.zeros((128, DEPTH, NPL), np.float32)
    for l in range(DEPTH):
        pt[:, l, P_BMOD:P_BMOD + 96] = I["b_mod"][l].reshape(96, 128).T
        pt[:, l, P_GPM:P_GPM + 16] = I["g_pre_mix"][l].reshape(16, 128).T
        pt[:, l, P_GPO:P_GPO + 16] = I["g_post_mix"][l].reshape(16, 128).T
        pt[:, l, P_GPF:P_GPF + 16] = I["g_pre_ff"][l].reshape(16, 128).T
        pt[:, l, P_GOF:P_GOF + 16] = I["g_post_ff"][l].reshape(16, 128).T
        pt[:, l, P_CW:P_CW + 64] = I["conv_w"][l].reshape(4, 16, 128).transpose(2, 0, 1).reshape(128, 64)
        pt[:, l, P_CB:P_CB + 16] = I["conv_b"][l].reshape(16, 128).T
        pt[:, l, P_BA:P_BA + 32] = I["lru_ba"][l].reshape(2, 16, 128).transpose(2, 0, 1).reshape(128, 32)
        pt[:, l, P_BX:P_BX + 32] = I["lru_bx"][l].reshape(2, 16, 128).transpose(2, 0, 1).reshape(128, 32)
        pt[:, l, P_LAM:P_LAM + 32] = I["lru_lam"][l].reshape(2, 16, 128).transpose(2, 0, 1).reshape(128, 32)
        pt[:, l, P_GQ] = I["g_q"][l]
        pt[:, l, P_GK] = I["g_k"][l]
    shared = {
        "ptab": np.ascontiguousarray(pt.reshape(128, DEPTH * NPL)),
        "cosT": cos, "sinT": sinS, "permd": perm,
        "w_mod": I["w_mod"], "w_in": I["w_in"], "w_attn_out": I["w_attn_out"], "w_lru_out": I["w_lru_out"],
        "w_cm_out": I["w_cm_out"], "w_out": I["w_out"], "w_ff1": I["w_ff1"], "w_ff2": I["w_ff2"],
        "lru_wa": I["lru_wa"], "lru_wx": I["lru_wx"],
        "wsT": np.ascontiguousarray(I["cm_ws"].transpose(0, 3, 1, 2)),
        "cm_bs": np.ascontiguousarray(I["cm_bs"].reshape(DEPTH, 1, D)),
        "cm_g": np.ascontiguousarray(I["cm_g"].reshape(DEPTH, 1, D)),
    }
    in_maps = []
    for i in range(8):
        xs = I["x_sample"][i]
        xp = I["x_prompt"][4 * i:4 * i + 4].reshape(1024, D)
        xT = np.ascontiguousarray(np.concatenate([xs, xp], 0).T)
        cond = np.stack([I["c"][i], I["c_ctx"]], -1)
        condT = np.ascontiguousarray(cond.reshape(16, 128, 2).transpose(1, 0, 2))
        kc_ = np.ascontiguousarray(I["cache_k"][i].transpose(0, 2, 3, 1).reshape(DEPTH, 512, 256))
        vc_ = np.ascontiguousarray(I["cache_v"][i].reshape(DEPTH, 256, 512))
        h0 = np.ascontiguousarray(I["state_lru"][i].reshape(DEPTH, 2, 16, 128).transpose(3, 0, 1, 2).reshape(128, DEPTH * 32))
        m = dict(shared)
        m.update({"xT0": xT, "condT": condT, "kctx": kc_, "vctx": vc_, "h0T": h0})
        in_maps.append(m)
    return in_maps


def _gather(results):
    y_p = np.zeros((32, 256, D), np.float32)
    y_s = np.zeros((8, 2048, D), np.float32)
    nk = np.zeros((32, DEPTH, 256, 4, 128), np.float32)
    nv = np.zeros((32, DEPTH, 256, 4, 128), np.float32)
    ns = np.zeros((32, DEPTH, 2, D), np.float32)
    for i, r in enumerate(results):
        y = np.asarray(r["yT"]).T
        y_s[i] = y[:2048]
        y_p[4 * i:4 * i + 4] = y[2048:].reshape(4, 256, D)
        k = np.asarray(r["knew"]).reshape(DEPTH, 4, 128, 4, 256)
        nk[4 * i:4 * i + 4] = k.transpose(3, 0, 4, 1, 2)
        v = np.asarray(r["vnew"]).reshape(DEPTH, 4, 256, 4, 128)
        nv[4 * i:4 * i + 4] = v.transpose(1, 0, 2, 3, 4)
        s = np.asarray(r["nsT"]).reshape(128, 4, DEPTH, 2, 16)
        ns[4 * i:4 * i + 4] = s.transpose(1, 2, 3, 4, 0).reshape(4, DEPTH, 2, D)
    return y_p, y_s, nk, nv, ns


def kernel(**inputs):
    in_maps = _prep(inputs)
    nc = build()
    res = run_bass_kernel_spmd(nc, in_maps, core_ids=list(range(8)))
    return _gather(res.results)
```
